# Optimizing a Trainium2 kernel written in Bass

```python
import math
import jax, jax.numpy as jnp
from jax import lax
import numpy as np

D_MODEL = 1024
BATCH = 8
SEQ = 2048
DEPTH = 1
DEC_BATCH = 8
DEC_SEQ = 8192
PAST_LEN = 128

D_MIX = D_MODEL
D_S5 = D_MIX // 2
S5_GROUP = 16
S5_GROUPS = D_S5 // S5_GROUP
S5_STATE = 64
D_GMLP = D_MIX - D_S5
GMLP_HEADS = 4
GMLP_HEAD_DIM = D_GMLP // GMLP_HEADS
CHUNK = 128
SCAN_CHUNK = 128
D_FF = ((8 * D_MODEL // 3 + 127) // 128) * 128
N_MOD = 9
EPS = 1e-6

kernel_name = "hymba_s5_gmlp_macaron_adaln_encoder"


def rmsnorm(x, g):
    xf = x.astype(jnp.float32)
    y = xf * lax.rsqrt(jnp.mean(xf * xf, axis=-1, keepdims=True) + EPS)
    return (y * g.astype(jnp.float32)).astype(x.dtype)


def layernorm(x, g, b):
    xf = x.astype(jnp.float32)
    mu = jnp.mean(xf, axis=-1, keepdims=True)
    xc = xf - mu
    y = xc * lax.rsqrt(jnp.mean(xc * xc, axis=-1, keepdims=True) + EPS)
    return (y * g.astype(jnp.float32) + b.astype(jnp.float32)).astype(x.dtype)


def modulate(h, shift, scale):
    return h * (1 + scale[:, None, :]) + shift[:, None, :]


def swiglu(h, w_in, w_out):
    gate, up = jnp.split(h @ w_in, 2, axis=-1)
    return (jax.nn.silu(gate) * up) @ w_out


def s5_discretise(lam_re, lam_im, log_step, b_re, b_im):
    dt = jnp.exp(log_step)[:, None]
    mag = jnp.exp(lam_re * dt)
    ab_re = mag * jnp.cos(lam_im * dt)
    ab_im = mag * jnp.sin(lam_im * dt)
    n_re = ab_re - 1.0
    n_im = ab_im
    den = lam_re * lam_re + lam_im * lam_im
    f_re = (n_re * lam_re + n_im * lam_im) / den
    f_im = (n_im * lam_re - n_re * lam_im) / den
    bb_re = f_re[..., None] * b_re - f_im[..., None] * b_im
    bb_im = f_re[..., None] * b_im + f_im[..., None] * b_re
    return ab_re, ab_im, bb_re, bb_im


def _linear_recurrence_op(left, right):
    a1r, a1i, b1r, b1i = left
    a2r, a2i, b2r, b2i = right
    ar = a2r * a1r - a2i * a1i
    ai = a2r * a1i + a2i * a1r
    br = a2r * b1r - a2i * b1i + b2r
    bi = a2r * b1i + a2i * b1r + b2i
    return ar, ai, br, bi


def s5_scan_direction(u, lam_re, lam_im, log_step, b_re, b_im, c_re, c_im):
    bsz, seq_len, n_groups, _ = u.shape
    n_state = lam_re.shape[-1]
    ab_re, ab_im, bb_re, bb_im = s5_discretise(lam_re, lam_im, log_step, b_re, b_im)
    n_seg = seq_len // SCAN_CHUNK
    u_seg = u.reshape(bsz, n_seg, SCAN_CHUNK, n_groups, S5_GROUP).transpose(1, 0, 2, 3, 4)
    a_re = jnp.broadcast_to(ab_re, (bsz, SCAN_CHUNK, n_groups, n_state))
    a_im = jnp.broadcast_to(ab_im, (bsz, SCAN_CHUNK, n_groups, n_state))

    def step(carry, u_c):
        s_re, s_im = carry
        bu_re = jnp.einsum('gph,btgh->btgp', bb_re, u_c)
        bu_im = jnp.einsum('gph,btgh->btgp', bb_im, u_c)
        cum_re, cum_im, loc_re, loc_im = lax.associative_scan(
            _linear_recurrence_op, (a_re, a_im, bu_re, bu_im), axis=1)
        x_re = loc_re + cum_re * s_re[:, None] - cum_im * s_im[:, None]
        x_im = loc_im + cum_re * s_im[:, None] + cum_im * s_re[:, None]
        y = (jnp.einsum('ghp,btgp->btgh', c_re, x_re)
             - jnp.einsum('ghp,btgp->btgh', c_im, x_im))
        return (x_re[:, -1], x_im[:, -1]), y

    init = (jnp.zeros((bsz, n_groups, n_state), jnp.float32),
            jnp.zeros((bsz, n_groups, n_state), jnp.float32))
    _, ys = lax.scan(step, init, u_seg)
    return ys.transpose(1, 0, 2, 3, 4).reshape(bsz, seq_len, n_groups, S5_GROUP)


def s5_mixer(u, lam_re_f, lam_im_f, log_step_f, b_re_f, b_im_f, c_re_f, c_im_f,
             lam_re_b, lam_im_b, log_step_b, b_re_b, b_im_b, c_re_b, c_im_b, d, w_glu):
    bsz, seq_len, _ = u.shape
    f32 = lambda a: a.astype(jnp.float32)
    uf = f32(u)
    ug = uf.reshape(bsz, seq_len, S5_GROUPS, S5_GROUP)
    y_f = s5_scan_direction(ug, f32(lam_re_f), f32(lam_im_f), f32(log_step_f),
                            f32(b_re_f), f32(b_im_f), f32(c_re_f), f32(c_im_f))
    y_b = s5_scan_direction(ug[:, ::-1], f32(lam_re_b), f32(lam_im_b), f32(log_step_b),
                            f32(b_re_b), f32(b_im_b), f32(c_re_b), f32(c_im_b))[:, ::-1]
    y = (y_f + y_b).reshape(bsz, seq_len, D_S5) + f32(d) * uf
    y = jax.nn.gelu(y).astype(u.dtype)
    return y * jax.nn.sigmoid(y @ w_glu)


def gmlp_mixer(z, ln_g, ln_b, w_sp, b_sp):
    z = jax.nn.gelu(z)
    u, v = jnp.split(z, 2, axis=-1)
    v = layernorm(v, ln_g, ln_b)
    bsz, seq_len, _ = v.shape
    vc = v.reshape(bsz, seq_len // CHUNK, CHUNK, GMLP_HEADS, GMLP_HEAD_DIM)
    mixed = jnp.einsum('hqk,bnkhc->bnqhc', w_sp, vc) + b_sp.T[None, None, :, :, None]
    return u * mixed.reshape(bsz, seq_len, D_GMLP)


def encoder_layer(x, c, w_ada, b_ada, norm_ffn1_g, ffn1_w_in, ffn1_w_out,
                  norm_mix_g, w_mix_in,
                  s5_lam_re_f, s5_lam_im_f, s5_log_step_f, s5_b_re_f, s5_b_im_f, s5_c_re_f, s5_c_im_f,
                  s5_lam_re_b, s5_lam_im_b, s5_log_step_b, s5_b_re_b, s5_b_im_b, s5_c_re_b, s5_c_im_b,
                  s5_d, s5_w_glu, gmlp_ln_g, gmlp_ln_b, gmlp_w_sp, gmlp_b_sp,
                  norm_out_s5_g, norm_out_gmlp_g, w_mix_out,
                  norm_ffn2_g, ffn2_w_in, ffn2_w_out):
    mod = jax.nn.silu(c) @ w_ada + b_ada
    sh1, sc1, g1, sh2, sc2, g2, sh3, sc3, g3 = jnp.split(mod, N_MOD, axis=-1)
    h = modulate(rmsnorm(x, norm_ffn1_g), sh1, sc1)
    x = x + 0.5 * g1[:, None, :] * swiglu(h, ffn1_w_in, ffn1_w_out)
    h = modulate(rmsnorm(x, norm_mix_g), sh2, sc2)
    z = h @ w_mix_in
    y_s5 = s5_mixer(z[..., :D_S5],
                    s5_lam_re_f, s5_lam_im_f, s5_log_step_f, s5_b_re_f, s5_b_im_f, s5_c_re_f, s5_c_im_f,
                    s5_lam_re_b, s5_lam_im_b, s5_log_step_b, s5_b_re_b, s5_b_im_b, s5_c_re_b, s5_c_im_b,
                    s5_d, s5_w_glu)
    y_gm = gmlp_mixer(z[..., D_S5:], gmlp_ln_g, gmlp_ln_b, gmlp_w_sp, gmlp_b_sp)
    y_cat = jnp.concatenate([rmsnorm(y_s5, norm_out_s5_g), rmsnorm(y_gm, norm_out_gmlp_g)], axis=-1)
    x = x + g2[:, None, :] * (y_cat @ w_mix_out)
    h = modulate(rmsnorm(x, norm_ffn2_g), sh3, sc3)
    x = x + 0.5 * g3[:, None, :] * swiglu(h, ffn2_w_in, ffn2_w_out)
    return x


def run_trunk(x, c, layer_params, final_norm_g):
    for l in range(DEPTH):
        x = encoder_layer(x, c, *[p[l] for p in layer_params])
    return rmsnorm(x, final_norm_g)


def setup_inputs(seed: int = 0) -> dict:
    key = jax.random.key(seed)
    ks = iter(jax.random.split(key, 64))
    nrm = lambda shape, s: jax.random.normal(next(ks), shape, jnp.float32) * s
    G, P, H = S5_GROUPS, S5_STATE, S5_GROUP

    def gain(shape):
        return 1.0 + nrm(shape, 0.02)

    def s5_dir():
        lam_re = -0.5 * (1.0 + nrm((DEPTH, G, P), 0.02))
        lam_im = math.pi * jnp.arange(P, dtype=jnp.float32)[None, None, :] + nrm((DEPTH, G, P), 0.01)
        log_step = jax.random.uniform(next(ks), (DEPTH, G), jnp.float32,
                                      math.log(0.001), math.log(0.1))
        b_re = nrm((DEPTH, G, P, H), (2 * H) ** -0.5)
        b_im = nrm((DEPTH, G, P, H), (2 * H) ** -0.5)
        c_re = nrm((DEPTH, G, H, P), (2 * P) ** -0.5)
        c_im = nrm((DEPTH, G, H, P), (2 * P) ** -0.5)
        return lam_re, lam_im, log_step, b_re, b_im, c_re, c_im

    x_prompt = nrm((BATCH, SEQ, D_MODEL), 1.0)
    x_sample = nrm((DEC_BATCH, DEC_SEQ, D_MODEL), 1.0)
    c_prompt = nrm((BATCH, D_MODEL), 1.0)
    c_sample = nrm((DEC_BATCH, D_MODEL), 1.0)
    w_ada = nrm((DEPTH, D_MODEL, N_MOD * D_MODEL), 0.5 * D_MODEL ** -0.5)
    b_ada = nrm((DEPTH, N_MOD * D_MODEL), 0.01)
    norm_ffn1_g = gain((DEPTH, D_MODEL))
    ffn1_w_in = nrm((DEPTH, D_MODEL, 2 * D_FF), D_MODEL ** -0.5)
    ffn1_w_out = nrm((DEPTH, D_FF, D_MODEL), D_FF ** -0.5)
    norm_mix_g = gain((DEPTH, D_MODEL))
    w_mix_in = nrm((DEPTH, D_MODEL, D_S5 + 2 * D_GMLP), D_MODEL ** -0.5)
    lre_f, lim_f, ls_f, bre_f, bim_f, cre_f, cim_f = s5_dir()
    lre_b, lim_b, ls_b, bre_b, bim_b, cre_b, cim_b = s5_dir()
    s5_d = nrm((DEPTH, D_S5), 1.0)
    s5_w_glu = nrm((DEPTH, D_S5, D_S5), D_S5 ** -0.5)
    gmlp_ln_g = gain((DEPTH, D_GMLP))
    gmlp_ln_b = nrm((DEPTH, D_GMLP), 0.01)
    gmlp_w_sp = nrm((DEPTH, GMLP_HEADS, CHUNK, CHUNK), CHUNK ** -0.5)
    gmlp_b_sp = 1.0 + nrm((DEPTH, GMLP_HEADS, CHUNK), 0.01)
    norm_out_s5_g = gain((DEPTH, D_S5))
    norm_out_gmlp_g = gain((DEPTH, D_GMLP))
    w_mix_out = nrm((DEPTH, D_MIX, D_MODEL), D_MIX ** -0.5)
    norm_ffn2_g = gain((DEPTH, D_MODEL))
    ffn2_w_in = nrm((DEPTH, D_MODEL, 2 * D_FF), D_MODEL ** -0.5)
    ffn2_w_out = nrm((DEPTH, D_FF, D_MODEL), D_FF ** -0.5)
    final_norm_g = gain((D_MODEL,))
    return {
        "x_prompt": x_prompt, "x_sample": x_sample,
        "c_prompt": c_prompt, "c_sample": c_sample,
        "w_ada": w_ada, "b_ada": b_ada,
        "norm_ffn1_g": norm_ffn1_g, "ffn1_w_in": ffn1_w_in, "ffn1_w_out": ffn1_w_out,
        "norm_mix_g": norm_mix_g, "w_mix_in": w_mix_in,
        "s5_lam_re_f": lre_f, "s5_lam_im_f": lim_f, "s5_log_step_f": ls_f,
        "s5_b_re_f": bre_f, "s5_b_im_f": bim_f, "s5_c_re_f": cre_f, "s5_c_im_f": cim_f,
        "s5_lam_re_b": lre_b, "s5_lam_im_b": lim_b, "s5_log_step_b": ls_b,
        "s5_b_re_b": bre_b, "s5_b_im_b": bim_b, "s5_c_re_b": cre_b, "s5_c_im_b": cim_b,
        "s5_d": s5_d, "s5_w_glu": s5_w_glu,
        "gmlp_ln_g": gmlp_ln_g, "gmlp_ln_b": gmlp_ln_b,
        "gmlp_w_sp": gmlp_w_sp, "gmlp_b_sp": gmlp_b_sp,
        "norm_out_s5_g": norm_out_s5_g, "norm_out_gmlp_g": norm_out_gmlp_g,
        "w_mix_out": w_mix_out,
        "norm_ffn2_g": norm_ffn2_g, "ffn2_w_in": ffn2_w_in, "ffn2_w_out": ffn2_w_out,
        "final_norm_g": final_norm_g,
    }


def reference(x_prompt, x_sample, c_prompt, c_sample, w_ada, b_ada,
              norm_ffn1_g, ffn1_w_in, ffn1_w_out, norm_mix_g, w_mix_in,
              s5_lam_re_f, s5_lam_im_f, s5_log_step_f, s5_b_re_f, s5_b_im_f, s5_c_re_f, s5_c_im_f,
              s5_lam_re_b, s5_lam_im_b, s5_log_step_b, s5_b_re_b, s5_b_im_b, s5_c_re_b, s5_c_im_b,
              s5_d, s5_w_glu, gmlp_ln_g, gmlp_ln_b, gmlp_w_sp, gmlp_b_sp,
              norm_out_s5_g, norm_out_gmlp_g, w_mix_out,
              norm_ffn2_g, ffn2_w_in, ffn2_w_out, final_norm_g):
    layer_params = (w_ada, b_ada, norm_ffn1_g, ffn1_w_in, ffn1_w_out, norm_mix_g, w_mix_in,
                    s5_lam_re_f, s5_lam_im_f, s5_log_step_f, s5_b_re_f, s5_b_im_f, s5_c_re_f, s5_c_im_f,
                    s5_lam_re_b, s5_lam_im_b, s5_log_step_b, s5_b_re_b, s5_b_im_b, s5_c_re_b, s5_c_im_b,
                    s5_d, s5_w_glu, gmlp_ln_g, gmlp_ln_b, gmlp_w_sp, gmlp_b_sp,
                    norm_out_s5_g, norm_out_gmlp_g, w_mix_out,
                    norm_ffn2_g, ffn2_w_in, ffn2_w_out)
    y_prompt = run_trunk(x_prompt, c_prompt, layer_params, final_norm_g)
    y_sample = run_trunk(x_sample, c_sample, layer_params, final_norm_g)
    return (y_prompt, y_sample)
```

```python
import numpy as np
import concourse.bass as bass
import concourse.mybir as mybir
from concourse.bass_utils import run_bass_kernel_spmd

F32 = mybir.dt.float32
BF16 = mybir.dt.bfloat16
I32 = mybir.dt.int32
AF = mybir.ActivationFunctionType
ALU = mybir.AluOpType

ENGS = ("pe", "act", "dve", "pool", "sp")
D = 1024
DFF = 2816
NFT = 22
TAU = 4
TT = 512
NSUB = TT // TAU
EPS = 1e-6
PI = 3.14159265358979


class Sched:
    def __init__(self, nc):
        self.nc = nc
        self.q = {e: [] for e in ENGS}
        self.cnt = {e: 0 for e in ENGS}
        self.esem = {}
        self.last_w = {}
        self.readers = {}
        self.seen = {e: {} for e in ENGS}
        self.dsem = {}
        self._ctx = []

    def open(self):
        for e in ENGS:
            cm = self.nc.semaphore("es_" + e)
            self.esem[e] = cm.__enter__()
            self._ctx.append(cm)

    def dma_sem(self, name):
        if name not in self.dsem:
            cm = self.nc.semaphore("ds_" + name)
            h = cm.__enter__()
            self._ctx.append(cm)
            self.dsem[name] = [h, 0]
        return name

    def close(self):
        for cm in reversed(self._ctx):
            cm.__exit__(None, None, None)

    def _need(self, eng, ev, waits):
        if ev is None:
            return
        sk, val = ev
        if sk == ("e", "pe") and eng == "pe":
            return
        if self.seen[eng].get(sk, 0) >= val:
            return
        self.seen[eng][sk] = val
        waits[sk] = max(waits.get(sk, 0), val)

    def _deps(self, eng, reads, writes):
        waits = {}
        for k in reads:
            self._need(eng, self.last_w.get(k), waits)
        for k in writes:
            self._need(eng, self.last_w.get(k), waits)
            for sk, val in self.readers.get(k, {}).items():
                self._need(eng, (sk, val), waits)
        return waits

    def _commit(self, ev, reads, writes):
        for k in reads:
            d = self.readers.setdefault(k, {})
            d[ev[0]] = max(d.get(ev[0], 0), ev[1])
        for k in writes:
            self.last_w[k] = ev
            self.readers[k] = {}

    def op(self, eng, fn, reads=(), writes=()):
        waits = self._deps(eng, reads, writes)
        self.cnt[eng] += 1
        ev = (("e", eng), self.cnt[eng])
        self.q[eng].append((waits, fn, None))
        self._commit(ev, reads, writes)
        return ev

    def dma(self, eng, fn, sem, reads=(), writes=()):
        self.dma_sem(sem)
        waits = self._deps(eng, reads, writes)
        if self.dsem[sem][1]:
            self._need(eng, (("d", sem), self.dsem[sem][1]), waits)
        self.dsem[sem][1] += 16
        ev = (("d", sem), self.dsem[sem][1])
        self.q[eng].append((waits, fn, sem))
        self._commit(ev, reads, writes)
        return ev

    def wait_all(self, eng):
        waits = {}
        for e in ENGS:
            if self.cnt[e] and e != eng:
                self._need(eng, (("e", e), self.cnt[e]), waits)
        for name, (h, c) in self.dsem.items():
            if c:
                self._need(eng, (("d", name), c), waits)
        if waits:
            self.q[eng].append((waits, None, None))

    def barrier(self):
        for e in ENGS:
            self.wait_all(e)

    def _semh(self, sk):
        return self.esem[sk[1]] if sk[0] == "e" else self.dsem[sk[1]][0]

    def emit(self):
        nc = self.nc
        with nc.Block() as block:
            def mk(ename):
                def body(eng):
                    for waits, fn, dsem in self.q[ename]:
                        for sk, val in waits.items():
                            eng.wait_ge(self._semh(sk), val)
                        if fn is None:
                            continue
                        ins = fn(eng)
                        if dsem is not None:
                            ins.then_inc(self.dsem[dsem][0], 16)
                        else:
                            ins.then_inc(self.esem[ename], 1)
                return body
            block.tensor(mk("pe"))
            block.scalar(mk("act"))
            block.vector(mk("dve"))
            block.gpsimd(mk("pool"))
            block.sync(mk("sp"))


class Pool_:
    def __init__(self, nc):
        self.nc = nc
        self.stack = []

    def sb(self, name, shape, dt):
        cm = self.nc.sbuf_tensor(name, list(shape), dt)
        t = cm.__enter__()
        self.stack.append(cm)
        return t

    def ps(self, name, shape, dt):
        cm = self.nc.psum_tensor(name, list(shape), dt)
        t = cm.__enter__()
        self.stack.append(cm)
        return t

    def mark(self):
        return len(self.stack)

    def release(self, mark):
        while len(self.stack) > mark:
            self.stack.pop().__exit__(None, None, None)


WNAMES = ["w_ada", "b_ada", "norm_ffn1_g", "ffn1_w_in", "ffn1_w_out", "norm_mix_g", "w_mix_in",
          "s5_lam_re_f", "s5_lam_im_f", "s5_log_step_f", "s5_b_re_f", "s5_b_im_f", "s5_c_re_f", "s5_c_im_f",
          "s5_lam_re_b", "s5_lam_im_b", "s5_log_step_b", "s5_b_re_b", "s5_b_im_b", "s5_c_re_b", "s5_c_im_b",
          "s5_d", "s5_w_glu", "gmlp_ln_g", "gmlp_ln_b", "gmlp_w_sp", "gmlp_b_sp",
          "norm_out_s5_g", "norm_out_gmlp_g", "w_mix_out", "norm_ffn2_g", "ffn2_w_in", "ffn2_w_out",
          "final_norm_g"]
WSHAPES = {
    "w_ada": [D, 9 * D], "b_ada": [9 * D], "norm_ffn1_g": [D], "ffn1_w_in": [D, 2 * DFF],
    "ffn1_w_out": [DFF, D], "norm_mix_g": [D], "w_mix_in": [D, 1536],
    "s5_lam_re_f": [32, 64], "s5_lam_im_f": [32, 64], "s5_log_step_f": [32],
    "s5_b_re_f": [32, 64, 16], "s5_b_im_f": [32, 64, 16], "s5_c_re_f": [32, 16, 64], "s5_c_im_f": [32, 16, 64],
    "s5_lam_re_b": [32, 64], "s5_lam_im_b": [32, 64], "s5_log_step_b": [32],
    "s5_b_re_b": [32, 64, 16], "s5_b_im_b": [32, 64, 16], "s5_c_re_b": [32, 16, 64], "s5_c_im_b": [32, 16, 64],
    "s5_d": [512], "s5_w_glu": [512, 512], "gmlp_ln_g": [512], "gmlp_ln_b": [512],
    "gmlp_w_sp": [4, 128, 128], "gmlp_b_sp": [4, 128],
    "norm_out_s5_g": [512], "norm_out_gmlp_g": [512], "w_mix_out": [D, D], "norm_ffn2_g": [D],
    "ffn2_w_in": [D, 2 * DFF], "ffn2_w_out": [DFF, D], "final_norm_g": [D],
}


def build(LP, LS, dbg=None):
    nc = bass.Bass("TRN2", target_bir_lowering=False)
    S = Sched(nc)
    S.open()
    M = Pool_(nc)
    LEN = [LP, LS]
    NMT = [LP // TT, LS // TT]

    def din(name, shape, dt=F32):
        return nc.dram_tensor(name, list(shape), dt, kind="ExternalInput").ap()

    x_in = [din("x_p", [LP, D]), din("x_s", [LS, D])]
    c_in = din("c", [2, D])
    W = {n: din(n, WSHAPES[n]) for n in WNAMES}
    y_out = [nc.dram_tensor("y_p", [LP, D], F32, kind="ExternalOutput").ap(),
             nc.dram_tensor("y_s", [LS, D], F32, kind="ExternalOutput").ap()]
    dbg_out = {}
    if dbg:
        for k, shp in dbg.items():
            dbg_out[k] = nc.dram_tensor("dbg_" + k, list(shp), F32, kind="ExternalOutput").ap()

    def scr(name, shape, dt):
        return nc.dram_tensor(name, list(shape), dt, kind="Internal").ap()

    Win_s = [scr("win_s%d" % k, [11, 128, 4096], BF16) for k in range(2)]
    Wout_s = [scr("wout_s%d" % k, [8, 128, NFT * 128], BF16) for k in range(2)]
    Wmi_s = scr("wmi_s", [3, 128, 4096], BF16)
    Wmo_s = scr("wmo_s", [8, 128, 1024], BF16)
    Wgl_s = scr("wgl_s", [128, 2048], BF16)
    SIN_s = scr("sin_s", [2, 128, 4096], BF16)
    SOUT_s = scr("sout_s", [2, 128, 4096], BF16)
    BD_s = scr("bd_s", [128, 4096], BF16)
    YC_s = scr("yc_s", [4, 128, 3072], BF16)
    X1_s = [scr("x1_s%d" % s, [LEN[s], D], F32) for s in range(2)]
    SF_s = [scr("sf_s%d" % s, [NMT[s], 128, 4096], BF16) for s in range(2)]
    SB_s = [scr("sb_s%d" % s, [NMT[s], 128, 4096], BF16) for s in range(2)]
    EB_s = [scr("eb_s%d" % s, [NMT[s], 128, 4096], F32) for s in range(2)]

    def OP(eng, meth, reads, writes, *a, **kw):
        return S.op(eng, lambda e: getattr(e, meth)(*a, **kw), reads, writes)

    def DMA(eng, sem, reads, writes, out, in_, **kw):
        return S.dma(eng, lambda e: e.dma_start(out=out, in_=in_, **kw), sem, reads, writes)

    def bc(ap, axis, n):
        shp = list(ap.shape)
        shp.insert(axis, n)
        return ap.unsqueeze(axis).broadcast_to(shp)

    ident_f = M.sb("ident_f", [128, 128], F32)
    ident_b = M.sb("ident_b", [128, 128], BF16)
    ones_b = M.sb("ones_b", [128, 128], BF16)
    eps_t = M.sb("eps_t", [128, 1], F32)
    modc = M.sb("modc", [128, 9, 2, 8], F32)
    wspT = M.sb("wspT", [128, 4, 128], BF16)
    bsp_c = M.sb("bsp_c", [128, 4], F32)
    scanA = M.sb("scanA", [128, 2, 4, 16], F32)
    scanB = M.sb("scanB", [128, 2, 4, 16], F32)
    PW = M.sb("PW", [128, 2, 2, 16, 16], F32)

    PG = [M.ps("pg%d" % i, [128, 512], F32) for i in range(4)]
    PO = [M.ps("po%d" % i, [128, 512], F32) for i in range(2)]
    TRA = M.ps("tra", [128, 8, 128], BF16)
    TRB = M.ps("trb", [128, 2, 4, 128], BF16)
    PGK = ["pg0", "pg1", "pg2", "pg3"]
    POK = ["po0", "po1"]

    iot = M.sb("iot", [128, 128], I32)
    OP("pool", "iota", [], ["iot"], iot[:], [[1, 128]], base=0, channel_multiplier=-1)
    OP("dve", "tensor_scalar", ["iot"], ["ident_f"], ident_f[:], iot[:], 0.0, None, ALU.is_equal)
    OP("dve", "tensor_copy", ["ident_f"], ["ident_b"], ident_b[:], ident_f[:])
    OP("dve", "memset", [], ["ones_b"], ones_b[:], 1.0)
    OP("dve", "memset", [], ["eps_t"], eps_t[:], EPS)

    mk_pro = M.mark()
    cT = M.sb("cT", [128, 8, 2], F32)
    bcol = M.sb("bcol", [128, 72], F32)
    ngc = M.sb("ngc", [128, 3, 8], F32)
    stg = [M.sb("stg%d" % i, [128, 4096], F32) for i in range(2)]
    stb = [M.sb("stb%d" % i, [128, 4096], BF16) for i in range(2)]
    for s_ in range(2):
        DMA("act", "pl%d" % (6 + s_), [], [("cTl", s_)], cT[:, :, s_], c_in[s_].rearrange("(dt p) -> p dt", p=128),
            allow_slow_non_contiguous=True)
    DMA("act", "pl1", [], ["bcol"], bcol[:], W["b_ada"].rearrange("(ft p) -> p ft", p=128), allow_slow_non_contiguous=True)
    for j, nm in enumerate(["norm_ffn1_g", "norm_mix_g", "norm_ffn2_g"]):
        DMA("act", "pl%d" % (2 + j), [], [("ngc", j)], ngc[:, j, :], W[nm].rearrange("(dt p) -> p dt", p=128),
            allow_slow_non_contiguous=True)
    OP("act", "activation", [("cTl", 0), ("cTl", 1)], ["cT"], cT[:], cT[:], AF.Silu)
    modps = PG[0]
    wada_v = W["w_ada"].rearrange("(dt p) f -> p dt f", p=128)
    for ch in range(18):
        sl = ch % 2
        DMA("sp", "cst%d" % sl, [], [("stg", sl)], stg[sl][:].rearrange("p (dt f) -> p dt f", dt=8),
            wada_v[:, :, ch * 512:(ch + 1) * 512])
        sv = stg[sl][:].rearrange("p (dt f) -> p dt f", dt=8)
        for f4 in range(4):
            ft = ch * 4 + f4
            for dt in range(8):
                OP("pe", "matmul", [("stg", sl), "cT"], ["pg0"], modps[:, ft * 2:ft * 2 + 2],
                   lhsT=sv[:, dt, f4 * 128:(f4 + 1) * 128], rhs=cT[:, dt, :], start=(dt == 0), stop=(dt == 7))
    OP("dve", "tensor_tensor", ["pg0", "bcol"], ["modc"], modc[:].rearrange("p k s d -> p k d s"),
       modps[:, 0:144].rearrange("p (k d s) -> p k d s", k=9, d=8),
       bc(bcol[:].rearrange("p (k d) -> p k d", k=9), 3, 2), ALU.add)
    for j in range(3):
        OP("dve", "scalar_tensor_tensor", ["modc", ("ngc", j)], ["modc"], modc[:, 3 * j + 1, :, :],
           modc[:, 3 * j + 1, :, :], 1.0, bc(ngc[:, j, :], 1, 2), ALU.add, ALU.mult)
    for k in (2, 8):
        OP("dve", "tensor_scalar", ["modc"], ["modc"], modc[:, k, :, :], modc[:, k, :, :], 0.5, None, ALU.mult)

    DMA("act", "pl3", [], ["bsp_c"], bsp_c[:], W["gmlp_b_sp"].rearrange("h q -> q h"), allow_slow_non_contiguous=True)
    wsp_n = M.sb("wsp_n", [128, 4, 128], F32)
    DMA("act", "pl4", [], ["wsp_n"], wsp_n[:], W["gmlp_w_sp"].rearrange("h q k -> q h k"))
    for h in range(4):
        OP("pe", "transpose", ["wsp_n", "ident_f"], ["po0"], PO[0][:, h * 128:(h + 1) * 128], wsp_n[:, h, :], ident_f[:])
    OP("act", "activation", ["po0"], ["wspT"], wspT[:].rearrange("p h q -> p (h q)"), PO[0][:], AF.Copy)
    gcat = M.sb("gcat", [128, 8], F32)
    DMA("act", "pl5", [], [("gcat", 0)], gcat[:, 0:4], W["norm_out_s5_g"].rearrange("(k p) -> p k", p=128),
        allow_slow_non_contiguous=True)
    DMA("act", "pl6", [], [("gcat", 1)], gcat[:, 4:8], W["norm_out_gmlp_g"].rearrange("(k p) -> p k", p=128),
        allow_slow_non_contiguous=True)
    dcol = M.sb("dcol", [128, 4], F32)
    DMA("act", "pl7", [], ["dcol"], dcol[:], W["s5_d"].rearrange("(k p) -> p k", p=128), allow_slow_non_contiguous=True)

    cvt_i = [0]

    def convert(src, dst, nfree, scale_bc=None):
        i = cvt_i[0]
        cvt_i[0] += 1
        sl = i % 2
        sshape = list(src.shape)
        sv = stg[sl][:, 0:nfree]
        if len(sshape) == 3:
            sv = sv.rearrange("p (a b) -> p a b", a=sshape[1])
        elif len(sshape) == 4:
            sv = sv.rearrange("p (a b c) -> p a b c", a=sshape[1], b=sshape[2])
        if len(sshape) == 4:
            for a_ in range(sshape[2]):
                DMA("sp", "cst%d_%d" % (sl, a_), [], [("stg", sl)] if a_ == 0 else [("stgx", sl, a_)], sv[:, :, a_, :], src[:, :, a_, :])
        else:
            DMA("sp", "cst%d" % sl, [], [("stg", sl)], sv, src)
        if scale_bc is not None:
            OP("dve", "tensor_tensor", [("stg", sl), ("gcat", 0), ("gcat", 1)], [("stb", sl)],
               stb[sl][:, 0:nfree].rearrange("p (a b) -> p a b", a=sshape[1]), sv, scale_bc, ALU.mult)
        elif i % 3 == 0:
            OP("act", "activation", [("stg", sl), ("stgx", sl, 1)], [("stb", sl), ("stgx", sl, 1)], stb[sl][:, 0:nfree], stg[sl][:, 0:nfree], AF.Copy)
        elif i % 3 == 1:
            OP("dve", "tensor_copy", [("stg", sl), ("stgx", sl, 1)], [("stb", sl), ("stgx", sl, 1)], stb[sl][:, 0:nfree], stg[sl][:, 0:nfree])
        else:
            OP("pool", "tensor_copy", [("stg", sl), ("stgx", sl, 1)], [("stb", sl), ("stgx", sl, 1)], stb[sl][:, 0:nfree], stg[sl][:, 0:nfree])
        DMA("act", "cso%d" % sl, [("stb", sl)], [], dst, stb[sl][:, 0:nfree])

    cvd_i = [0]
    deferred = []

    def cast_dma(dst, src):
        i = cvd_i[0]
        cvd_i[0] += 1
        DMA("pool", "cv%d" % (i % 3), [], [], dst, src)

    for k, (wi, wo) in enumerate([("ffn1_w_in", "ffn1_w_out"), ("ffn2_w_in", "ffn2_w_out")]):
        wiv = W[wi].rearrange("(dt p) (gu f) -> p dt gu f", p=128, gu=2)
        wov = W[wo].rearrange("(ft p) d -> p ft d", p=128)
        jobs = []
        for c in range(11):
            for gu in range(2):
                jobs.append((Win_s[k][c].rearrange("p (dt gu f) -> p dt gu f", dt=8, gu=2)[:, :, gu, :],
                             wiv[:, :, gu, c * 256:(c + 1) * 256]))
        for do in range(8):
            jobs.append((Wout_s[k][do].rearrange("p (ft f) -> p ft f", ft=NFT), wov[:, :, do * 128:(do + 1) * 128]))
        if k == 0:
            for dst, src in jobs:
                cast_dma(dst, src)
        else:
            deferred.extend(jobs)
    wmv = W["w_mix_in"].rearrange("(dt p) f -> p dt f", p=128)
    for c3 in range(3):
        cast_dma(Wmi_s[c3].rearrange("p (dt f) -> p dt f", dt=8), wmv[:, :, c3 * 512:(c3 + 1) * 512])
    cast_dma(Wgl_s.rearrange("p (kt f) -> p kt f", kt=4), W["s5_w_glu"].rearrange("(kt p) f -> p kt f", p=128))
    wmo = W["w_mix_out"].rearrange("(kt p) d -> p kt d", p=128)
    for do in range(8):
        convert(wmo[:, :, do * 128:(do + 1) * 128], Wmo_s[do], 1024, scale_bc=bc(gcat[:], 2, 128))

    def s5t(name, shape=(128, 16), dt=F32):
        return M.sb(name, list(shape), dt)

    tmpA = s5t("tmpA"); tmpB = s5t("tmpB"); tmpC = s5t("tmpC")
    tmpI = s5t("tmpI", (128, 16), I32)
    bdst = M.sb("bdst", [128, 4, 2, 4, 128], F32)
    sinst = M.sb("sinst", [128, 4, 4, 2, 128], BF16)
    soutst = M.sb("soutst", [128, 16, 2, 4, 2, 32], BF16)
    bmask = M.sb("bmask", [128, 4, 32], F32)
    OP("dve", "memset", [], ["bmask"], bmask[:], 0.0)
    for q in range(4):
        OP("dve", "memset", ["bmask"], ["bmask"], bmask[32 * q:32 * q + 32, q, :], 1.0)
    ZB = [[M.sb("zb%d_%d" % (q, ri), [128, 128], F32) for ri in range(2)] for q in range(4)]
    for q in range(4):
        for ri in range(2):
            OP("dve", "memset", [], [("zb", q, ri)], ZB[q][ri][:], 0.0)
    pli = [0]

    def plsem():
        pli[0] += 1
        return "pl%d" % (pli[0] % 8)

    def sin_of(out, th, key_out, key_th):
        OP("dve", "tensor_scalar", [key_th], ["tmpA"], tmpA[:], th[:], 1.0 / (2 * PI), None, ALU.mult)
        OP("dve", "tensor_copy", ["tmpA"], ["tmpI"], tmpI[:], tmpA[:])
        OP("dve", "tensor_copy", ["tmpI"], ["tmpA"], tmpA[:], tmpI[:])
        OP("dve", "scalar_tensor_tensor", ["tmpA", key_th], ["tmpB"], tmpB[:], tmpA[:], -2 * PI, th[:], ALU.mult, ALU.add)
        OP("dve", "tensor_scalar", ["tmpB"], ["tmpA"], tmpA[:], tmpB[:], PI, None, ALU.is_gt)
        OP("dve", "scalar_tensor_tensor", ["tmpA", "tmpB"], ["tmpC"], tmpC[:], tmpA[:], -2 * PI, tmpB[:], ALU.mult, ALU.add)
        OP("dve", "tensor_scalar", ["tmpC"], ["tmpA"], tmpA[:], tmpC[:], -PI, None, ALU.is_lt)
        OP("dve", "scalar_tensor_tensor", ["tmpA", "tmpC"], ["tmpB"], tmpB[:], tmpA[:], 2 * PI, tmpC[:], ALU.mult, ALU.add)
        OP("act", "activation", ["tmpB"], [key_out], out[:], tmpB[:], AF.Sin)

    def TT_(eng, out, a, b, op, r, w):
        OP(eng, "tensor_tensor", r, w, out, a, b, op)

    for d, sfx in enumerate(["f", "b"]):
        pf = "d%d_" % d
        mk_d = M.mark()
        lre = s5t(pf + "lre"); lim = s5t(pf + "lim"); lsb = s5t(pf + "lsb")
        DMA("act", plsem(), [], [pf + "lre"], lre[:], W["s5_lam_re_" + sfx].rearrange("(gp two) p -> (two p) gp", two=2),
            allow_slow_non_contiguous=True)
        DMA("act", plsem(), [], [pf + "lim"], lim[:], W["s5_lam_im_" + sfx].rearrange("(gp two) p -> (two p) gp", two=2),
            allow_slow_non_contiguous=True)
        lsv = W["s5_log_step_" + sfx].rearrange("(gp two) -> two gp", two=2)
        for two in range(2):
            DMA("act", plsem(), [], [(pf + "lsb", two)], lsb[two * 64:(two + 1) * 64, :], lsv[two].partition_broadcast(64),
                allow_slow_non_contiguous=True)
        Bre = s5t(pf + "Bre", (128, 16, 16)); Bim = s5t(pf + "Bim", (128, 16, 16))
        DMA("act", plsem(), [], [pf + "Bre"], Bre[:], W["s5_b_re_" + sfx].rearrange("(gp two) p h -> (two p) gp h", two=2))
        DMA("act", plsem(), [], [pf + "Bim"], Bim[:], W["s5_b_im_" + sfx].rearrange("(gp two) p h -> (two p) gp h", two=2))
        CT = []
        for t, cn in enumerate(["s5_c_re_" + sfx, "s5_c_im_" + sfx]):
            CA = M.sb(pf + "CA%d" % t, [128, 2, 128], F32)
            for gp in range(16):
                DMA("act", plsem(), [], [(pf + "CA%d" % t, gp)],
                    CA[16 * (gp % 8):16 * (gp % 8) + 16, gp // 8, :].rearrange("ho (two p) -> ho two p", two=2),
                    W[cn][2 * gp:2 * gp + 2].rearrange("two ho p -> ho two p"))
            ct = s5t(pf + "CT%d" % t, (128, 16, 16))
            for half in range(2):
                OP("pe", "transpose", [(pf + "CA%d" % t, gp_) for gp_ in range(16)] + ["ident_f"], ["po1"], PO[1][:, half * 128:(half + 1) * 128],
                   CA[:, half, :], ident_f[:])
            OP("act", "activation", ["po1"], [pf + "CT%d" % t], ct[:].rearrange("p g h -> p (g h)"), PO[1][:, 0:256], AF.Copy)
            CT.append(ct)
        dtt = s5t(pf + "dt"); xr = s5t(pf + "xr"); xi = s5t(pf + "xi"); xi2 = s5t(pf + "xi2")
        mag = s5t(pf + "mag"); sn = s5t(pf + "sn"); cs = s5t(pf + "cs")
        OP("act", "activation", [(pf + "lsb", 0), (pf + "lsb", 1)], [pf + "dt"], dtt[:], lsb[:], AF.Exp)
        TT_("dve", xr[:], lre[:], dtt[:], ALU.mult, [pf + "lre", pf + "dt"], [pf + "xr"])
        TT_("dve", xi[:], lim[:], dtt[:], ALU.mult, [pf + "lim", pf + "dt"], [pf + "xi"])
        OP("dve", "tensor_scalar", [pf + "xi"], [pf + "xi2"], xi2[:], xi[:], PI / 2, None, ALU.add)
        OP("act", "activation", [pf + "xr"], [pf + "mag"], mag[:], xr[:], AF.Exp)
        sin_of(sn, xi, pf + "sn", pf + "xi")
        sin_of(cs, xi2, pf + "cs", pf + "xi2")
        APr = s5t(pf + "APr", (128, 5, 16)); APi = s5t(pf + "APi", (128, 5, 16))
        kr, ki = pf + "APr", pf + "APi"
        OP("dve", "memset", [], [kr], APr[:, 0, :], 1.0)
        OP("dve", "memset", [], [ki], APi[:, 0, :], 0.0)
        TT_("dve", APr[:, 1, :], mag[:], cs[:], ALU.mult, [pf + "mag", pf + "cs", kr], [kr])
        TT_("dve", APi[:, 1, :], mag[:], sn[:], ALU.mult, [pf + "mag", pf + "sn", ki], [ki])
        for k in range(2, 5):
            TT_("dve", tmpA[:], APr[:, k - 1, :], APr[:, 1, :], ALU.mult, [kr], ["tmpA"])
            TT_("dve", tmpB[:], APi[:, k - 1, :], APi[:, 1, :], ALU.mult, [ki], ["tmpB"])
            TT_("dve", APr[:, k, :], tmpA[:], tmpB[:], ALU.subtract, ["tmpA", "tmpB", kr], [kr])
            TT_("dve", tmpA[:], APr[:, k - 1, :], APi[:, 1, :], ALU.mult, [kr, ki], ["tmpA"])
            TT_("dve", tmpB[:], APi[:, k - 1, :], APr[:, 1, :], ALU.mult, [kr, ki], ["tmpB"])
            TT_("dve", APi[:, k, :], tmpA[:], tmpB[:], ALU.add, ["tmpA", "tmpB", ki], [ki])
        OP("dve", "tensor_copy", [kr], ["scanA"], scanA[:, d, 0, :], APr[:, TAU, :])
        OP("dve", "tensor_copy", [kr], ["scanA"], scanA[:, d, 1, :], APr[:, TAU, :])
        OP("dve", "tensor_scalar", [ki], ["scanA"], scanA[:, d, 2, :], APi[:, TAU, :], -1.0, None, ALU.mult)
        OP("dve", "tensor_copy", [ki], ["scanA"], scanA[:, d, 3, :], APi[:, TAU, :])
        def pwi(j):
            return (j - 1) if d == 0 else (16 - j)
        OP("dve", "tensor_copy", [kr], ["PW"], PW[:, d, 0, pwi(1), :], APr[:, TAU, :])
        OP("dve", "tensor_copy", [ki], ["PW"], PW[:, d, 1, pwi(1), :], APi[:, TAU, :])
        for j in range(2, 17):
            pr_, pi_ = PW[:, d, 0, pwi(j - 1), :], PW[:, d, 1, pwi(j - 1), :]
            TT_("dve", tmpA[:], pr_, APr[:, TAU, :], ALU.mult, ["PW", kr], ["tmpA"])
            TT_("dve", tmpB[:], pi_, APi[:, TAU, :], ALU.mult, ["PW", ki], ["tmpB"])
            TT_("dve", PW[:, d, 0, pwi(j), :], tmpA[:], tmpB[:], ALU.subtract, ["tmpA", "tmpB", "PW"], ["PW"])
            TT_("dve", tmpA[:], pr_, APi[:, TAU, :], ALU.mult, ["PW", ki], ["tmpA"])
            TT_("dve", tmpB[:], pi_, APr[:, TAU, :], ALU.mult, ["PW", kr], ["tmpB"])
            TT_("dve", PW[:, d, 1, pwi(j), :], tmpA[:], tmpB[:], ALU.add, ["tmpA", "tmpB", "PW"], ["PW"])
        OP("dve", "tensor_copy", ["PW"], ["scanB"], scanB[:, d, 0, :], PW[:, d, 0, pwi(16), :])
        OP("dve", "tensor_copy", ["PW"], ["scanB"], scanB[:, d, 1, :], PW[:, d, 0, pwi(16), :])
        OP("dve", "tensor_scalar", ["PW"], ["scanB"], scanB[:, d, 2, :], PW[:, d, 1, pwi(16), :], -1.0, None, ALU.mult)
        OP("dve", "tensor_copy", ["PW"], ["scanB"], scanB[:, d, 3, :], PW[:, d, 1, pwi(16), :])
        fr = s5t(pf + "fr"); fi = s5t(pf + "fi"); nr = s5t(pf + "nr"); den = s5t(pf + "den")
        OP("dve", "tensor_scalar", [kr], [pf + "nr"], nr[:], APr[:, 1, :], -1.0, None, ALU.add)
        TT_("dve", tmpA[:], lre[:], lre[:], ALU.mult, [pf + "lre"], ["tmpA"])
        TT_("dve", tmpB[:], lim[:], lim[:], ALU.mult, [pf + "lim"], ["tmpB"])
        TT_("dve", den[:], tmpA[:], tmpB[:], ALU.add, ["tmpA", "tmpB"], [pf + "den"])
        OP("dve", "reciprocal", [pf + "den"], [pf + "den"], den[:], den[:])
        TT_("dve", tmpA[:], nr[:], lre[:], ALU.mult, [pf + "nr", pf + "lre"], ["tmpA"])
        TT_("dve", tmpB[:], APi[:, 1, :], lim[:], ALU.mult, [ki, pf + "lim"], ["tmpB"])
        TT_("dve", tmpC[:], tmpA[:], tmpB[:], ALU.add, ["tmpA", "tmpB"], ["tmpC"])
        TT_("dve", fr[:], tmpC[:], den[:], ALU.mult, ["tmpC", pf + "den"], [pf + "fr"])
        TT_("dve", tmpA[:], APi[:, 1, :], lre[:], ALU.mult, [ki, pf + "lre"], ["tmpA"])
        TT_("dve", tmpB[:], nr[:], lim[:], ALU.mult, [pf + "nr", pf + "lim"], ["tmpB"])
        TT_("dve", tmpC[:], tmpA[:], tmpB[:], ALU.subtract, ["tmpA", "tmpB"], ["tmpC"])
        TT_("dve", fi[:], tmpC[:], den[:], ALU.mult, ["tmpC", pf + "den"], [pf + "fi"])
        t3a = s5t(pf + "t3a", (128, 16, 16)); t3b = s5t(pf + "t3b", (128, 16, 16))
        t3r = s5t(pf + "t3r", (128, 16, 16)); t3i = s5t(pf + "t3i", (128, 16, 16))

        def cmul(outr, outi, inr, ini, fre, fim, kin, kf, kout, neg_im=False):
            frb, fib = bc(fre, 2, 16), bc(fim, 2, 16)
            TT_("dve", t3a[:], inr, frb, ALU.mult, kin + kf, [pf + "t3a"])
            TT_("dve", t3b[:], ini, fib, ALU.mult, kin + kf, [pf + "t3b"])
            TT_("dve", outr, t3a[:], t3b[:], ALU.subtract, [pf + "t3a", pf + "t3b"], kout)
            TT_("dve", t3a[:], inr, fib, ALU.mult, kin + kf, [pf + "t3a"])
            TT_("dve", t3b[:], ini, frb, ALU.mult, kin + kf, [pf + "t3b"])
            if neg_im:
                OP("dve", "scalar_tensor_tensor", [pf + "t3a", pf + "t3b"], kout, outi, t3a[:], -1.0, t3b[:],
                   ALU.mult, ALU.subtract)
            else:
                TT_("dve", outi, t3a[:], t3b[:], ALU.add, [pf + "t3a", pf + "t3b"], kout)

        bbr = s5t(pf + "bbr", (128, 16, 16)); bbi = s5t(pf + "bbi", (128, 16, 16))
        cmul(bbr[:], bbi[:], Bre[:], Bim[:], fr[:], fi[:], [pf + "Bre", pf + "Bim"], [pf + "fr", pf + "fi"],
             [pf + "bb"])
        MBP = M.sb(pf + "MBP", [128, 2, 4, 16, 32], F32)
        MCP = M.sb(pf + "MCP", [128, 2, 5, 16, 32], F32)
        OP("pool", "memset", [], [pf + "MBP"], MBP[:], 0.0)
        OP("pool", "memset", [], [pf + "MCP"], MCP[:], 0.0)
        for e in range(4):
            cmul(t3r[:], t3i[:], bbr[:], bbi[:], APr[:, e, :], APi[:, e, :], [pf + "bb"], [kr, ki], [pf + "t3ri"])
            for ri, src in enumerate([t3r, t3i]):
                for two in range(2):
                    ps_ = slice(two * 64, two * 64 + 64)
                    OP("dve", "tensor_copy", [pf + "t3ri", pf + "MBP"], [pf + "MBP"],
                       MBP[ps_, ri, e, :, two * 16:(two + 1) * 16], src[ps_, :, :])
        for k in range(5):
            cmul(t3r[:], t3i[:], CT[0][:], CT[1][:], APr[:, k, :], APi[:, k, :], [pf + "CT0", pf + "CT1"], [kr, ki],
                 [pf + "t3ri"], neg_im=True)
            for ri, src in enumerate([t3r, t3i]):
                for two in range(2):
                    ps_ = slice(two * 64, two * 64 + 64)
                    OP("dve", "tensor_copy", [pf + "t3ri", pf + "MCP"], [pf + "MCP"],
                       MCP[ps_, ri, k, :, two * 16:(two + 1) * 16], src[ps_, :, :])
        for ftq in range(4):
            for j in range(4):
                e = (TAU - 1 - j) if d == 0 else j
                for ri in range(2):
                    bank = (j * 2 + ri) % 2
                    OP("pe", "transpose", [pf + "MBP", "ident_f"], [POK[bank]], PO[bank][:, 0:128],
                       MBP[:, ri, e, 4 * ftq:4 * ftq + 4, :].rearrange("p a b -> p (a b)"), ident_f[:])
                    OP("act", "activation", [POK[bank]], ["sinst"], sinst[:, ftq, j, ri, :], PO[bank][:, 0:128], AF.Copy)
        DMA("act", plsem(), ["sinst"], [], SIN_s[d], sinst[:].rearrange("p a b c e -> p (a b c e)"))
        for i in range(4):
            k = (i + 1) if d == 0 else (TAU - i)
            for ri in range(2):
                OP("dve", "tensor_copy", [pf + "MCP", "soutst"], ["soutst"], soutst[:, :, d, i, ri, :], MCP[:, ri, k, :, :])
        for ft in range(4):
            for q in range(4):
                gp = 4 * ft + q
                for ri in range(2):
                    OP("dve", "tensor_copy", [pf + "MBP", ("zb", q, ri)], [("zb", q, ri)],
                       ZB[q][ri][:, 32 * q:32 * q + 32], MBP[:, ri, 0, gp, :])
            for dl in range(4):
                bank = dl % 2
                for q in range(4):
                    gp = 4 * ft + q
                    for ri in range(2):
                        OP("pe", "matmul", [("zb", q, ri), pf + "MCP"], [POK[bank]], PO[bank][:, 0:32],
                           lhsT=ZB[q][ri][:], rhs=MCP[:, ri, dl, gp, :], start=(q == 0 and ri == 0),
                           stop=(q == 3 and ri == 1))
                TT_("dve", bdst[:, ft, d, dl, :].rearrange("p (q c) -> p q c", q=4), bc(PO[bank][:, 0:32], 1, 4),
                    bmask[:], ALU.mult, [POK[bank], "bmask", "bdst"], ["bdst"])
            if d == 0:
                OP("dve", "scalar_tensor_tensor", ["ident_f", "dcol", "bdst"], ["bdst"], bdst[:, ft, 0, 0, :],
                   ident_f[:], dcol[:, ft:ft + 1], bdst[:, ft, 0, 0, :], ALU.mult, ALU.add)
        S.barrier()
        M.release(mk_d)
    OP("act", "activation", ["bdst"], [("stb", 0)], stb[0][:], bdst[:].rearrange("p a b c e -> p (a b c e)"), AF.Copy)
    for ft in range(4):
        DMA("act", plsem(), [("stb", 0)], [], YC_s[ft][:, 0:1024], stb[0][:, ft * 1024:(ft + 1) * 1024])
        DMA("act", plsem(), ["soutst"], [], YC_s[ft][:, 1024:3072],
            soutst[:, 4 * ft:4 * ft + 4].rearrange("p a b c e f -> p (a b c e f)"))

    S.barrier()
    M.release(mk_pro)

    XT = [M.sb("xt0", [128, 4, D], F32), None]
    HT = [M.sb("hT0", [128, 8, TT], BF16), None]

    def xk(xi, r=None):
        return [("xt", xi, r_) for r_ in range(4)] if r is None else ("xt", xi, r)
    xn = [M.sb("xn%d" % i, [128, D], BF16) for i in range(2)]
    ntmp = M.sb("ntmp", [128, 8, 128], F32)
    hh = M.sb("hh", [128, NFT, TT], BF16)
    sg = [M.sb("sg%d" % i, [128, TT], BF16) for i in range(2)]
    ob = [M.sb("ob%d" % i, [128, TT], BF16) for i in range(2)]
    st_ssq = M.sb("st_ssq", [128, 8], F32)
    st_rstd = M.sb("st_rstd", [128, 8], F32)
    Ud = M.sb("Ud", [128, 4, TAU, NSUB], BF16)
    ESr = M.sb("ESr", [128, NSUB, 2, 16], F32)
    Sxr = M.sb("Sxr", [128, 2, 16, NSUB], BF16)
    VE = M.sb("VE", [128, 9, 2, 16], F32)
    sct1 = M.sb("sct1", [128, 8, 2, 16], F32)
    sct2 = M.sb("sct2", [128, 8, 2, 16], F32)
    sctb = M.sb("sctb", [128, 2, 16, 16], F32)
    carry = [M.sb("carry%d" % d, [128, 2, 16], F32) for d in range(2)]
    Sx = [M.sb("Sx0", [128, 2, 16, NSUB], BF16), None]
    rings = {"w": [M.sb("wbuf%d" % i, [128, 4096], BF16) for i in range(2)],
             "o": [M.sb("obuf%d" % i, [128, 3072], BF16) for i in range(2)],
             "q": []}
    ring_i = {"w": 0, "o": 0, "q": 0}

    def wload(kind, src, nfree):
        i = ring_i[kind]
        ring_i[kind] += 1
        sl = i % len(rings[kind])
        buf = rings[kind][sl]
        key = (kind + "buf", sl)
        DMA("sp", "%s%d" % (kind, sl), [], [key], buf[:, 0:nfree], src)
        return buf[:, 0:nfree], key

    pre = {}

    def prefetch(tag, kind, src, nfree):
        if tag not in pre:
            pre[tag] = wload(kind, src, nfree)

    def take(tag, kind, src, nfree):
        if tag in pre:
            return pre.pop(tag)
        return wload(kind, src, nfree)

    def pump(g):
        if g is not None:
            next(g, None)

    def drain(g):
        if g is not None:
            for _ in g:
                pass

    def norm_gen(site, s, xi, hi):
        xt, hT = XT[xi], HT[hi]
        gs = modc[:, 3 * site + 1, s, :]
        sh = modc[:, 3 * site + 0, s, :]
        for r in range(4):
            sl = r % 2
            OP("act", "activation", [xk(xi, r)], [("xn", sl), ("ssq", r)], xn[sl][:], xt[:, r, :], AF.Square,
               accum_out=st_ssq[:, r:r + 1])
        ssk = [("ssq", r) for r in range(4)]
        rsk = [("rstd", r) for r in range(4)]
        OP("act", "activation", ssk, rsk, st_rstd[:, 0:4], st_ssq[:, 0:4], AF.Sqrt, bias=eps_t[:], scale=1.0 / D)
        OP("dve", "reciprocal", rsk, rsk, st_rstd[:, 0:4], st_rstd[:, 0:4])
        yield

        def tail(r):
            sl = r % 2
            for dt in range(8):
                OP("pe", "transpose", [("xn", sl), "ident_b"], ["tra"], TRA[:, dt, :], xn[sl][:, dt * 128:(dt + 1) * 128],
                   ident_b[:])
            OP("dve", "tensor_tensor", ["tra", "modc"], ["ntmp"], ntmp[:], TRA[:], bc(gs, 2, 128), ALU.mult)
            OP("dve", "tensor_tensor", ["ntmp", "modc"], [("hT", hi)], hT[:, :, r * 128:(r + 1) * 128], ntmp[:],
               bc(sh, 2, 128), ALU.add)

        for r in range(4):
            sl = r % 2
            OP("act", "activation", [xk(xi, r), ("rstd", r)], [("xn", sl)], xn[sl][:], xt[:, r, :], AF.Identity,
               scale=st_rstd[:, r:r + 1])
            if r > 0:
                tail(r - 1)
            yield
        tail(3)
        yield

    def norm_hT(site, s, xi=0, hi=0):
        drain(norm_gen(site, s, xi, hi))

    ep_i = [0]

    def epilogue(po_idx, gcol, do, xi=0):
        xt = XT[xi]
        i = ep_i[0]
        ep_i[0] += 1
        sl = i % 2
        OP("act", "activation", [POK[po_idx], "modc"], [("ob", sl)], ob[sl][:], PO[po_idx][:], AF.Identity, scale=gcol)
        for r in range(4):
            OP("pe", "transpose", [("ob", sl), "ident_b"], ["trb"], TRB[:, sl, r, :], ob[sl][:, r * 128:(r + 1) * 128],
               ident_b[:])
        OP("dve", "tensor_tensor", xk(xi) + ["trb"], xk(xi), xt[:, :, do * 128:(do + 1) * 128],
           xt[:, :, do * 128:(do + 1) * 128], TRB[:, sl, :, :], ALU.add)

    def ffn_step1(k, hi=0, filler=None):
        hT = HT[hi]
        nxt = take(("win", k), "w", Win_s[k][0], 4096)
        for c in range(11):
            wt, wk = nxt
            if c + 1 < 11:
                nxt = wload("w", Win_s[k][c + 1], 4096)
            wv = wt.rearrange("p (dt gu f) -> p dt gu f", dt=8, gu=2)
            for f2 in range(2):
                ft = 2 * c + f2
                b = ft % 2
                for gu in range(2):
                    for dt in range(8):
                        OP("pe", "matmul", [wk, ("hT", hi)], [PGK[2 * gu + b]], PG[2 * gu + b][:],
                           lhsT=wv[:, dt, gu, f2 * 128:(f2 + 1) * 128], rhs=hT[:, dt, :], start=(dt == 0), stop=(dt == 7))
                OP("act", "activation", [PGK[b]], [("sg", b)], sg[b][:], PG[b][:], AF.Silu)
                OP("dve", "tensor_tensor", [("sg", b), PGK[2 + b]], [("hh", ft)], hh[:, ft, :], sg[b][:], PG[2 + b][:],
                   ALU.mult)
                pump(filler)

    def ffn_step2(k, s, xi=0, filler=None):
        site = 0 if k == 0 else 2
        nxt = take(("wout", k), "o", Wout_s[k][0], NFT * 128)
        for do in range(8):
            wt, wk = nxt
            if do + 1 < 8:
                nxt = wload("o", Wout_s[k][do + 1], NFT * 128)
            wv = wt.rearrange("p (ft f) -> p ft f", ft=NFT)
            b = do % 2
            for ft in range(NFT):
                OP("pe", "matmul", [wk, ("hh", ft)], [POK[b]], PO[b][:], lhsT=wv[:, ft, :], rhs=hh[:, ft, :],
                   start=(ft == 0), stop=(ft == NFT - 1))
            epilogue(b, modc[:, 3 * site + 2, s, do:do + 1], do, xi)
            pump(filler)

    def ffn(k, s):
        norm_hT(0 if k == 0 else 2, s)
        ffn_step1(k)
        ffn_step2(k, s)

    def s5_u_gen(wt, wk, hi=0):
        hT = HT[hi]
        wv = wt.rearrange("p (dt f) -> p dt f", dt=8)
        for ft in range(4):
            b = ft % 2
            for dt in range(8):
                OP("pe", "matmul", [wk, ("hT", hi)], [POK[b]], PO[b][:], lhsT=wv[:, dt, ft * 128:(ft + 1) * 128],
                   rhs=hT[:, dt, :], start=(dt == 0), stop=(dt == 7))
            OP("act", "activation", [POK[b]], [("Ud", ft)], Ud[:, ft, :, :].rearrange("p j n -> p n j"),
               PO[b][:].rearrange("p (n j) -> p n j", j=TAU), AF.Copy)
            yield

    def s5_u(wt, wk):
        drain(s5_u_gen(wt, wk))

    def statein_gen(wt, wk, ES, esk):
        sv = wt.rearrange("p (q j r c) -> p q j r c", q=4, j=TAU, r=2)
        for ri in range(2):
            for qp in range(2):
                for q4 in range(4):
                    for q2 in range(2):
                        qq = 2 * qp + q2
                        rows = slice(32 * qq, 32 * qq + 32)
                        for j in range(TAU):
                            OP("pe", "matmul", [wk, ("Ud", q4)], [POK[q2]], PO[q2][:, q4 * NSUB:(q4 + 1) * NSUB],
                               lhsT=sv[rows, q4, j, ri, :], rhs=Ud[rows, q4, j, :], start=(j == 0), stop=(j == TAU - 1),
                               tile_position=(32 * qq, 0))
                    if q4 % 2 == 1:
                        yield
                for q2 in range(2):
                    qq = 2 * qp + q2
                    OP("act", "activation", [POK[q2]], [esk],
                       ES[:, :, ri, :].rearrange("p n (a b) -> p b a n", b=4)[:, qq],
                       PO[q2][:].rearrange("p (a n) -> p a n", a=4), AF.Copy)

    def scan2(ES, esk, d, reverse, SXo, sxk, eng="pool"):
        v5 = ES[:].rearrange("p (b k) r g -> p b k r g", b=8)
        ck = ("carry", d)

        def step(prev, cur, tab, t1, t2, nb, rk, wk_):
            ar2, nai, ai = tab[:, d, 0:2, :], tab[:, d, 2, :], tab[:, d, 3, :]
            if nb:
                ar2, nai, ai = bc(ar2, 1, nb), bc(nai, 1, nb), bc(ai, 1, nb)
                i0, i1 = (slice(None), slice(None), 0, slice(None)), (slice(None), slice(None), 1, slice(None))
            else:
                i0, i1 = (slice(None), 0, slice(None)), (slice(None), 1, slice(None))
            OP(eng, "tensor_tensor", rk + ["scanA", "scanB"], ["sct1"], t1, prev, ar2, ALU.mult)
            OP(eng, "tensor_tensor", rk + ["scanA", "scanB"], ["sct2"], t2[i0], prev[i1], nai, ALU.mult)
            OP(eng, "tensor_tensor", rk + ["scanA", "scanB"], ["sct2"], t2[i1], prev[i0], ai, ALU.mult)
            OP(eng, "tensor_tensor", ["sct1"] + rk, wk_, cur, cur, t1, ALU.add)
            OP(eng, "tensor_tensor", ["sct2"] + rk, wk_, cur, cur, t2, ALU.add)

        ks = range(14, -1, -1) if reverse else range(1, 16)
        for k in ks:
            kp = k + 1 if reverse else k - 1
            step(v5[:, :, kp], v5[:, :, k], scanA, sct1[:], sct2[:], 8, [esk], [esk])
        if not reverse:
            OP(eng, "tensor_copy", [ck], ["VE"], VE[:, 0], carry[d][:])
            for b in range(8):
                OP(eng, "tensor_copy", [esk, "VE"], ["VE"], VE[:, b + 1], v5[:, b, 15])
                step(VE[:, b], VE[:, b + 1], scanB, sct1[:, 0], sct2[:, 0], 0, ["VE"], ["VE"])
            vin = VE[:, 0:8]
            OP(eng, "tensor_copy", ["VE"], [ck], carry[d][:], VE[:, 8])
        else:
            OP(eng, "tensor_copy", [ck], ["VE"], VE[:, 8], carry[d][:])
            for b in range(7, -1, -1):
                OP(eng, "tensor_copy", [esk, "VE"], ["VE"], VE[:, b], v5[:, b, 0])
                step(VE[:, b + 1], VE[:, b], scanB, sct1[:, 0], sct2[:, 0], 0, ["VE"], ["VE"])
            vin = VE[:, 1:9]
            OP(eng, "tensor_copy", ["VE"], [ck], carry[d][:], VE[:, 0])
        for hb in range(4):
            bs = slice(2 * hb, 2 * hb + 2)
            sr = v5[:, bs, :, 0, :]
            si = v5[:, bs, :, 1, :]
            pr = bc(PW[:, d, 0], 1, 2)
            pi = bc(PW[:, d, 1], 1, 2)
            vr = bc(vin[:, bs, 0, :], 2, 16)
            vi = bc(vin[:, bs, 1, :], 2, 16)
            for (pa, va, tgt, op) in ((pr, vr, sr, ALU.add), (pi, vi, sr, ALU.subtract), (pr, vi, si, ALU.add),
                                      (pi, vr, si, ALU.add)):
                OP(eng, "tensor_tensor", ["PW", "VE"], ["sctb"], sctb[:], pa, va, ALU.mult)
                OP(eng, "tensor_tensor", ["sctb", esk], [esk], tgt, tgt, sctb[:], op)
        if not reverse:
            OP(eng, "tensor_copy", ["VE", sxk], [sxk], SXo[:, :, :, 0], VE[:, 0])
            OP(eng, "tensor_copy", [esk, sxk], [sxk], SXo[:, :, :, 1:NSUB].rearrange("p r g n -> p n r g"),
               ES[:, 0:NSUB - 1, :, :])
        else:
            OP(eng, "tensor_copy", ["VE", sxk], [sxk], SXo[:, :, :, NSUB - 1], VE[:, 8])
            OP(eng, "tensor_copy", [esk, sxk], [sxk], SXo[:, :, :, 0:NSUB - 1].rearrange("p r g n -> p n r g"),
               ES[:, 1:NSUB, :, :])

    def bwd_pass_gen(s):
        OP("pool", "memset", [("carry", 1)], [("carry", 1)], carry[1][:], 0.0)
        for m in range(NMT[s] - 1, -1, -1):
            DMA("pool", "ebld", [("ebd", s, m)], ["ESr"], ESr[:].rearrange("p n r g -> p (n r g)"), EB_s[s][m])
            scan2(ESr, "ESr", 1, True, Sxr, "Sxr")
            DMA("pool", "sbst", ["Sxr"], [("sbd", s, m)], SB_s[s][m], Sxr[:].rearrange("p r g n -> p (r g n)"))
            yield

    def bwd_pass(s):
        for _ in bwd_pass_gen(s):
            pass

    def load_x(src_ap, base, rk=(), xi=0):
        for r in range(4):
            DMA("act", "xld%d" % r, list(rk), [xk(xi, r)], XT[xi][:, r, :], src_ap[base + r * 128:base + (r + 1) * 128, :])

    def P_gen(s, m, xi):
        load_x(x_in[s], m * TT, (), xi)
        yield
        yield from norm_gen(0, s, xi, 0)

    def Q_gen(s, m, xi):
        base = m * TT
        for _ in range(2):
            if deferred:
                cast_dma(*deferred.pop(0))
        DMA("act", "x1st", xk(xi), [("x1d", s, m)], X1_s[s][base:base + TT, :].rearrange("(r p) d -> p r d", p=128),
            XT[xi][:])
        wt, wk = wload("q", Wmi_s[0], 4096)
        yield
        yield from norm_gen(1, s, xi, 1)
        wt2, wk2 = wload("q", SIN_s[0], 4096)
        yield from s5_u_gen(wt, wk, 1)
        if m == 0:
            OP("pool", "memset", [("carry", 0)], [("carry", 0)], carry[0][:], 0.0)
        yield from statein_gen(wt2, wk2, ESf, "ESf")
        wt3, wk3 = wload("q", SIN_s[1], 4096)
        scan2(ESf, "ESf", 0, False, Sx[0], ("Sx", 0))
        DMA("pool", "sfst", [("Sx", 0)], [("sfd", s, m)], SF_s[s][m], Sx[0][:].rearrange("p r g n -> p (r g n)"))
        yield
        yield from statein_gen(wt3, wk3, ESb, "ESb")
        DMA("act", "ebst", ["ESb"], [("ebd", s, m)], EB_s[s][m], ESb[:].rearrange("p n r g -> p (n r g)"))
        yield

    def PB_gen(s, m, xi):
        load_x(X1_s[s], m * TT, [("x1d", s, m)], xi)
        yield
        yield from norm_gen(1, s, xi, 0)

    def phase_b_mixer(s, m, xi):
        base = m * TT
        xt, hT = XT[xi], HT[0]
        DMA("act", "sfld", [("sfd", s, m)], [("Sx", 0)], Sx[0][:].rearrange("p r g n -> p (r g n)"), SF_s[s][m])
        DMA("act", "sbld", [("sbd", s, m)], [("Sx", 1)], Sx[1][:].rearrange("p r g n -> p (r g n)"), SB_s[s][m])
        wt, wk = take(("wmi",), "w", Wmi_s[0], 4096)
        s5_u(wt, wk)
        wu, wuk = wload("w", Wmi_s[1], 4096)
        wv_, wvk = wload("w", Wmi_s[2], 4096)
        wuv = wu.rearrange("p (dt f) -> p dt f", dt=8)
        wvv = wv_.rearrange("p (dt f) -> p dt f", dt=8)
        nxt_y = wload("o", YC_s[0], 3072)
        prefetch(("wgl",), "w", Wgl_s, 2048)

        def g_tail(r):
            p = r % 2
            tok = slice(r * 128, (r + 1) * 128)
            for ct in range(4):
                OP("pe", "transpose", [("ygn", p), "ident_b"], ["tra"], TRA[:, ct, :], ygn[p][:, ct * 128:(ct + 1) * 128],
                   ident_b[:])
            OP("act", "activation", ["tra"], [("ycat", 1)], ycatT[:, 4:8, tok], TRA[:, 0:4, :], AF.Copy)

        for r in range(4):
            p = r % 2
            ft = r
            tok = slice(r * 128, (r + 1) * 128)
            for dt in range(8):
                OP("pe", "matmul", [wuk, ("hT", 0)], ["pg0"], PG[0][:], lhsT=hT[:, dt, tok], rhs=wuv[:, dt, :], start=(dt == 0),
                   stop=(dt == 7))
            for dt in range(8):
                OP("pe", "matmul", [wvk, ("hT", 0)], ["pg1"], PG[1][:], lhsT=hT[:, dt, tok], rhs=wvv[:, dt, :], start=(dt == 0),
                   stop=(dt == 7))
            OP("act", "activation", ["pg0"], ["ug"], ug[p][:], PG[0][:], AF.Gelu_apprx_tanh)
            OP("act", "activation", ["pg1"], ["vg"], vg[p][:], PG[1][:], AF.Gelu_apprx_tanh)
            OP("dve", "bn_stats", ["vg"], ["bnst"], bnst[:], vg[p][:])
            OP("dve", "bn_aggr", ["bnst"], ["bnmv"], bnmv[:], bnst[:])
            OP("act", "activation", ["bnmv"], ["bnrs"], st_rstd[:, 4:5], bnmv[:, 1:2], AF.Sqrt, bias=eps_t[:], scale=1.0)
            OP("dve", "reciprocal", ["bnrs"], ["bnrs"], st_rstd[:, 4:5], st_rstd[:, 4:5])
            OP("dve", "tensor_scalar", ["vg", "bnmv", "bnrs"], ["vg"], vg[p][:], vg[p][:], bnmv[:, 0:1],
               st_rstd[:, 4:5], ALU.subtract, ALU.mult)
            OP("dve", "tensor_tensor", ["vg", "lng_rep"], ["vg"], vg[p][:], vg[p][:], lng_rep[:], ALU.mult)
            OP("dve", "tensor_tensor", ["vg", "lnb_rep"], [("vn2", p)], vn2[p][:], vg[p][:], lnb_rep[:], ALU.add)
            yc, yck = nxt_y
            if ft + 1 < 4:
                nxt_y = wload("o", YC_s[ft + 1], 3072)
            bdv = yc[:, 0:1024].rearrange("p (d l c) -> p d l c", d=2, l=TAU)
            sov = yc[:, 1024:3072].rearrange("p (g d i r c) -> p g d i r c", g=4, d=2, i=TAU, r=2)
            b = 2 + ft % 2
            for i in range(TAU):
                reg = PG[b][:, i * NSUB:(i + 1) * NSUB]
                first = True
                for j in range(TAU):
                    if j <= i:
                        OP("pe", "matmul", [yck, ("Ud", ft)], [PGK[b]], reg, lhsT=bdv[:, 0, i - j, :], rhs=Ud[:, ft, j, :],
                           start=first, stop=False)
                        first = False
                    if j >= i:
                        OP("pe", "matmul", [yck, ("Ud", ft)], [PGK[b]], reg, lhsT=bdv[:, 1, j - i, :], rhs=Ud[:, ft, j, :],
                           start=first, stop=False)
                        first = False
                for qq in range(4):
                    gp = 4 * ft + qq
                    for d in range(2):
                        for ri in range(2):
                            last = (qq == 3 and d == 1 and ri == 1)
                            OP("pe", "matmul", [yck, ("Sx", d)], [PGK[b]],
                               PG[b][32 * qq:32 * qq + 32, i * NSUB:(i + 1) * NSUB],
                               lhsT=sov[:, qq, d, i, ri, :], rhs=Sx[d][:, ri, gp, :], start=False, stop=last,
                               tile_position=(0, 32 * qq))
            OP("act", "activation", [PGK[b]], [("y1f", ft)], y1f[:, ft, :].rearrange("p (n i) -> p n i", i=TAU),
               PG[b][:].rearrange("p (i n) -> p n i", i=TAU), AF.Gelu_apprx_tanh)
            OP("act", "activation", [("y1f", ft)], [("y1b", ft)], y1b[:, ft, :], y1f[:, ft, :], AF.Copy)
            for h in range(4):
                OP("pe", "matmul", ["wspT", ("vn2", p)], [POK[p]], PO[p][:, h * 128:(h + 1) * 128], lhsT=wspT[:, h, :],
                   rhs=vn2[p][:, h * 128:(h + 1) * 128], start=True, stop=True)
            for h in range(4):
                OP("dve", "scalar_tensor_tensor", [POK[p], "bsp_c", "ug"], ["ygm"], ygm[p][:, h * 128:(h + 1) * 128],
                   PO[p][:, h * 128:(h + 1) * 128], bsp_c[:, h:h + 1], ug[p][:, h * 128:(h + 1) * 128], ALU.add, ALU.mult)
            OP("act", "activation", ["ygm"], [("ygn", p), "gssq"], ygn[p][:], ygm[p][:], AF.Square,
               accum_out=st_ssq[:, 5:6])
            OP("act", "activation", ["gssq"], ["grs"], st_rstd[:, 5:6], st_ssq[:, 5:6], AF.Sqrt, bias=eps_t[:], scale=1.0 / 512)
            OP("dve", "reciprocal", ["grs"], ["grs"], st_rstd[:, 5:6], st_rstd[:, 5:6])
            OP("act", "activation", ["ygm", "grs"], [("ygn", p)], ygn[p][:], ygm[p][:], AF.Identity,
               scale=st_rstd[:, 5:6])
            if r > 0:
                g_tail(r - 1)
        g_tail(3)
        wg, wgk = take(("wgl",), "w", Wgl_s, 2048)
        prefetch(("win", 1), "w", Win_s[1][0], 4096)
        prefetch(("wmo",), "o", Wmo_s[0], 1024)
        wgv = wg.rearrange("p (kt f) -> p kt f", kt=4)
        y1bk = [("y1b", ft) for ft in range(4)]
        for fo in range(4):
            b = fo % 2
            for kt in range(4):
                OP("pe", "matmul", [wgk] + y1bk, [POK[b]], PO[b][:], lhsT=wgv[:, kt, fo * 128:(fo + 1) * 128], rhs=y1b[:, kt, :],
                   start=(kt == 0), stop=(kt == 3))
            OP("act", "activation", [POK[b]], [("sg", b)], sg[b][:], PO[b][:], AF.Sigmoid)
            OP("dve", "tensor_tensor", [("y1f", fo), ("sg", b)], [("y1f", fo)], y1f[:, fo, :], y1f[:, fo, :], sg[b][:], ALU.mult)
        for fo in range(4):
            OP("act", "activation", [("y1f", fo)], [("y1b", fo)], sqb[:, fo, :], y1f[:, fo, :], AF.Square)
        for fo in range(4):
            OP("pe", "matmul", ["ones_b", ("y1b", fo)], ["po0"], PO[0][:], lhsT=ones_b[:], rhs=sqb[:, fo, :], start=(fo == 0),
               stop=(fo == 3))
        OP("act", "activation", ["po0"], ["ntmp"], rs5[:], PO[0][:], AF.Sqrt, bias=eps_t[:], scale=1.0 / 512)
        OP("dve", "reciprocal", ["ntmp"], ["ntmp"], rs5[:], rs5[:])
        OP("dve", "tensor_tensor", [("y1f", fo) for fo in range(4)] + ["ntmp"], [("ycat", 0)], ycatT[:, 0:4, :], y1f[:],
           bc(rs5[:], 1, 4), ALU.mult)
        nxt = take(("wmo",), "o", Wmo_s[0], 1024)
        for do in range(8):
            wt, wk = nxt
            if do + 1 < 8:
                nxt = wload("o", Wmo_s[do + 1], 1024)
            wv = wt.rearrange("p (kt f) -> p kt f", kt=8)
            b = do % 2
            for kt in range(8):
                OP("pe", "matmul", [wk, ("ycat", kt // 4)], [POK[b]], PO[b][:], lhsT=wv[:, kt, :], rhs=ycatT[:, kt, :],
                   start=(kt == 0), stop=(kt == 7))
            epilogue(b, modc[:, 5, s, do:do + 1], do, xi)

    def final_norm(s, m, xi):
        base = m * TT
        xt = XT[xi]
        for r in range(4):
            sl = r % 2
            OP("act", "activation", [xk(xi, r)], [("xn", sl), ("ssq", r)], xn[sl][:], xt[:, r, :], AF.Square,
               accum_out=st_ssq[:, r:r + 1])
            OP("act", "activation", [("ssq", r)], [("rstd", r)], st_rstd[:, r:r + 1], st_ssq[:, r:r + 1], AF.Sqrt,
               bias=eps_t[:], scale=1.0 / D)
            OP("dve", "reciprocal", [("rstd", r)], [("rstd", r)], st_rstd[:, r:r + 1], st_rstd[:, r:r + 1])
            OP("dve", "scalar_tensor_tensor", [xk(xi, r), ("rstd", r), "fg_rep"], ["ntmp"],
               ntmp[:].rearrange("p a b -> p (a b)"), xt[:, r, :], st_rstd[:, r:r + 1], fg_rep[:], ALU.mult, ALU.mult)
            DMA("act", "yo", ["ntmp"], [], y_out[s][base + r * 128:base + (r + 1) * 128, :],
                ntmp[:].rearrange("p a b -> p (a b)"))

    mk_ph = M.mark()
    ESf = M.sb("ESf", [128, NSUB, 2, 16], F32)
    ESb = M.sb("ESb", [128, NSUB, 2, 16], F32)
    XT[1] = M.sb("xt1", [128, 4, D], F32)
    HT[1] = M.sb("hT1", [128, 8, TT], BF16)
    rings["q"] = [M.sb("qbuf%d" % i, [128, 4096], BF16) for i in range(2)]
    tiles = [(s_, m_) for s_ in range(2) for m_ in range(NMT[s_])]
    bgen = None
    drain(P_gen(tiles[0][0], tiles[0][1], 0))
    qprev = None
    for i, (s_, m_) in enumerate(tiles):
        xi = i % 2
        prefetch(("wout", 0), "o", Wout_s[0][0], NFT * 128)
        ffn_step1(0, 0, qprev)
        drain(qprev)
        pnext = P_gen(tiles[i + 1][0], tiles[i + 1][1], 1 - xi) if i + 1 < len(tiles) else None
        if pnext is not None:
            prefetch(("win", 0), "w", Win_s[0][0], 4096)
        ffn_step2(0, s_, xi, pnext)
        drain(pnext)
        qprev = Q_gen(s_, m_, xi)
    drain(qprev)
    while deferred:
        cast_dma(*deferred.pop(0))
    S.barrier()
    M.release(mk_ph)
    rings["w"].append(M.sb("wbuf2", [128, 4096], BF16))
    lng_rep = M.sb("lng_rep", [128, 512], F32)
    lnb_rep = M.sb("lnb_rep", [128, 512], F32)
    fg_rep = M.sb("fg_rep", [128, D], F32)
    DMA("act", "pl0", [], ["lng_rep"], lng_rep[:], W["gmlp_ln_g"].partition_broadcast(128))
    DMA("act", "pl1", [], ["lnb_rep"], lnb_rep[:], W["gmlp_ln_b"].partition_broadcast(128))
    DMA("act", "pl2", [], ["fg_rep"], fg_rep[:], W["final_norm_g"].partition_broadcast(128))
    Sx[1] = M.sb("Sx1", [128, 2, 16, NSUB], BF16)
    y1f = M.sb("y1f", [128, 4, TT], F32)
    y1b = M.sb("y1b", [128, 4, TT], BF16)
    rs5 = ntmp[:].rearrange("p a b -> p (a b)")[:, 0:TT]
    XT[1] = M.sb("xt1b", [128, 4, D], F32)
    ycatT = M.sb("ycatT", [128, 8, TT], BF16)
    ug = [M.sb("ug0", [128, 512], F32)] * 2
    vg = [M.sb("vg0", [128, 512], F32)] * 2
    vn2 = [M.sb("vn2_%d" % i, [128, 512], BF16) for i in range(2)]
    ygm = [M.sb("ygm0", [128, 512], F32)] * 2
    ygn = [M.sb("ygn%d" % i, [128, 512], BF16) for i in range(2)]
    bnst = M.sb("bnst", [128, 6], F32)
    bnmv = M.sb("bnmv", [128, 2], F32)
    sqb = y1b
    bwd_pass(1)
    bwd_pass(0)
    tilesB = [(s_, m_) for s_ in (1, 0) for m_ in range(NMT[s_] - 1, -1, -1)]
    drain(PB_gen(tilesB[0][0], tilesB[0][1], 0))
    for i, (s_, m_) in enumerate(tilesB):
        xi = i % 2
        phase_b_mixer(s_, m_, xi)
        norm_hT(2, s_, xi, 0)
        prefetch(("wout", 1), "o", Wout_s[1][0], NFT * 128)
        ffn_step1(1, 0)
        pnext = PB_gen(tilesB[i + 1][0], tilesB[i + 1][1], 1 - xi) if i + 1 < len(tilesB) else None
        if pnext is not None:
            prefetch(("wmi",), "w", Wmi_s[0], 4096)
        ffn_step2(1, s_, xi, pnext)
        drain(pnext)
        final_norm(s_, m_, xi)

    S.barrier()
    S.emit()
    M.release(0)
    S.close()
    return nc


_NC_CACHE = {}


def kernel(**inputs):
    x_prompt = np.asarray(inputs["x_prompt"], dtype=np.float32)
    x_sample = np.asarray(inputs["x_sample"], dtype=np.float32)
    c_prompt = np.asarray(inputs["c_prompt"], dtype=np.float32)
    c_sample = np.asarray(inputs["c_sample"], dtype=np.float32)
    B, LP, _ = x_prompt.shape
    _, LS, _ = x_sample.shape
    n = 8
    key = (LP, LS)
    if key not in _NC_CACHE:
        _NC_CACHE[key] = build(LP, LS)
    nc = _NC_CACHE[key]
    wmap = {}
    for nm in WNAMES:
        a = np.asarray(inputs[nm], dtype=np.float32)
        if nm != "final_norm_g":
            a = a[0]
        wmap[nm] = np.ascontiguousarray(a)
    in_maps = []
    for i in range(n):
        mp = dict(wmap)
        mp["x_p"] = np.ascontiguousarray(x_prompt[i])
        mp["x_s"] = np.ascontiguousarray(x_sample[i])
        mp["c"] = np.ascontiguousarray(np.stack([c_prompt[i], c_sample[i]], axis=0))
        in_maps.append(mp)
    res = run_bass_kernel_spmd(nc, in_maps, core_ids=list(range(n)))
    yp = np.stack([np.asarray(res.results[i]["y_p"], dtype=np.float32) for i in range(n)], axis=0)
    ys = np.stack([np.asarray(res.results[i]["y_s"], dtype=np.float32) for i in range(n)], axis=0)
    return (yp, ys)
```

```python
import numpy as np
import concourse.bass as bass
import concourse.mybir as mybir
from concourse.bass_utils import run_bass_kernel_spmd

F32 = mybir.dt.float32
BF16 = mybir.dt.bfloat16
I32 = mybir.dt.int32
AF = mybir.ActivationFunctionType
ALU = mybir.AluOpType

ENGS = ("pe", "act", "dve", "pool", "sp")
D = 1024
DFF = 2816
NFT = 22
TAU = 4
TT = 512
NSUB = TT // TAU
EPS = 1e-6
PI = 3.14159265358979


class Sched:
    def __init__(self, nc):
        self.nc = nc
        self.q = {e: [] for e in ENGS}
        self.cnt = {e: 0 for e in ENGS}
        self.esem = {}
        self.last_w = {}
        self.readers = {}
        self.seen = {e: {} for e in ENGS}
        self.dsem = {}
        self._ctx = []

    def open(self):
        for e in ENGS:
            cm = self.nc.semaphore("es_" + e)
            self.esem[e] = cm.__enter__()
            self._ctx.append(cm)

    def dma_sem(self, name):
        if name not in self.dsem:
            cm = self.nc.semaphore("ds_" + name)
            h = cm.__enter__()
            self._ctx.append(cm)
            self.dsem[name] = [h, 0]
        return name

    def close(self):
        for cm in reversed(self._ctx):
            cm.__exit__(None, None, None)

    def _need(self, eng, ev, waits):
        if ev is None:
            return
        sk, val = ev
        if sk == ("e", "pe") and eng == "pe":
            return
        if self.seen[eng].get(sk, 0) >= val:
            return
        self.seen[eng][sk] = val
        waits[sk] = max(waits.get(sk, 0), val)

    def _deps(self, eng, reads, writes):
        waits = {}
        for k in reads:
            self._need(eng, self.last_w.get(k), waits)
        for k in writes:
            self._need(eng, self.last_w.get(k), waits)
            for sk, val in self.readers.get(k, {}).items():
                self._need(eng, (sk, val), waits)
        return waits

    def _commit(self, ev, reads, writes):
        for k in reads:
            d = self.readers.setdefault(k, {})
            d[ev[0]] = max(d.get(ev[0], 0), ev[1])
        for k in writes:
            self.last_w[k] = ev
            self.readers[k] = {}

    def op(self, eng, fn, reads=(), writes=()):
        waits = self._deps(eng, reads, writes)
        self.cnt[eng] += 1
        ev = (("e", eng), self.cnt[eng])
        self.q[eng].append((waits, fn, None))
        self._commit(ev, reads, writes)
        return ev

    def dma(self, eng, fn, sem, reads=(), writes=()):
        self.dma_sem(sem)
        waits = self._deps(eng, reads, writes)
        if self.dsem[sem][1]:
            self._need(eng, (("d", sem), self.dsem[sem][1]), waits)
        self.dsem[sem][1] += 16
        ev = (("d", sem), self.dsem[sem][1])
        self.q[eng].append((waits, fn, sem))
        self._commit(ev, reads, writes)
        return ev

    def wait_all(self, eng):
        waits = {}
        for e in ENGS:
            if self.cnt[e] and e != eng:
                self._need(eng, (("e", e), self.cnt[e]), waits)
        for name, (h, c) in self.dsem.items():
            if c:
                self._need(eng, (("d", name), c), waits)
        if waits:
            self.q[eng].append((waits, None, None))

    def barrier(self):
        for e in ENGS:
            self.wait_all(e)

    def _semh(self, sk):
        return self.esem[sk[1]] if sk[0] == "e" else self.dsem[sk[1]][0]

    def emit(self):
        nc = self.nc
        with nc.Block() as block:
            def mk(ename):
                def body(eng):
                    for waits, fn, dsem in self.q[ename]:
                        for sk, val in waits.items():
                            eng.wait_ge(self._semh(sk), val)
                        if fn is None:
                            continue
                        ins = fn(eng)
                        if dsem is not None:
                            ins.then_inc(self.dsem[dsem][0], 16)
                        else:
                            ins.then_inc(self.esem[ename], 1)
                return body
            block.tensor(mk("pe"))
            block.scalar(mk("act"))
            block.vector(mk("dve"))
            block.gpsimd(mk("pool"))
            block.sync(mk("sp"))


class Pool_:
    def __init__(self, nc):
        self.nc = nc
        self.stack = []

    def sb(self, name, shape, dt):
        cm = self.nc.sbuf_tensor(name, list(shape), dt)
        t = cm.__enter__()
        self.stack.append(cm)
        return t

    def ps(self, name, shape, dt):
        cm = self.nc.psum_tensor(name, list(shape), dt)
        t = cm.__enter__()
        self.stack.append(cm)
        return t

    def mark(self):
        return len(self.stack)

    def release(self, mark):
        while len(self.stack) > mark:
            self.stack.pop().__exit__(None, None, None)


WNAMES = ["w_ada", "b_ada", "norm_ffn1_g", "ffn1_w_in", "ffn1_w_out", "norm_mix_g", "w_mix_in",
          "s5_lam_re_f", "s5_lam_im_f", "s5_log_step_f", "s5_b_re_f", "s5_b_im_f", "s5_c_re_f", "s5_c_im_f",
          "s5_lam_re_b", "s5_lam_im_b", "s5_log_step_b", "s5_b_re_b", "s5_b_im_b", "s5_c_re_b", "s5_c_im_b",
          "s5_d", "s5_w_glu", "gmlp_ln_g", "gmlp_ln_b", "gmlp_w_sp", "gmlp_b_sp",
          "norm_out_s5_g", "norm_out_gmlp_g", "w_mix_out", "norm_ffn2_g", "ffn2_w_in", "ffn2_w_out",
          "final_norm_g"]
WSHAPES = {
    "w_ada": [D, 9 * D], "b_ada": [9 * D], "norm_ffn1_g": [D], "ffn1_w_in": [D, 2 * DFF],
    "ffn1_w_out": [DFF, D], "norm_mix_g": [D], "w_mix_in": [D, 1536],
    "s5_lam_re_f": [32, 64], "s5_lam_im_f": [32, 64], "s5_log_step_f": [32],
    "s5_b_re_f": [32, 64, 16], "s5_b_im_f": [32, 64, 16], "s5_c_re_f": [32, 16, 64], "s5_c_im_f": [32, 16, 64],
    "s5_lam_re_b": [32, 64], "s5_lam_im_b": [32, 64], "s5_log_step_b": [32],
    "s5_b_re_b": [32, 64, 16], "s5_b_im_b": [32, 64, 16], "s5_c_re_b": [32, 16, 64], "s5_c_im_b": [32, 16, 64],
    "s5_d": [512], "s5_w_glu": [512, 512], "gmlp_ln_g": [512], "gmlp_ln_b": [512],
    "gmlp_w_sp": [4, 128, 128], "gmlp_b_sp": [4, 128],
    "norm_out_s5_g": [512], "norm_out_gmlp_g": [512], "w_mix_out": [D, D], "norm_ffn2_g": [D],
    "ffn2_w_in": [D, 2 * DFF], "ffn2_w_out": [DFF, D], "final_norm_g": [D],
}


def build(LP, LS, dbg=None):
    nc = bass.Bass("TRN2", target_bir_lowering=False)
    S = Sched(nc)
    S.open()
    M = Pool_(nc)
    LEN = [LP, LS]
    NMT = [LP // TT, LS // TT]

    def din(name, shape, dt=F32):
        return nc.dram_tensor(name, list(shape), dt, kind="ExternalInput").ap()

    x_in = [din("x_p", [LP, D]), din("x_s", [LS, D])]
    c_in = din("c", [2, D])
    W = {n: din(n, WSHAPES[n]) for n in WNAMES}
    y_out = [nc.dram_tensor("y_p", [LP, D], F32, kind="ExternalOutput").ap(),
             nc.dram_tensor("y_s", [LS, D], F32, kind="ExternalOutput").ap()]
    dbg_out = {}
    if dbg:
        for k, shp in dbg.items():
            dbg_out[k] = nc.dram_tensor("dbg_" + k, list(shp), F32, kind="ExternalOutput").ap()

    def scr(name, shape, dt):
        return nc.dram_tensor(name, list(shape), dt, kind="Internal").ap()

    Win_s = [scr("win_s%d" % k, [11, 128, 4096], BF16) for k in range(2)]
    Wout_s = [scr("wout_s%d" % k, [8, 128, NFT * 128], BF16) for k in range(2)]
    Wmi_s = scr("wmi_s", [3, 128, 4096], BF16)
    Wmo_s = scr("wmo_s", [8, 128, 1024], BF16)
    Wgl_s = scr("wgl_s", [128, 2048], BF16)
    SIN_s = scr("sin_s", [2, 128, 4096], BF16)
    SOUT_s = scr("sout_s", [2, 128, 4096], BF16)
    BD_s = scr("bd_s", [128, 4096], BF16)
    YC_s = scr("yc_s", [4, 128, 3072], BF16)
    X1_s = [scr("x1_s%d" % s, [LEN[s], D], F32) for s in range(2)]
    SF_s = [scr("sf_s%d" % s, [NMT[s], 128, 4096], BF16) for s in range(2)]
    SB_s = [scr("sb_s%d" % s, [NMT[s], 128, 4096], BF16) for s in range(2)]
    EB_s = [scr("eb_s%d" % s, [NMT[s], 128, 4096], F32) for s in range(2)]

    def OP(eng, meth, reads, writes, *a, **kw):
        return S.op(eng, lambda e: getattr(e, meth)(*a, **kw), reads, writes)

    def DMA(eng, sem, reads, writes, out, in_, **kw):
        return S.dma(eng, lambda e: e.dma_start(out=out, in_=in_, **kw), sem, reads, writes)

    def bc(ap, axis, n):
        shp = list(ap.shape)
        shp.insert(axis, n)
        return ap.unsqueeze(axis).broadcast_to(shp)

    ident_f = M.sb("ident_f", [128, 128], F32)
    ident_b = M.sb("ident_b", [128, 128], BF16)
    ones_b = M.sb("ones_b", [128, 128], BF16)
    eps_t = M.sb("eps_t", [128, 1], F32)
    modc = M.sb("modc", [128, 9, 2, 8], F32)
    wspT = M.sb("wspT", [128, 4, 128], BF16)
    bsp_c = M.sb("bsp_c", [128, 4], F32)
    scanA = M.sb("scanA", [128, 2, 4, 16], F32)
    scanB = M.sb("scanB", [128, 2, 4, 16], F32)
    PW = M.sb("PW", [128, 2, 2, 16, 16], F32)

    PG = [M.ps("pg%d" % i, [128, 512], F32) for i in range(4)]
    PO = [M.ps("po%d" % i, [128, 512], F32) for i in range(2)]
    TRA = M.ps("tra", [128, 8, 128], BF16)
    TRB = M.ps("trb", [128, 2, 4, 128], BF16)
    PGK = ["pg0", "pg1", "pg2", "pg3"]
    POK = ["po0", "po1"]

    iot = M.sb("iot", [128, 128], I32)
    OP("pool", "iota", [], ["iot"], iot[:], [[1, 128]], base=0, channel_multiplier=-1)
    OP("dve", "tensor_scalar", ["iot"], ["ident_f"], ident_f[:], iot[:], 0.0, None, ALU.is_equal)
    OP("dve", "tensor_copy", ["ident_f"], ["ident_b"], ident_b[:], ident_f[:])
    OP("dve", "memset", [], ["ones_b"], ones_b[:], 1.0)
    OP("dve", "memset", [], ["eps_t"], eps_t[:], EPS)

    mk_pro = M.mark()
    cT = M.sb("cT", [128, 8, 2], F32)
    bcol = M.sb("bcol", [128, 72], F32)
    ngc = M.sb("ngc", [128, 3, 8], F32)
    stg = [M.sb("stg%d" % i, [128, 4096], F32) for i in range(2)]
    stb = [M.sb("stb%d" % i, [128, 4096], BF16) for i in range(2)]
    for s_ in range(2):
        DMA("act", "pl%d" % (6 + s_), [], [("cTl", s_)], cT[:, :, s_], c_in[s_].rearrange("(dt p) -> p dt", p=128),
            allow_slow_non_contiguous=True)
    DMA("act", "pl1", [], ["bcol"], bcol[:], W["b_ada"].rearrange("(ft p) -> p ft", p=128), allow_slow_non_contiguous=True)
    for j, nm in enumerate(["norm_ffn1_g", "norm_mix_g", "norm_ffn2_g"]):
        DMA("act", "pl%d" % (2 + j), [], [("ngc", j)], ngc[:, j, :], W[nm].rearrange("(dt p) -> p dt", p=128),
            allow_slow_non_contiguous=True)
    OP("act", "activation", [("cTl", 0), ("cTl", 1)], ["cT"], cT[:], cT[:], AF.Silu)
    modps = PG[0]
    wada_v = W["w_ada"].rearrange("(dt p) f -> p dt f", p=128)
    for ch in range(18):
        sl = ch % 2
        DMA("sp", "cst%d" % sl, [], [("stg", sl)], stg[sl][:].rearrange("p (dt f) -> p dt f", dt=8),
            wada_v[:, :, ch * 512:(ch + 1) * 512])
        sv = stg[sl][:].rearrange("p (dt f) -> p dt f", dt=8)
        for f4 in range(4):
            ft = ch * 4 + f4
            for dt in range(8):
                OP("pe", "matmul", [("stg", sl), "cT"], ["pg0"], modps[:, ft * 2:ft * 2 + 2],
                   lhsT=sv[:, dt, f4 * 128:(f4 + 1) * 128], rhs=cT[:, dt, :], start=(dt == 0), stop=(dt == 7))
    OP("dve", "tensor_tensor", ["pg0", "bcol"], ["modc"], modc[:].rearrange("p k s d -> p k d s"),
       modps[:, 0:144].rearrange("p (k d s) -> p k d s", k=9, d=8),
       bc(bcol[:].rearrange("p (k d) -> p k d", k=9), 3, 2), ALU.add)
    for j in range(3):
        OP("dve", "scalar_tensor_tensor", ["modc", ("ngc", j)], ["modc"], modc[:, 3 * j + 1, :, :],
           modc[:, 3 * j + 1, :, :], 1.0, bc(ngc[:, j, :], 1, 2), ALU.add, ALU.mult)
    for k in (2, 8):
        OP("dve", "tensor_scalar", ["modc"], ["modc"], modc[:, k, :, :], modc[:, k, :, :], 0.5, None, ALU.mult)

    DMA("act", "pl3", [], ["bsp_c"], bsp_c[:], W["gmlp_b_sp"].rearrange("h q -> q h"), allow_slow_non_contiguous=True)
    wsp_n = M.sb("wsp_n", [128, 4, 128], F32)
    DMA("act", "pl4", [], ["wsp_n"], wsp_n[:], W["gmlp_w_sp"].rearrange("h q k -> q h k"))
    for h in range(4):
        OP("pe", "transpose", ["wsp_n", "ident_f"], ["po0"], PO[0][:, h * 128:(h + 1) * 128], wsp_n[:, h, :], ident_f[:])
    OP("act", "activation", ["po0"], ["wspT"], wspT[:].rearrange("p h q -> p (h q)"), PO[0][:], AF.Copy)
    gcat = M.sb("gcat", [128, 8], F32)
    DMA("act", "pl5", [], [("gcat", 0)], gcat[:, 0:4], W["norm_out_s5_g"].rearrange("(k p) -> p k", p=128),
        allow_slow_non_contiguous=True)
    DMA("act", "pl6", [], [("gcat", 1)], gcat[:, 4:8], W["norm_out_gmlp_g"].rearrange("(k p) -> p k", p=128),
        allow_slow_non_contiguous=True)
    dcol = M.sb("dcol", [128, 4], F32)
    DMA("act", "pl7", [], ["dcol"], dcol[:], W["s5_d"].rearrange("(k p) -> p k", p=128), allow_slow_non_contiguous=True)

    cvt_i = [0]

    def convert(src, dst, nfree, scale_bc=None):
        i = cvt_i[0]
        cvt_i[0] += 1
        sl = i % 2
        sshape = list(src.shape)
        sv = stg[sl][:, 0:nfree]
        if len(sshape) == 3:
            sv = sv.rearrange("p (a b) -> p a b", a=sshape[1])
        elif len(sshape) == 4:
            sv = sv.rearrange("p (a b c) -> p a b c", a=sshape[1], b=sshape[2])
        if len(sshape) == 4:
            for a_ in range(sshape[2]):
                DMA("sp", "cst%d_%d" % (sl, a_), [], [("stg", sl)] if a_ == 0 else [("stgx", sl, a_)], sv[:, :, a_, :], src[:, :, a_, :])
        else:
            DMA("sp", "cst%d" % sl, [], [("stg", sl)], sv, src)
        if scale_bc is not None:
            OP("dve", "tensor_tensor", [("stg", sl), ("gcat", 0), ("gcat", 1)], [("stb", sl)],
               stb[sl][:, 0:nfree].rearrange("p (a b) -> p a b", a=sshape[1]), sv, scale_bc, ALU.mult)
        elif i % 3 == 0:
            OP("act", "activation", [("stg", sl), ("stgx", sl, 1)], [("stb", sl), ("stgx", sl, 1)], stb[sl][:, 0:nfree], stg[sl][:, 0:nfree], AF.Copy)
        elif i % 3 == 1:
            OP("dve", "tensor_copy", [("stg", sl), ("stgx", sl, 1)], [("stb", sl), ("stgx", sl, 1)], stb[sl][:, 0:nfree], stg[sl][:, 0:nfree])
        else:
            OP("pool", "tensor_copy", [("stg", sl), ("stgx", sl, 1)], [("stb", sl), ("stgx", sl, 1)], stb[sl][:, 0:nfree], stg[sl][:, 0:nfree])
        DMA("act", "cso%d" % sl, [("stb", sl)], [], dst, stb[sl][:, 0:nfree])

    cvd_i = [0]
    deferred = []

    def cast_dma(dst, src):
        i = cvd_i[0]
        cvd_i[0] += 1
        DMA("pool", "cv%d" % (i % 3), [], [], dst, src)

    for k, (wi, wo) in enumerate([("ffn1_w_in", "ffn1_w_out"), ("ffn2_w_in", "ffn2_w_out")]):
        wiv = W[wi].rearrange("(dt p) (gu f) -> p dt gu f", p=128, gu=2)
        wov = W[wo].rearrange("(ft p) d -> p ft d", p=128)
        jobs = []
        for c in range(11):
            for gu in range(2):
                jobs.append((Win_s[k][c].rearrange("p (dt gu f) -> p dt gu f", dt=8, gu=2)[:, :, gu, :],
                             wiv[:, :, gu, c * 256:(c + 1) * 256]))
        for do in range(8):
            jobs.append((Wout_s[k][do].rearrange("p (ft f) -> p ft f", ft=NFT), wov[:, :, do * 128:(do + 1) * 128]))
        if k == 0:
            for dst, src in jobs:
                cast_dma(dst, src)
        else:
            deferred.extend(jobs)
    wmv = W["w_mix_in"].rearrange("(dt p) f -> p dt f", p=128)
    for c3 in range(3):
        cast_dma(Wmi_s[c3].rearrange("p (dt f) -> p dt f", dt=8), wmv[:, :, c3 * 512:(c3 + 1) * 512])
    cast_dma(Wgl_s.rearrange("p (kt f) -> p kt f", kt=4), W["s5_w_glu"].rearrange("(kt p) f -> p kt f", p=128))
    wmo = W["w_mix_out"].rearrange("(kt p) d -> p kt d", p=128)
    for do in range(8):
        convert(wmo[:, :, do * 128:(do + 1) * 128], Wmo_s[do], 1024, scale_bc=bc(gcat[:], 2, 128))

    def s5t(name, shape=(128, 16), dt=F32):
        return M.sb(name, list(shape), dt)

    tmpA = s5t("tmpA"); tmpB = s5t("tmpB"); tmpC = s5t("tmpC")
    tmpI = s5t("tmpI", (128, 16), I32)
    bdst = M.sb("bdst", [128, 4, 2, 4, 128], F32)
    sinst = M.sb("sinst", [128, 4, 4, 2, 128], BF16)
    soutst = M.sb("soutst", [128, 16, 2, 4, 2, 32], BF16)
    bmask = M.sb("bmask", [128, 4, 32], F32)
    OP("dve", "memset", [], ["bmask"], bmask[:], 0.0)
    for q in range(4):
        OP("dve", "memset", ["bmask"], ["bmask"], bmask[32 * q:32 * q + 32, q, :], 1.0)
    ZB = [[M.sb("zb%d_%d" % (q, ri), [128, 128], F32) for ri in range(2)] for q in range(4)]
    for q in range(4):
        for ri in range(2):
            OP("dve", "memset", [], [("zb", q, ri)], ZB[q][ri][:], 0.0)
    pli = [0]

    def plsem():
        pli[0] += 1
        return "pl%d" % (pli[0] % 8)

    def sin_of(out, th, key_out, key_th):
        OP("dve", "tensor_scalar", [key_th], ["tmpA"], tmpA[:], th[:], 1.0 / (2 * PI), None, ALU.mult)
        OP("dve", "tensor_copy", ["tmpA"], ["tmpI"], tmpI[:], tmpA[:])
        OP("dve", "tensor_copy", ["tmpI"], ["tmpA"], tmpA[:], tmpI[:])
        OP("dve", "scalar_tensor_tensor", ["tmpA", key_th], ["tmpB"], tmpB[:], tmpA[:], -2 * PI, th[:], ALU.mult, ALU.add)
        OP("dve", "tensor_scalar", ["tmpB"], ["tmpA"], tmpA[:], tmpB[:], PI, None, ALU.is_gt)
        OP("dve", "scalar_tensor_tensor", ["tmpA", "tmpB"], ["tmpC"], tmpC[:], tmpA[:], -2 * PI, tmpB[:], ALU.mult, ALU.add)
        OP("dve", "tensor_scalar", ["tmpC"], ["tmpA"], tmpA[:], tmpC[:], -PI, None, ALU.is_lt)
        OP("dve", "scalar_tensor_tensor", ["tmpA", "tmpC"], ["tmpB"], tmpB[:], tmpA[:], 2 * PI, tmpC[:], ALU.mult, ALU.add)
        OP("act", "activation", ["tmpB"], [key_out], out[:], tmpB[:], AF.Sin)

    def TT_(eng, out, a, b, op, r, w):
        OP(eng, "tensor_tensor", r, w, out, a, b, op)

    for d, sfx in enumerate(["f", "b"]):
        pf = "d%d_" % d
        mk_d = M.mark()
        lre = s5t(pf + "lre"); lim = s5t(pf + "lim"); lsb = s5t(pf + "lsb")
        DMA("act", plsem(), [], [pf + "lre"], lre[:], W["s5_lam_re_" + sfx].rearrange("(gp two) p -> (two p) gp", two=2),
            allow_slow_non_contiguous=True)
        DMA("act", plsem(), [], [pf + "lim"], lim[:], W["s5_lam_im_" + sfx].rearrange("(gp two) p -> (two p) gp", two=2),
            allow_slow_non_contiguous=True)
        lsv = W["s5_log_step_" + sfx].rearrange("(gp two) -> two gp", two=2)
        for two in range(2):
            DMA("act", plsem(), [], [(pf + "lsb", two)], lsb[two * 64:(two + 1) * 64, :], lsv[two].partition_broadcast(64),
                allow_slow_non_contiguous=True)
        Bre = s5t(pf + "Bre", (128, 16, 16)); Bim = s5t(pf + "Bim", (128, 16, 16))
        DMA("act", plsem(), [], [pf + "Bre"], Bre[:], W["s5_b_re_" + sfx].rearrange("(gp two) p h -> (two p) gp h", two=2))
        DMA("act", plsem(), [], [pf + "Bim"], Bim[:], W["s5_b_im_" + sfx].rearrange("(gp two) p h -> (two p) gp h", two=2))
        CT = []
        for t, cn in enumerate(["s5_c_re_" + sfx, "s5_c_im_" + sfx]):
            CA = M.sb(pf + "CA%d" % t, [128, 2, 128], F32)
            for gp in range(16):
                DMA("act", plsem(), [], [(pf + "CA%d" % t, gp)],
                    CA[16 * (gp % 8):16 * (gp % 8) + 16, gp // 8, :].rearrange("ho (two p) -> ho two p", two=2),
                    W[cn][2 * gp:2 * gp + 2].rearrange("two ho p -> ho two p"))
            ct = s5t(pf + "CT%d" % t, (128, 16, 16))
            for half in range(2):
                OP("pe", "transpose", [(pf + "CA%d" % t, gp_) for gp_ in range(16)] + ["ident_f"], ["po1"], PO[1][:, half * 128:(half + 1) * 128],
                   CA[:, half, :], ident_f[:])
            OP("act", "activation", ["po1"], [pf + "CT%d" % t], ct[:].rearrange("p g h -> p (g h)"), PO[1][:, 0:256], AF.Copy)
            CT.append(ct)
        dtt = s5t(pf + "dt"); xr = s5t(pf + "xr"); xi = s5t(pf + "xi"); xi2 = s5t(pf + "xi2")
        mag = s5t(pf + "mag"); sn = s5t(pf + "sn"); cs = s5t(pf + "cs")
        OP("act", "activation", [(pf + "lsb", 0), (pf + "lsb", 1)], [pf + "dt"], dtt[:], lsb[:], AF.Exp)
        TT_("dve", xr[:], lre[:], dtt[:], ALU.mult, [pf + "lre", pf + "dt"], [pf + "xr"])
        TT_("dve", xi[:], lim[:], dtt[:], ALU.mult, [pf + "lim", pf + "dt"], [pf + "xi"])
        OP("dve", "tensor_scalar", [pf + "xi"], [pf + "xi2"], xi2[:], xi[:], PI / 2, None, ALU.add)
        OP("act", "activation", [pf + "xr"], [pf + "mag"], mag[:], xr[:], AF.Exp)
        sin_of(sn, xi, pf + "sn", pf + "xi")
        sin_of(cs, xi2, pf + "cs", pf + "xi2")
        APr = s5t(pf + "APr", (128, 5, 16)); APi = s5t(pf + "APi", (128, 5, 16))
        kr, ki = pf + "APr", pf + "APi"
        OP("dve", "memset", [], [kr], APr[:, 0, :], 1.0)
        OP("dve", "memset", [], [ki], APi[:, 0, :], 0.0)
        TT_("dve", APr[:, 1, :], mag[:], cs[:], ALU.mult, [pf + "mag", pf + "cs", kr], [kr])
        TT_("dve", APi[:, 1, :], mag[:], sn[:], ALU.mult, [pf + "mag", pf + "sn", ki], [ki])
        for k in range(2, 5):
            TT_("dve", tmpA[:], APr[:, k - 1, :], APr[:, 1, :], ALU.mult, [kr], ["tmpA"])
            TT_("dve", tmpB[:], APi[:, k - 1, :], APi[:, 1, :], ALU.mult, [ki], ["tmpB"])
            TT_("dve", APr[:, k, :], tmpA[:], tmpB[:], ALU.subtract, ["tmpA", "tmpB", kr], [kr])
            TT_("dve", tmpA[:], APr[:, k - 1, :], APi[:, 1, :], ALU.mult, [kr, ki], ["tmpA"])
            TT_("dve", tmpB[:], APi[:, k - 1, :], APr[:, 1, :], ALU.mult, [kr, ki], ["tmpB"])
            TT_("dve", APi[:, k, :], tmpA[:], tmpB[:], ALU.add, ["tmpA", "tmpB", ki], [ki])
        OP("dve", "tensor_copy", [kr], ["scanA"], scanA[:, d, 0, :], APr[:, TAU, :])
        OP("dve", "tensor_copy", [kr], ["scanA"], scanA[:, d, 1, :], APr[:, TAU, :])
        OP("dve", "tensor_scalar", [ki], ["scanA"], scanA[:, d, 2, :], APi[:, TAU, :], -1.0, None, ALU.mult)
        OP("dve", "tensor_copy", [ki], ["scanA"], scanA[:, d, 3, :], APi[:, TAU, :])
        def pwi(j):
            return (j - 1) if d == 0 else (16 - j)
        OP("dve", "tensor_copy", [kr], ["PW"], PW[:, d, 0, pwi(1), :], APr[:, TAU, :])
        OP("dve", "tensor_copy", [ki], ["PW"], PW[:, d, 1, pwi(1), :], APi[:, TAU, :])
        for j in range(2, 17):
            pr_, pi_ = PW[:, d, 0, pwi(j - 1), :], PW[:, d, 1, pwi(j - 1), :]
            TT_("dve", tmpA[:], pr_, APr[:, TAU, :], ALU.mult, ["PW", kr], ["tmpA"])
            TT_("dve", tmpB[:], pi_, APi[:, TAU, :], ALU.mult, ["PW", ki], ["tmpB"])
            TT_("dve", PW[:, d, 0, pwi(j), :], tmpA[:], tmpB[:], ALU.subtract, ["tmpA", "tmpB", "PW"], ["PW"])
            TT_("dve", tmpA[:], pr_, APi[:, TAU, :], ALU.mult, ["PW", ki], ["tmpA"])
            TT_("dve", tmpB[:], pi_, APr[:, TAU, :], ALU.mult, ["PW", kr], ["tmpB"])
            TT_("dve", PW[:, d, 1, pwi(j), :], tmpA[:], tmpB[:], ALU.add, ["tmpA", "tmpB", "PW"], ["PW"])
        OP("dve", "tensor_copy", ["PW"], ["scanB"], scanB[:, d, 0, :], PW[:, d, 0, pwi(16), :])
        OP("dve", "tensor_copy", ["PW"], ["scanB"], scanB[:, d, 1, :], PW[:, d, 0, pwi(16), :])
        OP("dve", "tensor_scalar", ["PW"], ["scanB"], scanB[:, d, 2, :], PW[:, d, 1, pwi(16), :], -1.0, None, ALU.mult)
        OP("dve", "tensor_copy", ["PW"], ["scanB"], scanB[:, d, 3, :], PW[:, d, 1, pwi(16), :])
        fr = s5t(pf + "fr"); fi = s5t(pf + "fi"); nr = s5t(pf + "nr"); den = s5t(pf + "den")
        OP("dve", "tensor_scalar", [kr], [pf + "nr"], nr[:], APr[:, 1, :], -1.0, None, ALU.add)
        TT_("dve", tmpA[:], lre[:], lre[:], ALU.mult, [pf + "lre"], ["tmpA"])
        TT_("dve", tmpB[:], lim[:], lim[:], ALU.mult, [pf + "lim"], ["tmpB"])
        TT_("dve", den[:], tmpA[:], tmpB[:], ALU.add, ["tmpA", "tmpB"], [pf + "den"])
        OP("dve", "reciprocal", [pf + "den"], [pf + "den"], den[:], den[:])
        TT_("dve", tmpA[:], nr[:], lre[:], ALU.mult, [pf + "nr", pf + "lre"], ["tmpA"])
        TT_("dve", tmpB[:], APi[:, 1, :], lim[:], ALU.mult, [ki, pf + "lim"], ["tmpB"])
        TT_("dve", tmpC[:], tmpA[:], tmpB[:], ALU.add, ["tmpA", "tmpB"], ["tmpC"])
        TT_("dve", fr[:], tmpC[:], den[:], ALU.mult, ["tmpC", pf + "den"], [pf + "fr"])
        TT_("dve", tmpA[:], APi[:, 1, :], lre[:], ALU.mult, [ki, pf + "lre"], ["tmpA"])
        TT_("dve", tmpB[:], nr[:], lim[:], ALU.mult, [pf + "nr", pf + "lim"], ["tmpB"])
        TT_("dve", tmpC[:], tmpA[:], tmpB[:], ALU.subtract, ["tmpA", "tmpB"], ["tmpC"])
        TT_("dve", fi[:], tmpC[:], den[:], ALU.mult, ["tmpC", pf + "den"], [pf + "fi"])
        t3a = s5t(pf + "t3a", (128, 16, 16)); t3b = s5t(pf + "t3b", (128, 16, 16))
        t3r = s5t(pf + "t3r", (128, 16, 16)); t3i = s5t(pf + "t3i", (128, 16, 16))

        def cmul(outr, outi, inr, ini, fre, fim, kin, kf, kout, neg_im=False):
            frb, fib = bc(fre, 2, 16), bc(fim, 2, 16)
            TT_("dve", t3a[:], inr, frb, ALU.mult, kin + kf, [pf + "t3a"])
            TT_("dve", t3b[:], ini, fib, ALU.mult, kin + kf, [pf + "t3b"])
            TT_("dve", outr, t3a[:], t3b[:], ALU.subtract, [pf + "t3a", pf + "t3b"], kout)
            TT_("dve", t3a[:], inr, fib, ALU.mult, kin + kf, [pf + "t3a"])
            TT_("dve", t3b[:], ini, frb, ALU.mult, kin + kf, [pf + "t3b"])
            if neg_im:
                OP("dve", "scalar_tensor_tensor", [pf + "t3a", pf + "t3b"], kout, outi, t3a[:], -1.0, t3b[:],
                   ALU.mult, ALU.subtract)
            else:
                TT_("dve", outi, t3a[:], t3b[:], ALU.add, [pf + "t3a", pf + "t3b"], kout)

        bbr = s5t(pf + "bbr", (128, 16, 16)); bbi = s5t(pf + "bbi", (128, 16, 16))
        cmul(bbr[:], bbi[:], Bre[:], Bim[:], fr[:], fi[:], [pf + "Bre", pf + "Bim"], [pf + "fr", pf + "fi"],
             [pf + "bb"])
        MBP = M.sb(pf + "MBP", [128, 2, 4, 16, 32], F32)
        MCP = M.sb(pf + "MCP", [128, 2, 5, 16, 32], F32)
        OP("pool", "memset", [], [pf + "MBP"], MBP[:], 0.0)
        OP("pool", "memset", [], [pf + "MCP"], MCP[:], 0.0)
        for e in range(4):
            cmul(t3r[:], t3i[:], bbr[:], bbi[:], APr[:, e, :], APi[:, e, :], [pf + "bb"], [kr, ki], [pf + "t3ri"])
            for ri, src in enumerate([t3r, t3i]):
                for two in range(2):
                    ps_ = slice(two * 64, two * 64 + 64)
                    OP("dve", "tensor_copy", [pf + "t3ri", pf + "MBP"], [pf + "MBP"],
                       MBP[ps_, ri, e, :, two * 16:(two + 1) * 16], src[ps_, :, :])
        for k in range(5):
            cmul(t3r[:], t3i[:], CT[0][:], CT[1][:], APr[:, k, :], APi[:, k, :], [pf + "CT0", pf + "CT1"], [kr, ki],
                 [pf + "t3ri"], neg_im=True)
            for ri, src in enumerate([t3r, t3i]):
                for two in range(2):
                    ps_ = slice(two * 64, two * 64 + 64)
                    OP("dve", "tensor_copy", [pf + "t3ri", pf + "MCP"], [pf + "MCP"],
                       MCP[ps_, ri, k, :, two * 16:(two + 1) * 16], src[ps_, :, :])
        for ftq in range(4):
            for j in range(4):
                e = (TAU - 1 - j) if d == 0 else j
                for ri in range(2):
                    bank = (j * 2 + ri) % 2
                    OP("pe", "transpose", [pf + "MBP", "ident_f"], [POK[bank]], PO[bank][:, 0:128],
                       MBP[:, ri, e, 4 * ftq:4 * ftq + 4, :].rearrange("p a b -> p (a b)"), ident_f[:])
                    OP("act", "activation", [POK[bank]], ["sinst"], sinst[:, ftq, j, ri, :], PO[bank][:, 0:128], AF.Copy)
        DMA("act", plsem(), ["sinst"], [], SIN_s[d], sinst[:].rearrange("p a b c e -> p (a b c e)"))
        for i in range(4):
            k = (i + 1) if d == 0 else (TAU - i)
            for ri in range(2):
                OP("dve", "tensor_copy", [pf + "MCP", "soutst"], ["soutst"], soutst[:, :, d, i, ri, :], MCP[:, ri, k, :, :])
        for ft in range(4):
            for q in range(4):
                gp = 4 * ft + q
                for ri in range(2):
                    OP("dve", "tensor_copy", [pf + "MBP", ("zb", q, ri)], [("zb", q, ri)],
                       ZB[q][ri][:, 32 * q:32 * q + 32], MBP[:, ri, 0, gp, :])
            for dl in range(4):
                bank = dl % 2
                for q in range(4):
                    gp = 4 * ft + q
                    for ri in range(2):
                        OP("pe", "matmul", [("zb", q, ri), pf + "MCP"], [POK[bank]], PO[bank][:, 0:32],
                           lhsT=ZB[q][ri][:], rhs=MCP[:, ri, dl, gp, :], start=(q == 0 and ri == 0),
                           stop=(q == 3 and ri == 1))
                TT_("dve", bdst[:, ft, d, dl, :].rearrange("p (q c) -> p q c", q=4), bc(PO[bank][:, 0:32], 1, 4),
                    bmask[:], ALU.mult, [POK[bank], "bmask", "bdst"], ["bdst"])
            if d == 0:
                OP("dve", "scalar_tensor_tensor", ["ident_f", "dcol", "bdst"], ["bdst"], bdst[:, ft, 0, 0, :],
                   ident_f[:], dcol[:, ft:ft + 1], bdst[:, ft, 0, 0, :], ALU.mult, ALU.add)
        S.barrier()
        M.release(mk_d)
    OP("act", "activation", ["bdst"], [("stb", 0)], stb[0][:], bdst[:].rearrange("p a b c e -> p (a b c e)"), AF.Copy)
    for ft in range(4):
        DMA("act", plsem(), [("stb", 0)], [], YC_s[ft][:, 0:1024], stb[0][:, ft * 1024:(ft + 1) * 1024])
        DMA("act", plsem(), ["soutst"], [], YC_s[ft][:, 1024:3072],
            soutst[:, 4 * ft:4 * ft + 4].rearrange("p a b c e f -> p (a b c e f)"))

    S.barrier()
    M.release(mk_pro)

    XT = [M.sb("xt0", [128, 4, D], F32), None]
    HT = [M.sb("hT0", [128, 8, TT], BF16), None]

    def xk(xi, r=None):
        return [("xt", xi, r_) for r_ in range(4)] if r is None else ("xt", xi, r)
    xn = [M.sb("xn%d" % i, [128, D], BF16) for i in range(2)]
    ntmp = M.sb("ntmp", [128, 8, 128], F32)
    hh = M.sb("hh", [128, NFT, TT], BF16)
    sg = [M.sb("sg%d" % i, [128, TT], BF16) for i in range(2)]
    ob = [M.sb("ob0", [128, TT], BF16)] * 2
    st_ssq = M.sb("st_ssq", [128, 8], F32)
    st_rstd = M.sb("st_rstd", [128, 8], F32)
    Ud = M.sb("Ud", [128, 4, TAU, NSUB], BF16)
    ESr = M.sb("ESr", [128, NSUB, 2, 16], F32)
    Sxr = M.sb("Sxr", [128, 2, 16, NSUB], BF16)
    VE = M.sb("VE", [128, 9, 2, 16], F32)
    sct1 = M.sb("sct1", [128, 8, 2, 16], F32)
    sct2 = M.sb("sct2", [128, 8, 2, 16], F32)
    sctb = M.sb("sctb", [128, 2, 16, 16], F32)
    carry = [M.sb("carry%d" % d, [128, 2, 16], F32) for d in range(2)]
    Sx = [M.sb("Sx0", [128, 2, 16, NSUB], BF16), None]
    rings = {"w": [M.sb("wbuf%d" % i, [128, 4096], BF16) for i in range(2)],
             "o": [M.sb("obuf%d" % i, [128, 3072], BF16) for i in range(2)],
             "q": []}
    ring_i = {"w": 0, "o": 0, "q": 0}

    def wload(kind, src, nfree):
        i = ring_i[kind]
        ring_i[kind] += 1
        sl = i % len(rings[kind])
        buf = rings[kind][sl]
        key = (kind + "buf", sl)
        DMA("sp", "%s%d" % (kind, sl), [], [key], buf[:, 0:nfree], src)
        return buf[:, 0:nfree], key

    def pump(g):
        if g is not None:
            next(g, None)

    def drain(g):
        if g is not None:
            for _ in g:
                pass

    def norm_gen(site, s, xi, hi):
        xt, hT = XT[xi], HT[hi]
        gs = modc[:, 3 * site + 1, s, :]
        sh = modc[:, 3 * site + 0, s, :]
        for r in range(4):
            sl = r % 2
            OP("act", "activation", [xk(xi, r)], [("xn", sl), ("ssq", r)], xn[sl][:], xt[:, r, :], AF.Square,
               accum_out=st_ssq[:, r:r + 1])
        ssk = [("ssq", r) for r in range(4)]
        rsk = [("rstd", r) for r in range(4)]
        OP("act", "activation", ssk, rsk, st_rstd[:, 0:4], st_ssq[:, 0:4], AF.Sqrt, bias=eps_t[:], scale=1.0 / D)
        OP("dve", "reciprocal", rsk, rsk, st_rstd[:, 0:4], st_rstd[:, 0:4])
        yield

        def tail(r):
            sl = r % 2
            for dt in range(8):
                OP("pe", "transpose", [("xn", sl), "ident_b"], ["tra"], TRA[:, dt, :], xn[sl][:, dt * 128:(dt + 1) * 128],
                   ident_b[:])
            OP("dve", "tensor_tensor", ["tra", "modc"], ["ntmp"], ntmp[:], TRA[:], bc(gs, 2, 128), ALU.mult)
            OP("dve", "tensor_tensor", ["ntmp", "modc"], [("hT", hi)], hT[:, :, r * 128:(r + 1) * 128], ntmp[:],
               bc(sh, 2, 128), ALU.add)

        for r in range(4):
            sl = r % 2
            OP("act", "activation", [xk(xi, r), ("rstd", r)], [("xn", sl)], xn[sl][:], xt[:, r, :], AF.Identity,
               scale=st_rstd[:, r:r + 1])
            if r > 0:
                tail(r - 1)
            yield
        tail(3)
        yield

    def norm_hT(site, s, xi=0, hi=0):
        drain(norm_gen(site, s, xi, hi))

    ep_i = [0]

    def epilogue(po_idx, gcol, do, xi=0):
        xt = XT[xi]
        i = ep_i[0]
        ep_i[0] += 1
        sl = i % 2
        OP("act", "activation", [POK[po_idx], "modc"], ["ob"], ob[sl][:], PO[po_idx][:], AF.Identity, scale=gcol)
        for r in range(4):
            OP("pe", "transpose", ["ob", "ident_b"], ["trb"], TRB[:, sl, r, :], ob[sl][:, r * 128:(r + 1) * 128],
               ident_b[:])
        OP("dve", "tensor_tensor", xk(xi) + ["trb"], xk(xi), xt[:, :, do * 128:(do + 1) * 128],
           xt[:, :, do * 128:(do + 1) * 128], TRB[:, sl, :, :], ALU.add)

    def ffn_step1(k, hi=0, filler=None):
        hT = HT[hi]
        nxt = wload("w", Win_s[k][0], 4096)
        for c in range(11):
            wt, wk = nxt
            if c + 1 < 11:
                nxt = wload("w", Win_s[k][c + 1], 4096)
            wv = wt.rearrange("p (dt gu f) -> p dt gu f", dt=8, gu=2)
            for f2 in range(2):
                ft = 2 * c + f2
                b = ft % 2
                for gu in range(2):
                    for dt in range(8):
                        OP("pe", "matmul", [wk, ("hT", hi)], [PGK[2 * gu + b]], PG[2 * gu + b][:],
                           lhsT=wv[:, dt, gu, f2 * 128:(f2 + 1) * 128], rhs=hT[:, dt, :], start=(dt == 0), stop=(dt == 7))
                OP("act", "activation", [PGK[b]], [("sg", b)], sg[b][:], PG[b][:], AF.Silu)
                OP("dve", "tensor_tensor", [("sg", b), PGK[2 + b]], [("hh", ft)], hh[:, ft, :], sg[b][:], PG[2 + b][:],
                   ALU.mult)
                pump(filler)

    def ffn_step2(k, s, xi=0, filler=None):
        site = 0 if k == 0 else 2
        nxt = wload("o", Wout_s[k][0], NFT * 128)
        for do in range(8):
            wt, wk = nxt
            if do + 1 < 8:
                nxt = wload("o", Wout_s[k][do + 1], NFT * 128)
            wv = wt.rearrange("p (ft f) -> p ft f", ft=NFT)
            b = do % 2
            for ft in range(NFT):
                OP("pe", "matmul", [wk, ("hh", ft)], [POK[b]], PO[b][:], lhsT=wv[:, ft, :], rhs=hh[:, ft, :],
                   start=(ft == 0), stop=(ft == NFT - 1))
            epilogue(b, modc[:, 3 * site + 2, s, do:do + 1], do, xi)
            pump(filler)

    def ffn(k, s):
        norm_hT(0 if k == 0 else 2, s)
        ffn_step1(k)
        ffn_step2(k, s)

    def s5_u_gen(wt, wk, hi=0):
        hT = HT[hi]
        wv = wt.rearrange("p (dt f) -> p dt f", dt=8)
        for ft in range(4):
            b = ft % 2
            for dt in range(8):
                OP("pe", "matmul", [wk, ("hT", hi)], [POK[b]], PO[b][:], lhsT=wv[:, dt, ft * 128:(ft + 1) * 128],
                   rhs=hT[:, dt, :], start=(dt == 0), stop=(dt == 7))
            OP("act", "activation", [POK[b]], [("Ud", ft)], Ud[:, ft, :, :].rearrange("p j n -> p n j"),
               PO[b][:].rearrange("p (n j) -> p n j", j=TAU), AF.Copy)
            yield

    def s5_u(wt, wk):
        drain(s5_u_gen(wt, wk))

    def statein_gen(wt, wk, ES, esk):
        sv = wt.rearrange("p (q j r c) -> p q j r c", q=4, j=TAU, r=2)
        for ri in range(2):
            for qp in range(2):
                for q4 in range(4):
                    for q2 in range(2):
                        qq = 2 * qp + q2
                        rows = slice(32 * qq, 32 * qq + 32)
                        for j in range(TAU):
                            OP("pe", "matmul", [wk, ("Ud", q4)], [POK[q2]], PO[q2][:, q4 * NSUB:(q4 + 1) * NSUB],
                               lhsT=sv[rows, q4, j, ri, :], rhs=Ud[rows, q4, j, :], start=(j == 0), stop=(j == TAU - 1),
                               tile_position=(32 * qq, 0))
                    if q4 % 2 == 1:
                        yield
                for q2 in range(2):
                    qq = 2 * qp + q2
                    OP("act", "activation", [POK[q2]], [esk],
                       ES[:, :, ri, :].rearrange("p n (a b) -> p b a n", b=4)[:, qq],
                       PO[q2][:].rearrange("p (a n) -> p a n", a=4), AF.Copy)

    def scan2(ES, esk, d, reverse, SXo, sxk, eng="pool"):
        v5 = ES[:].rearrange("p (b k) r g -> p b k r g", b=8)
        ck = ("carry", d)

        def step(prev, cur, tab, t1, t2, nb, rk, wk_):
            ar2, nai, ai = tab[:, d, 0:2, :], tab[:, d, 2, :], tab[:, d, 3, :]
            if nb:
                ar2, nai, ai = bc(ar2, 1, nb), bc(nai, 1, nb), bc(ai, 1, nb)
                i0, i1 = (slice(None), slice(None), 0, slice(None)), (slice(None), slice(None), 1, slice(None))
            else:
                i0, i1 = (slice(None), 0, slice(None)), (slice(None), 1, slice(None))
            OP(eng, "tensor_tensor", rk + ["scanA", "scanB"], ["sct1"], t1, prev, ar2, ALU.mult)
            OP(eng, "tensor_tensor", rk + ["scanA", "scanB"], ["sct2"], t2[i0], prev[i1], nai, ALU.mult)
            OP(eng, "tensor_tensor", rk + ["scanA", "scanB"], ["sct2"], t2[i1], prev[i0], ai, ALU.mult)
            OP(eng, "tensor_tensor", ["sct1"] + rk, wk_, cur, cur, t1, ALU.add)
            OP(eng, "tensor_tensor", ["sct2"] + rk, wk_, cur, cur, t2, ALU.add)

        ks = range(14, -1, -1) if reverse else range(1, 16)
        for k in ks:
            kp = k + 1 if reverse else k - 1
            step(v5[:, :, kp], v5[:, :, k], scanA, sct1[:], sct2[:], 8, [esk], [esk])
        if not reverse:
            OP(eng, "tensor_copy", [ck], ["VE"], VE[:, 0], carry[d][:])
            for b in range(8):
                OP(eng, "tensor_copy", [esk, "VE"], ["VE"], VE[:, b + 1], v5[:, b, 15])
                step(VE[:, b], VE[:, b + 1], scanB, sct1[:, 0], sct2[:, 0], 0, ["VE"], ["VE"])
            vin = VE[:, 0:8]
            OP(eng, "tensor_copy", ["VE"], [ck], carry[d][:], VE[:, 8])
        else:
            OP(eng, "tensor_copy", [ck], ["VE"], VE[:, 8], carry[d][:])
            for b in range(7, -1, -1):
                OP(eng, "tensor_copy", [esk, "VE"], ["VE"], VE[:, b], v5[:, b, 0])
                step(VE[:, b + 1], VE[:, b], scanB, sct1[:, 0], sct2[:, 0], 0, ["VE"], ["VE"])
            vin = VE[:, 1:9]
            OP(eng, "tensor_copy", ["VE"], [ck], carry[d][:], VE[:, 0])
        for hb in range(4):
            bs = slice(2 * hb, 2 * hb + 2)
            sr = v5[:, bs, :, 0, :]
            si = v5[:, bs, :, 1, :]
            pr = bc(PW[:, d, 0], 1, 2)
            pi = bc(PW[:, d, 1], 1, 2)
            vr = bc(vin[:, bs, 0, :], 2, 16)
            vi = bc(vin[:, bs, 1, :], 2, 16)
            for (pa, va, tgt, op) in ((pr, vr, sr, ALU.add), (pi, vi, sr, ALU.subtract), (pr, vi, si, ALU.add),
                                      (pi, vr, si, ALU.add)):
                OP(eng, "tensor_tensor", ["PW", "VE"], ["sctb"], sctb[:], pa, va, ALU.mult)
                OP(eng, "tensor_tensor", ["sctb", esk], [esk], tgt, tgt, sctb[:], op)
        if not reverse:
            OP(eng, "tensor_copy", ["VE", sxk], [sxk], SXo[:, :, :, 0], VE[:, 0])
            OP(eng, "tensor_copy", [esk, sxk], [sxk], SXo[:, :, :, 1:NSUB].rearrange("p r g n -> p n r g"),
               ES[:, 0:NSUB - 1, :, :])
        else:
            OP(eng, "tensor_copy", ["VE", sxk], [sxk], SXo[:, :, :, NSUB - 1], VE[:, 8])
            OP(eng, "tensor_copy", [esk, sxk], [sxk], SXo[:, :, :, 0:NSUB - 1].rearrange("p r g n -> p n r g"),
               ES[:, 1:NSUB, :, :])

    def bwd_pass_gen(s):
        OP("pool", "memset", [("carry", 1)], [("carry", 1)], carry[1][:], 0.0)
        for m in range(NMT[s] - 1, -1, -1):
            DMA("pool", "ebld", [("ebd", s, m)], ["ESr"], ESr[:].rearrange("p n r g -> p (n r g)"), EB_s[s][m])
            scan2(ESr, "ESr", 1, True, Sxr, "Sxr")
            DMA("pool", "sbst", ["Sxr"], [("sbd", s, m)], SB_s[s][m], Sxr[:].rearrange("p r g n -> p (r g n)"))
            yield

    def bwd_pass(s):
        for _ in bwd_pass_gen(s):
            pass

    def load_x(src_ap, base, rk=(), xi=0):
        for r in range(4):
            DMA("act", "xld%d" % r, list(rk), [xk(xi, r)], XT[xi][:, r, :], src_ap[base + r * 128:base + (r + 1) * 128, :])

    def P_gen(s, m, xi):
        load_x(x_in[s], m * TT, (), xi)
        yield
        yield from norm_gen(0, s, xi, 0)

    def Q_gen(s, m, xi):
        base = m * TT
        for _ in range(2):
            if deferred:
                cast_dma(*deferred.pop(0))
        DMA("act", "x1st", xk(xi), [("x1d", s, m)], X1_s[s][base:base + TT, :].rearrange("(r p) d -> p r d", p=128),
            XT[xi][:])
        wt, wk = wload("q", Wmi_s[0], 4096)
        yield
        yield from norm_gen(1, s, xi, 1)
        wt2, wk2 = wload("q", SIN_s[0], 4096)
        yield from s5_u_gen(wt, wk, 1)
        if m == 0:
            OP("pool", "memset", [("carry", 0)], [("carry", 0)], carry[0][:], 0.0)
        yield from statein_gen(wt2, wk2, ESf, "ESf")
        wt3, wk3 = wload("q", SIN_s[1], 4096)
        scan2(ESf, "ESf", 0, False, Sx[0], ("Sx", 0))
        DMA("pool", "sfst", [("Sx", 0)], [("sfd", s, m)], SF_s[s][m], Sx[0][:].rearrange("p r g n -> p (r g n)"))
        yield
        yield from statein_gen(wt3, wk3, ESb, "ESb")
        DMA("act", "ebst", ["ESb"], [("ebd", s, m)], EB_s[s][m], ESb[:].rearrange("p n r g -> p (n r g)"))
        yield

    def PB_gen(s, m, xi):
        load_x(X1_s[s], m * TT, [("x1d", s, m)], xi)
        yield
        yield from norm_gen(1, s, xi, 0)

    def phase_b_mixer(s, m, xi):
        base = m * TT
        xt, hT = XT[xi], HT[0]
        DMA("act", "sfld", [("sfd", s, m)], [("Sx", 0)], Sx[0][:].rearrange("p r g n -> p (r g n)"), SF_s[s][m])
        DMA("act", "sbld", [("sbd", s, m)], [("Sx", 1)], Sx[1][:].rearrange("p r g n -> p (r g n)"), SB_s[s][m])
        wt, wk = wload("w", Wmi_s[0], 4096)
        s5_u(wt, wk)
        wu, wuk = wload("w", Wmi_s[1], 4096)
        wv_, wvk = wload("w", Wmi_s[2], 4096)
        wuv = wu.rearrange("p (dt f) -> p dt f", dt=8)
        wvv = wv_.rearrange("p (dt f) -> p dt f", dt=8)
        nxt_y = wload("o", YC_s[0], 3072)

        def g_tail(r):
            p = r % 2
            tok = slice(r * 128, (r + 1) * 128)
            for ct in range(4):
                OP("pe", "transpose", [("ygn", p), "ident_b"], ["tra"], TRA[:, ct, :], ygn[p][:, ct * 128:(ct + 1) * 128],
                   ident_b[:])
            OP("act", "activation", ["tra"], [("ycat", 1)], ycatT[:, 4:8, tok], TRA[:, 0:4, :], AF.Copy)

        ny = [nxt_y]
        def zpart(r):
            p = r % 2
            ft = r
            tok = slice(r * 128, (r + 1) * 128)
            for dt in range(8):
                OP("pe", "matmul", [wuk, ("hT", 0)], ["pg0"], PG[0][:], lhsT=hT[:, dt, tok], rhs=wuv[:, dt, :], start=(dt == 0),
                   stop=(dt == 7))
            for dt in range(8):
                OP("pe", "matmul", [wvk, ("hT", 0)], ["pg1"], PG[1][:], lhsT=hT[:, dt, tok], rhs=wvv[:, dt, :], start=(dt == 0),
                   stop=(dt == 7))
            OP("act", "activation", ["pg0"], [("ug", p)], ug[p][:], PG[0][:], AF.Gelu_apprx_tanh)
            OP("act", "activation", ["pg1"], ["vg"], vg[p][:], PG[1][:], AF.Gelu_apprx_tanh)
            OP("dve", "bn_stats", ["vg"], ["bnst"], bnst[:], vg[p][:])
            OP("dve", "bn_aggr", ["bnst"], ["bnmv"], bnmv[:], bnst[:])
            OP("act", "activation", ["bnmv"], ["bnrs"], st_rstd[:, 4:5], bnmv[:, 1:2], AF.Sqrt, bias=eps_t[:], scale=1.0)
            OP("dve", "reciprocal", ["bnrs"], ["bnrs"], st_rstd[:, 4:5], st_rstd[:, 4:5])
            OP("dve", "tensor_scalar", ["vg", "bnmv", "bnrs"], ["vg"], vg[p][:], vg[p][:], bnmv[:, 0:1],
               st_rstd[:, 4:5], ALU.subtract, ALU.mult)
            OP("dve", "tensor_tensor", ["vg", "lng_rep"], ["vg"], vg[p][:], vg[p][:], lng_rep[:], ALU.mult)
            OP("dve", "tensor_tensor", ["vg", "lnb_rep"], [("vn2", p)], vn2[p][:], vg[p][:], lnb_rep[:], ALU.add)

        def ypart(r):
            p = r % 2
            ft = r
            tok = slice(r * 128, (r + 1) * 128)
            yc, yck = ny[0]
            if ft + 1 < 4:
                ny[0] = wload("o", YC_s[ft + 1], 3072)
            bdv = yc[:, 0:1024].rearrange("p (d l c) -> p d l c", d=2, l=TAU)
            sov = yc[:, 1024:3072].rearrange("p (g d i r c) -> p g d i r c", g=4, d=2, i=TAU, r=2)
            b = 2 + ft % 2
            for i in range(TAU):
                reg = PG[b][:, i * NSUB:(i + 1) * NSUB]
                first = True
                for j in range(TAU):
                    if j <= i:
                        OP("pe", "matmul", [yck, ("Ud", ft)], [PGK[b]], reg, lhsT=bdv[:, 0, i - j, :], rhs=Ud[:, ft, j, :],
                           start=first, stop=False)
                        first = False
                    if j >= i:
                        OP("pe", "matmul", [yck, ("Ud", ft)], [PGK[b]], reg, lhsT=bdv[:, 1, j - i, :], rhs=Ud[:, ft, j, :],
                           start=first, stop=False)
                        first = False
                for qq in range(4):
                    gp = 4 * ft + qq
                    for d in range(2):
                        for ri in range(2):
                            last = (qq == 3 and d == 1 and ri == 1)
                            OP("pe", "matmul", [yck, ("Sx", d)], [PGK[b]],
                               PG[b][32 * qq:32 * qq + 32, i * NSUB:(i + 1) * NSUB],
                               lhsT=sov[:, qq, d, i, ri, :], rhs=Sx[d][:, ri, gp, :], start=False, stop=last,
                               tile_position=(0, 32 * qq))
            OP("act", "activation", [PGK[b]], [("y1f", ft)], y1f[:, ft, :].rearrange("p (n i) -> p n i", i=TAU),
               PG[b][:].rearrange("p (i n) -> p n i", i=TAU), AF.Gelu_apprx_tanh)
            OP("act", "activation", [("y1f", ft)], [("y1b", ft)], y1b[:, ft, :], y1f[:, ft, :], AF.Copy)

        def sppart(r):
            p = r % 2
            ft = r
            tok = slice(r * 128, (r + 1) * 128)
            for h in range(4):
                OP("pe", "matmul", ["wspT", ("vn2", p)], [POK[p]], PO[p][:, h * 128:(h + 1) * 128], lhsT=wspT[:, h, :],
                   rhs=vn2[p][:, h * 128:(h + 1) * 128], start=True, stop=True)
            for h in range(4):
                OP("dve", "scalar_tensor_tensor", [POK[p], "bsp_c", ("ug", p)], ["ygm"], ygm[p][:, h * 128:(h + 1) * 128],
                   PO[p][:, h * 128:(h + 1) * 128], bsp_c[:, h:h + 1], ug[p][:, h * 128:(h + 1) * 128], ALU.add, ALU.mult)
            OP("act", "activation", ["ygm"], [("ygn", p), "gssq"], ygn[p][:], ygm[p][:], AF.Square,
               accum_out=st_ssq[:, 5:6])
            OP("act", "activation", ["gssq"], ["grs"], st_rstd[:, 5:6], st_ssq[:, 5:6], AF.Sqrt, bias=eps_t[:], scale=1.0 / 512)
            OP("dve", "reciprocal", ["grs"], ["grs"], st_rstd[:, 5:6], st_rstd[:, 5:6])
            OP("act", "activation", ["ygm", "grs"], [("ygn", p)], ygn[p][:], ygm[p][:], AF.Identity,
               scale=st_rstd[:, 5:6])

        zpart(0)
        ypart(0)
        zpart(1)
        sppart(0)
        ypart(1)
        zpart(2)
        sppart(1)
        g_tail(0)
        ypart(2)
        zpart(3)
        sppart(2)
        g_tail(1)
        ypart(3)
        sppart(3)
        g_tail(2)
        g_tail(3)
        wg, wgk = wload("w", Wgl_s, 2048)
        wgv = wg.rearrange("p (kt f) -> p kt f", kt=4)
        y1bk = [("y1b", ft) for ft in range(4)]
        for fo in range(4):
            b = fo % 2
            for kt in range(4):
                OP("pe", "matmul", [wgk] + y1bk, [POK[b]], PO[b][:], lhsT=wgv[:, kt, fo * 128:(fo + 1) * 128], rhs=y1b[:, kt, :],
                   start=(kt == 0), stop=(kt == 3))
            OP("act", "activation", [POK[b]], [("sg", b)], sg[b][:], PO[b][:], AF.Sigmoid)
            OP("dve", "tensor_tensor", [("y1f", fo), ("sg", b)], [("y1f", fo)], y1f[:, fo, :], y1f[:, fo, :], sg[b][:], ALU.mult)
        for fo in range(4):
            OP("act", "activation", [("y1f", fo)], [("y1b", fo)], sqb[:, fo, :], y1f[:, fo, :], AF.Square)
        for fo in range(4):
            OP("pe", "matmul", ["ones_b", ("y1b", fo)], ["po0"], PO[0][:], lhsT=ones_b[:], rhs=sqb[:, fo, :], start=(fo == 0),
               stop=(fo == 3))
        OP("act", "activation", ["po0"], ["ntmp"], rs5[:], PO[0][:], AF.Sqrt, bias=eps_t[:], scale=1.0 / 512)
        OP("dve", "reciprocal", ["ntmp"], ["ntmp"], rs5[:], rs5[:])
        OP("dve", "tensor_tensor", [("y1f", fo) for fo in range(4)] + ["ntmp"], [("ycat", 0)], ycatT[:, 0:4, :], y1f[:],
           bc(rs5[:], 1, 4), ALU.mult)
        nxt = wload("o", Wmo_s[0], 1024)
        for do in range(8):
            wt, wk = nxt
            if do + 1 < 8:
                nxt = wload("o", Wmo_s[do + 1], 1024)
            wv = wt.rearrange("p (kt f) -> p kt f", kt=8)
            b = do % 2
            for kt in range(8):
                OP("pe", "matmul", [wk, ("ycat", kt // 4)], [POK[b]], PO[b][:], lhsT=wv[:, kt, :], rhs=ycatT[:, kt, :],
                   start=(kt == 0), stop=(kt == 7))
            epilogue(b, modc[:, 5, s, do:do + 1], do, xi)

    def final_norm(s, m, xi):
        base = m * TT
        xt = XT[xi]
        for r in range(4):
            sl = r % 2
            OP("act", "activation", [xk(xi, r)], [("xn", sl), ("ssq", r)], xn[sl][:], xt[:, r, :], AF.Square,
               accum_out=st_ssq[:, r:r + 1])
            OP("act", "activation", [("ssq", r)], [("rstd", r)], st_rstd[:, r:r + 1], st_ssq[:, r:r + 1], AF.Sqrt,
               bias=eps_t[:], scale=1.0 / D)
            OP("dve", "reciprocal", [("rstd", r)], [("rstd", r)], st_rstd[:, r:r + 1], st_rstd[:, r:r + 1])
            OP("dve", "scalar_tensor_tensor", [xk(xi, r), ("rstd", r), "fg_rep"], ["ntmp"],
               ntmp[:].rearrange("p a b -> p (a b)"), xt[:, r, :], st_rstd[:, r:r + 1], fg_rep[:], ALU.mult, ALU.mult)
            DMA("act", "yo", ["ntmp"], [], y_out[s][base + r * 128:base + (r + 1) * 128, :],
                ntmp[:].rearrange("p a b -> p (a b)"))

    mk_ph = M.mark()
    ESf = M.sb("ESf", [128, NSUB, 2, 16], F32)
    ESb = M.sb("ESb", [128, NSUB, 2, 16], F32)
    XT[1] = M.sb("xt1", [128, 4, D], F32)
    HT[1] = M.sb("hT1", [128, 8, TT], BF16)
    rings["q"] = [M.sb("qbuf%d" % i, [128, 4096], BF16) for i in range(2)]
    tiles = [(s_, m_) for s_ in range(2) for m_ in range(NMT[s_])]
    bgen = None
    drain(P_gen(tiles[0][0], tiles[0][1], 0))
    qprev = None
    for i, (s_, m_) in enumerate(tiles):
        xi = i % 2
        ffn_step1(0, 0, qprev)
        drain(qprev)
        pnext = P_gen(tiles[i + 1][0], tiles[i + 1][1], 1 - xi) if i + 1 < len(tiles) else None
        ffn_step2(0, s_, xi, pnext)
        drain(pnext)
        qprev = Q_gen(s_, m_, xi)
    drain(qprev)
    while deferred:
        cast_dma(*deferred.pop(0))
    S.barrier()
    M.release(mk_ph)
    rings["w"].append(M.sb("wbuf2", [128, 4096], BF16))
    lng_rep = M.sb("lng_rep", [128, 512], F32)
    lnb_rep = M.sb("lnb_rep", [128, 512], F32)
    fg_rep = M.sb("fg_rep", [128, D], F32)
    DMA("act", "pl0", [], ["lng_rep"], lng_rep[:], W["gmlp_ln_g"].partition_broadcast(128))
    DMA("act", "pl1", [], ["lnb_rep"], lnb_rep[:], W["gmlp_ln_b"].partition_broadcast(128))
    DMA("act", "pl2", [], ["fg_rep"], fg_rep[:], W["final_norm_g"].partition_broadcast(128))
    Sx[1] = M.sb("Sx1", [128, 2, 16, NSUB], BF16)
    y1f = M.sb("y1f", [128, 4, TT], F32)
    y1b = M.sb("y1b", [128, 4, TT], BF16)
    rs5 = ntmp[:].rearrange("p a b -> p (a b)")[:, 0:TT]
    XT[1] = M.sb("xt1b", [128, 4, D], F32)
    ycatT = M.sb("ycatT", [128, 8, TT], BF16)
    ug = [M.sb("ug%d" % i, [128, 512], F32) for i in range(2)]
    vg = [M.sb("vg0", [128, 512], F32)] * 2
    vn2 = [M.sb("vn2_%d" % i, [128, 512], BF16) for i in range(2)]
    ygm = [M.sb("ygm0", [128, 512], F32)] * 2
    ygn = [M.sb("ygn%d" % i, [128, 512], BF16) for i in range(2)]
    bnst = M.sb("bnst", [128, 6], F32)
    bnmv = M.sb("bnmv", [128, 2], F32)
    sqb = y1b
    bwd_pass(1)
    bwd_pass(0)
    tilesB = [(s_, m_) for s_ in (1, 0) for m_ in range(NMT[s_] - 1, -1, -1)]
    drain(PB_gen(tilesB[0][0], tilesB[0][1], 0))
    for i, (s_, m_) in enumerate(tilesB):
        xi = i % 2
        phase_b_mixer(s_, m_, xi)
        norm_hT(2, s_, xi, 0)
        ffn_step1(1, 0)
        pnext = PB_gen(tilesB[i + 1][0], tilesB[i + 1][1], 1 - xi) if i + 1 < len(tilesB) else None
        ffn_step2(1, s_, xi, pnext)
        drain(pnext)
        final_norm(s_, m_, xi)

    S.barrier()
    S.emit()
    M.release(0)
    S.close()
    return nc


_NC_CACHE = {}


def kernel(**inputs):
    x_prompt = np.asarray(inputs["x_prompt"], dtype=np.float32)
    x_sample = np.asarray(inputs["x_sample"], dtype=np.float32)
    c_prompt = np.asarray(inputs["c_prompt"], dtype=np.float32)
    c_sample = np.asarray(inputs["c_sample"], dtype=np.float32)
    B, LP, _ = x_prompt.shape
    _, LS, _ = x_sample.shape
    n = 8
    key = (LP, LS)
    if key not in _NC_CACHE:
        _NC_CACHE[key] = build(LP, LS)
    nc = _NC_CACHE[key]
    wmap = {}
    for nm in WNAMES:
        a = np.asarray(inputs[nm], dtype=np.float32)
        if nm != "final_norm_g":
            a = a[0]
        wmap[nm] = np.ascontiguousarray(a)
    in_maps = []
    for i in range(n):
        mp = dict(wmap)
        mp["x_p"] = np.ascontiguousarray(x_prompt[i])
        mp["x_s"] = np.ascontiguousarray(x_sample[i])
        mp["c"] = np.ascontiguousarray(np.stack([c_prompt[i], c_sample[i]], axis=0))
        in_maps.append(mp)
    res = run_bass_kernel_spmd(nc, in_maps, core_ids=list(range(n)))
    yp = np.stack([np.asarray(res.results[i]["y_p"], dtype=np.float32) for i in range(n)], axis=0)
    ys = np.stack([np.asarray(res.results[i]["y_s"], dtype=np.float32) for i in range(n)], axis=0)
    return (yp, ys)
```

```python
import numpy as np
import concourse.bass as bass
import concourse.mybir as mybir
from concourse.bass_utils import run_bass_kernel_spmd

F32 = mybir.dt.float32
BF16 = mybir.dt.bfloat16
I32 = mybir.dt.int32
AF = mybir.ActivationFunctionType
ALU = mybir.AluOpType

ENGS = ("pe", "act", "dve", "pool", "sp")
D = 1024
DFF = 2816
NFT = 22
TAU = 4
TT = 512
NSUB = TT // TAU
EPS = 1e-6
PI = 3.14159265358979


class Sched:
    def __init__(self, nc):
        self.nc = nc
        self.q = {e: [] for e in ENGS}
        self.cnt = {e: 0 for e in ENGS}
        self.esem = {}
        self.last_w = {}
        self.readers = {}
        self.seen = {e: {} for e in ENGS}
        self.dsem = {}
        self._ctx = []

    def open(self):
        for e in ENGS:
            cm = self.nc.semaphore("es_" + e)
            self.esem[e] = cm.__enter__()
            self._ctx.append(cm)

    def dma_sem(self, name):
        if name not in self.dsem:
            cm = self.nc.semaphore("ds_" + name)
            h = cm.__enter__()
            self._ctx.append(cm)
            self.dsem[name] = [h, 0]
        return name

    def close(self):
        for cm in reversed(self._ctx):
            cm.__exit__(None, None, None)

    def _need(self, eng, ev, waits):
        if ev is None:
            return
        sk, val = ev
        if sk == ("e", "pe") and eng == "pe":
            return
        if self.seen[eng].get(sk, 0) >= val:
            return
        self.seen[eng][sk] = val
        waits[sk] = max(waits.get(sk, 0), val)

    def _deps(self, eng, reads, writes):
        waits = {}
        for k in reads:
            self._need(eng, self.last_w.get(k), waits)
        for k in writes:
            self._need(eng, self.last_w.get(k), waits)
            for sk, val in self.readers.get(k, {}).items():
                self._need(eng, (sk, val), waits)
        return waits

    def _commit(self, ev, reads, writes):
        for k in reads:
            d = self.readers.setdefault(k, {})
            d[ev[0]] = max(d.get(ev[0], 0), ev[1])
        for k in writes:
            self.last_w[k] = ev
            self.readers[k] = {}

    def op(self, eng, fn, reads=(), writes=()):
        waits = self._deps(eng, reads, writes)
        self.cnt[eng] += 1
        ev = (("e", eng), self.cnt[eng])
        self.q[eng].append((waits, fn, None))
        self._commit(ev, reads, writes)
        return ev

    def dma(self, eng, fn, sem, reads=(), writes=()):
        self.dma_sem(sem)
        waits = self._deps(eng, reads, writes)
        if self.dsem[sem][1]:
            self._need(eng, (("d", sem), self.dsem[sem][1]), waits)
        self.dsem[sem][1] += 16
        ev = (("d", sem), self.dsem[sem][1])
        self.q[eng].append((waits, fn, sem))
        self._commit(ev, reads, writes)
        return ev

    def wait_all(self, eng):
        waits = {}
        for e in ENGS:
            if self.cnt[e] and e != eng:
                self._need(eng, (("e", e), self.cnt[e]), waits)
        for name, (h, c) in self.dsem.items():
            if c:
                self._need(eng, (("d", name), c), waits)
        if waits:
            self.q[eng].append((waits, None, None))

    def barrier(self):
        for e in ENGS:
            self.wait_all(e)

    def _semh(self, sk):
        return self.esem[sk[1]] if sk[0] == "e" else self.dsem[sk[1]][0]

    def emit(self):
        nc = self.nc
        with nc.Block() as block:
            def mk(ename):
                def body(eng):
                    for waits, fn, dsem in self.q[ename]:
                        for sk, val in waits.items():
                            eng.wait_ge(self._semh(sk), val)
                        if fn is None:
                            continue
                        ins = fn(eng)
                        if dsem is not None:
                            ins.then_inc(self.dsem[dsem][0], 16)
                        else:
                            ins.then_inc(self.esem[ename], 1)
                return body
            block.tensor(mk("pe"))
            block.scalar(mk("act"))
            block.vector(mk("dve"))
            block.gpsimd(mk("pool"))
            block.sync(mk("sp"))


class Pool_:
    def __init__(self, nc):
        self.nc = nc
        self.stack = []

    def sb(self, name, shape, dt):
        cm = self.nc.sbuf_tensor(name, list(shape), dt)
        t = cm.__enter__()
        self.stack.append(cm)
        return t

    def ps(self, name, shape, dt):
        cm = self.nc.psum_tensor(name, list(shape), dt)
        t = cm.__enter__()
        self.stack.append(cm)
        return t

    def mark(self):
        return len(self.stack)

    def release(self, mark):
        while len(self.stack) > mark:
            self.stack.pop().__exit__(None, None, None)


WNAMES = ["w_ada", "b_ada", "norm_ffn1_g", "ffn1_w_in", "ffn1_w_out", "norm_mix_g", "w_mix_in",
          "s5_lam_re_f", "s5_lam_im_f", "s5_log_step_f", "s5_b_re_f", "s5_b_im_f", "s5_c_re_f", "s5_c_im_f",
          "s5_lam_re_b", "s5_lam_im_b", "s5_log_step_b", "s5_b_re_b", "s5_b_im_b", "s5_c_re_b", "s5_c_im_b",
          "s5_d", "s5_w_glu", "gmlp_ln_g", "gmlp_ln_b", "gmlp_w_sp", "gmlp_b_sp",
          "norm_out_s5_g", "norm_out_gmlp_g", "w_mix_out", "norm_ffn2_g", "ffn2_w_in", "ffn2_w_out",
          "final_norm_g"]
WSHAPES = {
    "w_ada": [D, 9 * D], "b_ada": [9 * D], "norm_ffn1_g": [D], "ffn1_w_in": [D, 2 * DFF],
    "ffn1_w_out": [DFF, D], "norm_mix_g": [D], "w_mix_in": [D, 1536],
    "s5_lam_re_f": [32, 64], "s5_lam_im_f": [32, 64], "s5_log_step_f": [32],
    "s5_b_re_f": [32, 64, 16], "s5_b_im_f": [32, 64, 16], "s5_c_re_f": [32, 16, 64], "s5_c_im_f": [32, 16, 64],
    "s5_lam_re_b": [32, 64], "s5_lam_im_b": [32, 64], "s5_log_step_b": [32],
    "s5_b_re_b": [32, 64, 16], "s5_b_im_b": [32, 64, 16], "s5_c_re_b": [32, 16, 64], "s5_c_im_b": [32, 16, 64],
    "s5_d": [512], "s5_w_glu": [512, 512], "gmlp_ln_g": [512], "gmlp_ln_b": [512],
    "gmlp_w_sp": [4, 128, 128], "gmlp_b_sp": [4, 128],
    "norm_out_s5_g": [512], "norm_out_gmlp_g": [512], "w_mix_out": [D, D], "norm_ffn2_g": [D],
    "ffn2_w_in": [D, 2 * DFF], "ffn2_w_out": [DFF, D], "final_norm_g": [D],
}


def build(LP, LS, dbg=None):
    nc = bass.Bass("TRN2", target_bir_lowering=False)
    S = Sched(nc)
    S.open()
    M = Pool_(nc)
    LEN = [LP, LS]
    NMT = [LP // TT, LS // TT]

    def din(name, shape, dt=F32):
        return nc.dram_tensor(name, list(shape), dt, kind="ExternalInput").ap()

    x_in = [din("x_p", [LP, D]), din("x_s", [LS, D])]
    c_in = din("c", [2, D])
    W = {n: din(n, WSHAPES[n]) for n in WNAMES}
    y_out = [nc.dram_tensor("y_p", [LP, D], F32, kind="ExternalOutput").ap(),
             nc.dram_tensor("y_s", [LS, D], F32, kind="ExternalOutput").ap()]
    dbg_out = {}
    if dbg:
        for k, shp in dbg.items():
            dbg_out[k] = nc.dram_tensor("dbg_" + k, list(shp), F32, kind="ExternalOutput").ap()

    def scr(name, shape, dt):
        return nc.dram_tensor(name, list(shape), dt, kind="Internal").ap()

    Win_s = [scr("win_s%d" % k, [11, 128, 4096], BF16) for k in range(2)]
    Wout_s = [scr("wout_s%d" % k, [8, 128, NFT * 128], BF16) for k in range(2)]
    Wmi_s = scr("wmi_s", [3, 128, 4096], BF16)
    Wmo_s = scr("wmo_s", [8, 128, 1024], BF16)
    Wgl_s = scr("wgl_s", [128, 2048], BF16)
    SIN_s = scr("sin_s", [2, 128, 4096], BF16)
    SOUT_s = scr("sout_s", [2, 128, 4096], BF16)
    BD_s = scr("bd_s", [128, 4096], BF16)
    YC_s = scr("yc_s", [4, 128, 3072], BF16)
    X1_s = [scr("x1_s%d" % s, [LEN[s], D], F32) for s in range(2)]
    SF_s = [scr("sf_s%d" % s, [NMT[s], 128, 4096], BF16) for s in range(2)]
    SB_s = [scr("sb_s%d" % s, [NMT[s], 128, 4096], BF16) for s in range(2)]
    EB_s = [scr("eb_s%d" % s, [NMT[s], 128, 4096], F32) for s in range(2)]

    def OP(eng, meth, reads, writes, *a, **kw):
        return S.op(eng, lambda e: getattr(e, meth)(*a, **kw), reads, writes)

    def DMA(eng, sem, reads, writes, out, in_, **kw):
        return S.dma(eng, lambda e: e.dma_start(out=out, in_=in_, **kw), sem, reads, writes)

    def bc(ap, axis, n):
        shp = list(ap.shape)
        shp.insert(axis, n)
        return ap.unsqueeze(axis).broadcast_to(shp)

    ident_f = M.sb("ident_f", [128, 128], F32)
    ident_b = M.sb("ident_b", [128, 128], BF16)
    ones_b = M.sb("ones_b", [128, 128], BF16)
    eps_t = M.sb("eps_t", [128, 1], F32)
    modc = M.sb("modc", [128, 9, 2, 8], F32)
    wspT = M.sb("wspT", [128, 4, 128], BF16)
    bsp_c = M.sb("bsp_c", [128, 4], F32)
    scanA = M.sb("scanA", [128, 2, 4, 16], F32)
    scanB = M.sb("scanB", [128, 2, 4, 16], F32)
    PW = M.sb("PW", [128, 2, 2, 16, 16], F32)

    PG = [M.ps("pg%d" % i, [128, 512], F32) for i in range(4)]
    PO = [M.ps("po%d" % i, [128, 512], F32) for i in range(2)]
    TRA = M.ps("tra", [128, 8, 128], BF16)
    TRB = M.ps("trb", [128, 2, 4, 128], BF16)
    PGK = ["pg0", "pg1", "pg2", "pg3"]
    POK = ["po0", "po1"]

    iot = M.sb("iot", [128, 128], I32)
    OP("pool", "iota", [], ["iot"], iot[:], [[1, 128]], base=0, channel_multiplier=-1)
    OP("dve", "tensor_scalar", ["iot"], ["ident_f"], ident_f[:], iot[:], 0.0, None, ALU.is_equal)
    OP("dve", "tensor_copy", ["ident_f"], ["ident_b"], ident_b[:], ident_f[:])
    OP("dve", "memset", [], ["ones_b"], ones_b[:], 1.0)
    OP("dve", "memset", [], ["eps_t"], eps_t[:], EPS)

    mk_pro = M.mark()
    cT = M.sb("cT", [128, 8, 2], F32)
    bcol = M.sb("bcol", [128, 72], F32)
    ngc = M.sb("ngc", [128, 3, 8], F32)
    stg = [M.sb("stg%d" % i, [128, 4096], F32) for i in range(2)]
    stb = [M.sb("stb%d" % i, [128, 4096], BF16) for i in range(2)]
    for s_ in range(2):
        DMA("act", "pl%d" % (6 + s_), [], [("cTl", s_)], cT[:, :, s_], c_in[s_].rearrange("(dt p) -> p dt", p=128),
            allow_slow_non_contiguous=True)
    DMA("act", "pl1", [], ["bcol"], bcol[:], W["b_ada"].rearrange("(ft p) -> p ft", p=128), allow_slow_non_contiguous=True)
    for j, nm in enumerate(["norm_ffn1_g", "norm_mix_g", "norm_ffn2_g"]):
        DMA("act", "pl%d" % (2 + j), [], [("ngc", j)], ngc[:, j, :], W[nm].rearrange("(dt p) -> p dt", p=128),
            allow_slow_non_contiguous=True)
    OP("act", "activation", [("cTl", 0), ("cTl", 1)], ["cT"], cT[:], cT[:], AF.Silu)
    modps = PG[0]
    wada_v = W["w_ada"].rearrange("(dt p) f -> p dt f", p=128)
    for ch in range(18):
        sl = ch % 2
        DMA("sp", "cst%d" % sl, [], [("stg", sl)], stg[sl][:].rearrange("p (dt f) -> p dt f", dt=8),
            wada_v[:, :, ch * 512:(ch + 1) * 512])
        sv = stg[sl][:].rearrange("p (dt f) -> p dt f", dt=8)
        for f4 in range(4):
            ft = ch * 4 + f4
            for dt in range(8):
                OP("pe", "matmul", [("stg", sl), "cT"], ["pg0"], modps[:, ft * 2:ft * 2 + 2],
                   lhsT=sv[:, dt, f4 * 128:(f4 + 1) * 128], rhs=cT[:, dt, :], start=(dt == 0), stop=(dt == 7))
    OP("dve", "tensor_tensor", ["pg0", "bcol"], ["modc"], modc[:].rearrange("p k s d -> p k d s"),
       modps[:, 0:144].rearrange("p (k d s) -> p k d s", k=9, d=8),
       bc(bcol[:].rearrange("p (k d) -> p k d", k=9), 3, 2), ALU.add)
    for j in range(3):
        OP("dve", "scalar_tensor_tensor", ["modc", ("ngc", j)], ["modc"], modc[:, 3 * j + 1, :, :],
           modc[:, 3 * j + 1, :, :], 1.0, bc(ngc[:, j, :], 1, 2), ALU.add, ALU.mult)
    for k in (2, 8):
        OP("dve", "tensor_scalar", ["modc"], ["modc"], modc[:, k, :, :], modc[:, k, :, :], 0.5, None, ALU.mult)

    DMA("act", "pl3", [], ["bsp_c"], bsp_c[:], W["gmlp_b_sp"].rearrange("h q -> q h"), allow_slow_non_contiguous=True)
    wsp_n = M.sb("wsp_n", [128, 4, 128], F32)
    DMA("act", "pl4", [], ["wsp_n"], wsp_n[:], W["gmlp_w_sp"].rearrange("h q k -> q h k"))
    for h in range(4):
        OP("pe", "transpose", ["wsp_n", "ident_f"], ["po0"], PO[0][:, h * 128:(h + 1) * 128], wsp_n[:, h, :], ident_f[:])
    OP("act", "activation", ["po0"], ["wspT"], wspT[:].rearrange("p h q -> p (h q)"), PO[0][:], AF.Copy)
    gcat = M.sb("gcat", [128, 8], F32)
    DMA("act", "pl5", [], [("gcat", 0)], gcat[:, 0:4], W["norm_out_s5_g"].rearrange("(k p) -> p k", p=128),
        allow_slow_non_contiguous=True)
    DMA("act", "pl6", [], [("gcat", 1)], gcat[:, 4:8], W["norm_out_gmlp_g"].rearrange("(k p) -> p k", p=128),
        allow_slow_non_contiguous=True)
    dcol = M.sb("dcol", [128, 4], F32)
    DMA("act", "pl7", [], ["dcol"], dcol[:], W["s5_d"].rearrange("(k p) -> p k", p=128), allow_slow_non_contiguous=True)

    cvt_i = [0]

    def convert(src, dst, nfree, scale_bc=None):
        i = cvt_i[0]
        cvt_i[0] += 1
        sl = i % 2
        sshape = list(src.shape)
        sv = stg[sl][:, 0:nfree]
        if len(sshape) == 3:
            sv = sv.rearrange("p (a b) -> p a b", a=sshape[1])
        elif len(sshape) == 4:
            sv = sv.rearrange("p (a b c) -> p a b c", a=sshape[1], b=sshape[2])
        if len(sshape) == 4:
            for a_ in range(sshape[2]):
                DMA("sp", "cst%d_%d" % (sl, a_), [], [("stg", sl)] if a_ == 0 else [("stgx", sl, a_)], sv[:, :, a_, :], src[:, :, a_, :])
        else:
            DMA("sp", "cst%d" % sl, [], [("stg", sl)], sv, src)
        if scale_bc is not None:
            OP("dve", "tensor_tensor", [("stg", sl), ("gcat", 0), ("gcat", 1)], [("stb", sl)],
               stb[sl][:, 0:nfree].rearrange("p (a b) -> p a b", a=sshape[1]), sv, scale_bc, ALU.mult)
        elif i % 3 == 0:
            OP("act", "activation", [("stg", sl), ("stgx", sl, 1)], [("stb", sl), ("stgx", sl, 1)], stb[sl][:, 0:nfree], stg[sl][:, 0:nfree], AF.Copy)
        elif i % 3 == 1:
            OP("dve", "tensor_copy", [("stg", sl), ("stgx", sl, 1)], [("stb", sl), ("stgx", sl, 1)], stb[sl][:, 0:nfree], stg[sl][:, 0:nfree])
        else:
            OP("pool", "tensor_copy", [("stg", sl), ("stgx", sl, 1)], [("stb", sl), ("stgx", sl, 1)], stb[sl][:, 0:nfree], stg[sl][:, 0:nfree])
        DMA("act", "cso%d" % sl, [("stb", sl)], [], dst, stb[sl][:, 0:nfree])

    cvd_i = [0]
    deferred = []

    def cast_dma(dst, src):
        i = cvd_i[0]
        cvd_i[0] += 1
        DMA("pool", "cv%d" % (i % 3), [], [], dst, src)

    for k, (wi, wo) in enumerate([("ffn1_w_in", "ffn1_w_out"), ("ffn2_w_in", "ffn2_w_out")]):
        wiv = W[wi].rearrange("(dt p) (gu f) -> p dt gu f", p=128, gu=2)
        wov = W[wo].rearrange("(ft p) d -> p ft d", p=128)
        jobs = []
        for c in range(11):
            for gu in range(2):
                jobs.append((Win_s[k][c].rearrange("p (dt gu f) -> p dt gu f", dt=8, gu=2)[:, :, gu, :],
                             wiv[:, :, gu, c * 256:(c + 1) * 256]))
        for do in range(8):
            jobs.append((Wout_s[k][do].rearrange("p (ft f) -> p ft f", ft=NFT), wov[:, :, do * 128:(do + 1) * 128]))
        if k == 0:
            for dst, src in jobs:
                cast_dma(dst, src)
        else:
            deferred.extend(jobs)
    wmv = W["w_mix_in"].rearrange("(dt p) f -> p dt f", p=128)
    for c3 in range(3):
        cast_dma(Wmi_s[c3].rearrange("p (dt f) -> p dt f", dt=8), wmv[:, :, c3 * 512:(c3 + 1) * 512])
    cast_dma(Wgl_s.rearrange("p (kt f) -> p kt f", kt=4), W["s5_w_glu"].rearrange("(kt p) f -> p kt f", p=128))
    wmo = W["w_mix_out"].rearrange("(kt p) d -> p kt d", p=128)
    for do in range(8):
        convert(wmo[:, :, do * 128:(do + 1) * 128], Wmo_s[do], 1024, scale_bc=bc(gcat[:], 2, 128))

    def s5t(name, shape=(128, 16), dt=F32):
        return M.sb(name, list(shape), dt)

    tmpA = s5t("tmpA"); tmpB = s5t("tmpB"); tmpC = s5t("tmpC")
    tmpI = s5t("tmpI", (128, 16), I32)
    bdst = M.sb("bdst", [128, 4, 2, 4, 128], F32)
    sinst = M.sb("sinst", [128, 4, 4, 2, 128], BF16)
    soutst = M.sb("soutst", [128, 16, 2, 4, 2, 32], BF16)
    bmask = M.sb("bmask", [128, 4, 32], F32)
    OP("dve", "memset", [], ["bmask"], bmask[:], 0.0)
    for q in range(4):
        OP("dve", "memset", ["bmask"], ["bmask"], bmask[32 * q:32 * q + 32, q, :], 1.0)
    ZB = [[M.sb("zb%d_%d" % (q, ri), [128, 128], F32) for ri in range(2)] for q in range(4)]
    for q in range(4):
        for ri in range(2):
            OP("dve", "memset", [], [("zb", q, ri)], ZB[q][ri][:], 0.0)
    pli = [0]

    def plsem():
        pli[0] += 1
        return "pl%d" % (pli[0] % 8)

    def sin_of(out, th, key_out, key_th):
        OP("dve", "tensor_scalar", [key_th], ["tmpA"], tmpA[:], th[:], 1.0 / (2 * PI), None, ALU.mult)
        OP("dve", "tensor_copy", ["tmpA"], ["tmpI"], tmpI[:], tmpA[:])
        OP("dve", "tensor_copy", ["tmpI"], ["tmpA"], tmpA[:], tmpI[:])
        OP("dve", "scalar_tensor_tensor", ["tmpA", key_th], ["tmpB"], tmpB[:], tmpA[:], -2 * PI, th[:], ALU.mult, ALU.add)
        OP("dve", "tensor_scalar", ["tmpB"], ["tmpA"], tmpA[:], tmpB[:], PI, None, ALU.is_gt)
        OP("dve", "scalar_tensor_tensor", ["tmpA", "tmpB"], ["tmpC"], tmpC[:], tmpA[:], -2 * PI, tmpB[:], ALU.mult, ALU.add)
        OP("dve", "tensor_scalar", ["tmpC"], ["tmpA"], tmpA[:], tmpC[:], -PI, None, ALU.is_lt)
        OP("dve", "scalar_tensor_tensor", ["tmpA", "tmpC"], ["tmpB"], tmpB[:], tmpA[:], 2 * PI, tmpC[:], ALU.mult, ALU.add)
        OP("act", "activation", ["tmpB"], [key_out], out[:], tmpB[:], AF.Sin)

    def TT_(eng, out, a, b, op, r, w):
        OP(eng, "tensor_tensor", r, w, out, a, b, op)

    for d, sfx in enumerate(["f", "b"]):
        pf = "d%d_" % d
        mk_d = M.mark()
        lre = s5t(pf + "lre"); lim = s5t(pf + "lim"); lsb = s5t(pf + "lsb")
        DMA("act", plsem(), [], [pf + "lre"], lre[:], W["s5_lam_re_" + sfx].rearrange("(gp two) p -> (two p) gp", two=2),
            allow_slow_non_contiguous=True)
        DMA("act", plsem(), [], [pf + "lim"], lim[:], W["s5_lam_im_" + sfx].rearrange("(gp two) p -> (two p) gp", two=2),
            allow_slow_non_contiguous=True)
        lsv = W["s5_log_step_" + sfx].rearrange("(gp two) -> two gp", two=2)
        for two in range(2):
            DMA("act", plsem(), [], [(pf + "lsb", two)], lsb[two * 64:(two + 1) * 64, :], lsv[two].partition_broadcast(64),
                allow_slow_non_contiguous=True)
        Bre = s5t(pf + "Bre", (128, 16, 16)); Bim = s5t(pf + "Bim", (128, 16, 16))
        DMA("act", plsem(), [], [pf + "Bre"], Bre[:], W["s5_b_re_" + sfx].rearrange("(gp two) p h -> (two p) gp h", two=2))
        DMA("act", plsem(), [], [pf + "Bim"], Bim[:], W["s5_b_im_" + sfx].rearrange("(gp two) p h -> (two p) gp h", two=2))
        CT = []
        for t, cn in enumerate(["s5_c_re_" + sfx, "s5_c_im_" + sfx]):
            CA = M.sb(pf + "CA%d" % t, [128, 2, 128], F32)
            for gp in range(16):
                DMA("act", plsem(), [], [(pf + "CA%d" % t, gp)],
                    CA[16 * (gp % 8):16 * (gp % 8) + 16, gp // 8, :].rearrange("ho (two p) -> ho two p", two=2),
                    W[cn][2 * gp:2 * gp + 2].rearrange("two ho p -> ho two p"))
            ct = s5t(pf + "CT%d" % t, (128, 16, 16))
            for half in range(2):
                OP("pe", "transpose", [(pf + "CA%d" % t, gp_) for gp_ in range(16)] + ["ident_f"], ["po1"], PO[1][:, half * 128:(half + 1) * 128],
                   CA[:, half, :], ident_f[:])
            OP("act", "activation", ["po1"], [pf + "CT%d" % t], ct[:].rearrange("p g h -> p (g h)"), PO[1][:, 0:256], AF.Copy)
            CT.append(ct)
        dtt = s5t(pf + "dt"); xr = s5t(pf + "xr"); xi = s5t(pf + "xi"); xi2 = s5t(pf + "xi2")
        mag = s5t(pf + "mag"); sn = s5t(pf + "sn"); cs = s5t(pf + "cs")
        OP("act", "activation", [(pf + "lsb", 0), (pf + "lsb", 1)], [pf + "dt"], dtt[:], lsb[:], AF.Exp)
        TT_("dve", xr[:], lre[:], dtt[:], ALU.mult, [pf + "lre", pf + "dt"], [pf + "xr"])
        TT_("dve", xi[:], lim[:], dtt[:], ALU.mult, [pf + "lim", pf + "dt"], [pf + "xi"])
        OP("dve", "tensor_scalar", [pf + "xi"], [pf + "xi2"], xi2[:], xi[:], PI / 2, None, ALU.add)
        OP("act", "activation", [pf + "xr"], [pf + "mag"], mag[:], xr[:], AF.Exp)
        sin_of(sn, xi, pf + "sn", pf + "xi")
        sin_of(cs, xi2, pf + "cs", pf + "xi2")
        APr = s5t(pf + "APr", (128, 5, 16)); APi = s5t(pf + "APi", (128, 5, 16))
        kr, ki = pf + "APr", pf + "APi"
        OP("dve", "memset", [], [kr], APr[:, 0, :], 1.0)
        OP("dve", "memset", [], [ki], APi[:, 0, :], 0.0)
        TT_("dve", APr[:, 1, :], mag[:], cs[:], ALU.mult, [pf + "mag", pf + "cs", kr], [kr])
        TT_("dve", APi[:, 1, :], mag[:], sn[:], ALU.mult, [pf + "mag", pf + "sn", ki], [ki])
        for k in range(2, 5):
            TT_("dve", tmpA[:], APr[:, k - 1, :], APr[:, 1, :], ALU.mult, [kr], ["tmpA"])
            TT_("dve", tmpB[:], APi[:, k - 1, :], APi[:, 1, :], ALU.mult, [ki], ["tmpB"])
            TT_("dve", APr[:, k, :], tmpA[:], tmpB[:], ALU.subtract, ["tmpA", "tmpB", kr], [kr])
            TT_("dve", tmpA[:], APr[:, k - 1, :], APi[:, 1, :], ALU.mult, [kr, ki], ["tmpA"])
            TT_("dve", tmpB[:], APi[:, k - 1, :], APr[:, 1, :], ALU.mult, [kr, ki], ["tmpB"])
            TT_("dve", APi[:, k, :], tmpA[:], tmpB[:], ALU.add, ["tmpA", "tmpB", ki], [ki])
        OP("dve", "tensor_copy", [kr], ["scanA"], scanA[:, d, 0, :], APr[:, TAU, :])
        OP("dve", "tensor_copy", [kr], ["scanA"], scanA[:, d, 1, :], APr[:, TAU, :])
        OP("dve", "tensor_scalar", [ki], ["scanA"], scanA[:, d, 2, :], APi[:, TAU, :], -1.0, None, ALU.mult)
        OP("dve", "tensor_copy", [ki], ["scanA"], scanA[:, d, 3, :], APi[:, TAU, :])
        def pwi(j):
            return (j - 1) if d == 0 else (16 - j)
        OP("dve", "tensor_copy", [kr], ["PW"], PW[:, d, 0, pwi(1), :], APr[:, TAU, :])
        OP("dve", "tensor_copy", [ki], ["PW"], PW[:, d, 1, pwi(1), :], APi[:, TAU, :])
        for j in range(2, 17):
            pr_, pi_ = PW[:, d, 0, pwi(j - 1), :], PW[:, d, 1, pwi(j - 1), :]
            TT_("dve", tmpA[:], pr_, APr[:, TAU, :], ALU.mult, ["PW", kr], ["tmpA"])
            TT_("dve", tmpB[:], pi_, APi[:, TAU, :], ALU.mult, ["PW", ki], ["tmpB"])
            TT_("dve", PW[:, d, 0, pwi(j), :], tmpA[:], tmpB[:], ALU.subtract, ["tmpA", "tmpB", "PW"], ["PW"])
            TT_("dve", tmpA[:], pr_, APi[:, TAU, :], ALU.mult, ["PW", ki], ["tmpA"])
            TT_("dve", tmpB[:], pi_, APr[:, TAU, :], ALU.mult, ["PW", kr], ["tmpB"])
            TT_("dve", PW[:, d, 1, pwi(j), :], tmpA[:], tmpB[:], ALU.add, ["tmpA", "tmpB", "PW"], ["PW"])
        OP("dve", "tensor_copy", ["PW"], ["scanB"], scanB[:, d, 0, :], PW[:, d, 0, pwi(16), :])
        OP("dve", "tensor_copy", ["PW"], ["scanB"], scanB[:, d, 1, :], PW[:, d, 0, pwi(16), :])
        OP("dve", "tensor_scalar", ["PW"], ["scanB"], scanB[:, d, 2, :], PW[:, d, 1, pwi(16), :], -1.0, None, ALU.mult)
        OP("dve", "tensor_copy", ["PW"], ["scanB"], scanB[:, d, 3, :], PW[:, d, 1, pwi(16), :])
        fr = s5t(pf + "fr"); fi = s5t(pf + "fi"); nr = s5t(pf + "nr"); den = s5t(pf + "den")
        OP("dve", "tensor_scalar", [kr], [pf + "nr"], nr[:], APr[:, 1, :], -1.0, None, ALU.add)
        TT_("dve", tmpA[:], lre[:], lre[:], ALU.mult, [pf + "lre"], ["tmpA"])
        TT_("dve", tmpB[:], lim[:], lim[:], ALU.mult, [pf + "lim"], ["tmpB"])
        TT_("dve", den[:], tmpA[:], tmpB[:], ALU.add, ["tmpA", "tmpB"], [pf + "den"])
        OP("dve", "reciprocal", [pf + "den"], [pf + "den"], den[:], den[:])
        TT_("dve", tmpA[:], nr[:], lre[:], ALU.mult, [pf + "nr", pf + "lre"], ["tmpA"])
        TT_("dve", tmpB[:], APi[:, 1, :], lim[:], ALU.mult, [ki, pf + "lim"], ["tmpB"])
        TT_("dve", tmpC[:], tmpA[:], tmpB[:], ALU.add, ["tmpA", "tmpB"], ["tmpC"])
        TT_("dve", fr[:], tmpC[:], den[:], ALU.mult, ["tmpC", pf + "den"], [pf + "fr"])
        TT_("dve", tmpA[:], APi[:, 1, :], lre[:], ALU.mult, [ki, pf + "lre"], ["tmpA"])
        TT_("dve", tmpB[:], nr[:], lim[:], ALU.mult, [pf + "nr", pf + "lim"], ["tmpB"])
        TT_("dve", tmpC[:], tmpA[:], tmpB[:], ALU.subtract, ["tmpA", "tmpB"], ["tmpC"])
        TT_("dve", fi[:], tmpC[:], den[:], ALU.mult, ["tmpC", pf + "den"], [pf + "fi"])
        t3a = s5t(pf + "t3a", (128, 16, 16)); t3b = s5t(pf + "t3b", (128, 16, 16))
        t3r = s5t(pf + "t3r", (128, 16, 16)); t3i = s5t(pf + "t3i", (128, 16, 16))

        def cmul(outr, outi, inr, ini, fre, fim, kin, kf, kout, neg_im=False):
            frb, fib = bc(fre, 2, 16), bc(fim, 2, 16)
            TT_("dve", t3a[:], inr, frb, ALU.mult, kin + kf, [pf + "t3a"])
            TT_("dve", t3b[:], ini, fib, ALU.mult, kin + kf, [pf + "t3b"])
            TT_("dve", outr, t3a[:], t3b[:], ALU.subtract, [pf + "t3a", pf + "t3b"], kout)
            TT_("dve", t3a[:], inr, fib, ALU.mult, kin + kf, [pf + "t3a"])
            TT_("dve", t3b[:], ini, frb, ALU.mult, kin + kf, [pf + "t3b"])
            if neg_im:
                OP("dve", "scalar_tensor_tensor", [pf + "t3a", pf + "t3b"], kout, outi, t3a[:], -1.0, t3b[:],
                   ALU.mult, ALU.subtract)
            else:
                TT_("dve", outi, t3a[:], t3b[:], ALU.add, [pf + "t3a", pf + "t3b"], kout)

        bbr = s5t(pf + "bbr", (128, 16, 16)); bbi = s5t(pf + "bbi", (128, 16, 16))
        cmul(bbr[:], bbi[:], Bre[:], Bim[:], fr[:], fi[:], [pf + "Bre", pf + "Bim"], [pf + "fr", pf + "fi"],
             [pf + "bb"])
        MBP = M.sb(pf + "MBP", [128, 2, 4, 16, 32], F32)
        MCP = M.sb(pf + "MCP", [128, 2, 5, 16, 32], F32)
        OP("pool", "memset", [], [pf + "MBP"], MBP[:], 0.0)
        OP("pool", "memset", [], [pf + "MCP"], MCP[:], 0.0)
        for e in range(4):
            cmul(t3r[:], t3i[:], bbr[:], bbi[:], APr[:, e, :], APi[:, e, :], [pf + "bb"], [kr, ki], [pf + "t3ri"])
            for ri, src in enumerate([t3r, t3i]):
                for two in range(2):
                    ps_ = slice(two * 64, two * 64 + 64)
                    OP("dve", "tensor_copy", [pf + "t3ri", pf + "MBP"], [pf + "MBP"],
                       MBP[ps_, ri, e, :, two * 16:(two + 1) * 16], src[ps_, :, :])
        for k in range(5):
            cmul(t3r[:], t3i[:], CT[0][:], CT[1][:], APr[:, k, :], APi[:, k, :], [pf + "CT0", pf + "CT1"], [kr, ki],
                 [pf + "t3ri"], neg_im=True)
            for ri, src in enumerate([t3r, t3i]):
                for two in range(2):
                    ps_ = slice(two * 64, two * 64 + 64)
                    OP("dve", "tensor_copy", [pf + "t3ri", pf + "MCP"], [pf + "MCP"],
                       MCP[ps_, ri, k, :, two * 16:(two + 1) * 16], src[ps_, :, :])
        for ftq in range(4):
            for j in range(4):
                e = (TAU - 1 - j) if d == 0 else j
                for ri in range(2):
                    bank = (j * 2 + ri) % 2
                    OP("pe", "transpose", [pf + "MBP", "ident_f"], [POK[bank]], PO[bank][:, 0:128],
                       MBP[:, ri, e, 4 * ftq:4 * ftq + 4, :].rearrange("p a b -> p (a b)"), ident_f[:])
                    OP("act", "activation", [POK[bank]], ["sinst"], sinst[:, ftq, j, ri, :], PO[bank][:, 0:128], AF.Copy)
        DMA("act", plsem(), ["sinst"], [], SIN_s[d], sinst[:].rearrange("p a b c e -> p (a b c e)"))
        for i in range(4):
            k = (i + 1) if d == 0 else (TAU - i)
            for ri in range(2):
                OP("dve", "tensor_copy", [pf + "MCP", "soutst"], ["soutst"], soutst[:, :, d, i, ri, :], MCP[:, ri, k, :, :])
        for ft in range(4):
            for q in range(4):
                gp = 4 * ft + q
                for ri in range(2):
                    OP("dve", "tensor_copy", [pf + "MBP", ("zb", q, ri)], [("zb", q, ri)],
                       ZB[q][ri][:, 32 * q:32 * q + 32], MBP[:, ri, 0, gp, :])
            for dl in range(4):
                bank = dl % 2
                for q in range(4):
                    gp = 4 * ft + q
                    for ri in range(2):
                        OP("pe", "matmul", [("zb", q, ri), pf + "MCP"], [POK[bank]], PO[bank][:, 0:32],
                           lhsT=ZB[q][ri][:], rhs=MCP[:, ri, dl, gp, :], start=(q == 0 and ri == 0),
                           stop=(q == 3 and ri == 1))
                TT_("dve", bdst[:, ft, d, dl, :].rearrange("p (q c) -> p q c", q=4), bc(PO[bank][:, 0:32], 1, 4),
                    bmask[:], ALU.mult, [POK[bank], "bmask", "bdst"], ["bdst"])
            if d == 0:
                OP("dve", "scalar_tensor_tensor", ["ident_f", "dcol", "bdst"], ["bdst"], bdst[:, ft, 0, 0, :],
                   ident_f[:], dcol[:, ft:ft + 1], bdst[:, ft, 0, 0, :], ALU.mult, ALU.add)
        S.barrier()
        M.release(mk_d)
    OP("act", "activation", ["bdst"], [("stb", 0)], stb[0][:], bdst[:].rearrange("p a b c e -> p (a b c e)"), AF.Copy)
    for ft in range(4):
        DMA("act", plsem(), [("stb", 0)], [], YC_s[ft][:, 0:1024], stb[0][:, ft * 1024:(ft + 1) * 1024])
        DMA("act", plsem(), ["soutst"], [], YC_s[ft][:, 1024:3072],
            soutst[:, 4 * ft:4 * ft + 4].rearrange("p a b c e f -> p (a b c e f)"))

    S.barrier()
    M.release(mk_pro)

    XT = [M.sb("xt0", [128, 4, D], F32), None]
    HT = [M.sb("hT0", [128, 8, TT], BF16), None]

    def xk(xi, r=None):
        return [("xt", xi, r_) for r_ in range(4)] if r is None else ("xt", xi, r)
    xn = [M.sb("xn%d" % i, [128, D], BF16) for i in range(2)]
    ntmp = M.sb("ntmp", [128, 8, 128], F32)
    hh = M.sb("hh", [128, NFT, TT], BF16)
    sg = [M.sb("sg%d" % i, [128, TT], BF16) for i in range(2)]
    ob = [M.sb("ob0", [128, TT], BF16)] * 2
    st_ssq = M.sb("st_ssq", [128, 8], F32)
    st_rstd = M.sb("st_rstd", [128, 8], F32)
    Ud = M.sb("Ud", [128, 4, TAU, NSUB], BF16)
    ESr = M.sb("ESr", [128, NSUB, 2, 16], F32)
    Sxr = M.sb("Sxr", [128, 2, 16, NSUB], BF16)
    VE = M.sb("VE", [128, 9, 2, 16], F32)
    sct1 = M.sb("sct1", [128, 8, 2, 16], F32)
    sct2 = M.sb("sct2", [128, 8, 2, 16], F32)
    sctb = M.sb("sctb", [128, 2, 16, 16], F32)
    carry = [M.sb("carry%d" % d, [128, 2, 16], F32) for d in range(2)]
    Sx = [M.sb("Sx0", [128, 2, 16, NSUB], BF16), None]
    rings = {"w": [M.sb("wbuf%d" % i, [128, 4096], BF16) for i in range(2)],
             "o": [M.sb("obuf%d" % i, [128, 3072], BF16) for i in range(2)],
             "q": []}
    ring_i = {"w": 0, "o": 0, "q": 0}

    def wload(kind, src, nfree):
        i = ring_i[kind]
        ring_i[kind] += 1
        sl = i % len(rings[kind])
        buf = rings[kind][sl]
        key = (kind + "buf", sl)
        DMA("sp", "%s%d" % (kind, sl), [], [key], buf[:, 0:nfree], src)
        return buf[:, 0:nfree], key

    def pump(g):
        if g is not None:
            next(g, None)

    def drain(g):
        if g is not None:
            for _ in g:
                pass

    def norm_gen(site, s, xi, hi):
        xt, hT = XT[xi], HT[hi]
        gs = modc[:, 3 * site + 1, s, :]
        sh = modc[:, 3 * site + 0, s, :]
        for r in range(4):
            sl = r % 2
            OP("act", "activation", [xk(xi, r)], [("xn", sl), ("ssq", r)], xn[sl][:], xt[:, r, :], AF.Square,
               accum_out=st_ssq[:, r:r + 1])
        ssk = [("ssq", r) for r in range(4)]
        rsk = [("rstd", r) for r in range(4)]
        OP("act", "activation", ssk, rsk, st_rstd[:, 0:4], st_ssq[:, 0:4], AF.Sqrt, bias=eps_t[:], scale=1.0 / D)
        OP("dve", "reciprocal", rsk, rsk, st_rstd[:, 0:4], st_rstd[:, 0:4])
        yield

        def tail(r):
            sl = r % 2
            for dt in range(8):
                OP("pe", "transpose", [("xn", sl), "ident_b"], ["tra"], TRA[:, dt, :], xn[sl][:, dt * 128:(dt + 1) * 128],
                   ident_b[:])
            OP("dve", "tensor_tensor", ["tra", "modc"], ["ntmp"], ntmp[:], TRA[:], bc(gs, 2, 128), ALU.mult)
            OP("dve", "tensor_tensor", ["ntmp", "modc"], [("hT", hi)], hT[:, :, r * 128:(r + 1) * 128], ntmp[:],
               bc(sh, 2, 128), ALU.add)

        for r in range(4):
            sl = r % 2
            OP("act", "activation", [xk(xi, r), ("rstd", r)], [("xn", sl)], xn[sl][:], xt[:, r, :], AF.Identity,
               scale=st_rstd[:, r:r + 1])
            if r > 0:
                tail(r - 1)
            yield
        tail(3)
        yield

    def norm_hT(site, s, xi=0, hi=0):
        drain(norm_gen(site, s, xi, hi))

    ep_i = [0]

    def epilogue(po_idx, gcol, do, xi=0):
        xt = XT[xi]
        i = ep_i[0]
        ep_i[0] += 1
        sl = i % 2
        OP("act", "activation", [POK[po_idx], "modc"], ["ob"], ob[sl][:], PO[po_idx][:], AF.Identity, scale=gcol)
        for r in range(4):
            OP("pe", "transpose", ["ob", "ident_b"], ["trb"], TRB[:, sl, r, :], ob[sl][:, r * 128:(r + 1) * 128],
               ident_b[:])
        OP("dve", "tensor_tensor", xk(xi) + ["trb"], xk(xi), xt[:, :, do * 128:(do + 1) * 128],
           xt[:, :, do * 128:(do + 1) * 128], TRB[:, sl, :, :], ALU.add)

    def ffn_step1(k, hi=0, filler=None):
        hT = HT[hi]
        nxt = wload("w", Win_s[k][0], 4096)
        for c in range(11):
            wt, wk = nxt
            if c + 1 < 11:
                nxt = wload("w", Win_s[k][c + 1], 4096)
            wv = wt.rearrange("p (dt gu f) -> p dt gu f", dt=8, gu=2)
            for f2 in range(2):
                ft = 2 * c + f2
                b = ft % 2
                for gu in range(2):
                    for dt in range(8):
                        OP("pe", "matmul", [wk, ("hT", hi)], [PGK[2 * gu + b]], PG[2 * gu + b][:],
                           lhsT=wv[:, dt, gu, f2 * 128:(f2 + 1) * 128], rhs=hT[:, dt, :], start=(dt == 0), stop=(dt == 7))
                OP("act", "activation", [PGK[b]], [("sg", b)], sg[b][:], PG[b][:], AF.Silu)
                OP("dve", "tensor_tensor", [("sg", b), PGK[2 + b]], [("hh", ft)], hh[:, ft, :], sg[b][:], PG[2 + b][:],
                   ALU.mult)
                pump(filler)

    def ffn_step2(k, s, xi=0, filler=None):
        site = 0 if k == 0 else 2
        nxt = wload("o", Wout_s[k][0], NFT * 128)
        for do in range(8):
            wt, wk = nxt
            if do + 1 < 8:
                nxt = wload("o", Wout_s[k][do + 1], NFT * 128)
            wv = wt.rearrange("p (ft f) -> p ft f", ft=NFT)
            b = do % 2
            for ft in range(NFT):
                OP("pe", "matmul", [wk, ("hh", ft)], [POK[b]], PO[b][:], lhsT=wv[:, ft, :], rhs=hh[:, ft, :],
                   start=(ft == 0), stop=(ft == NFT - 1))
            epilogue(b, modc[:, 3 * site + 2, s, do:do + 1], do, xi)
            pump(filler)

    def ffn(k, s):
        norm_hT(0 if k == 0 else 2, s)
        ffn_step1(k)
        ffn_step2(k, s)

    def s5_u_gen(wt, wk, hi=0):
        hT = HT[hi]
        wv = wt.rearrange("p (dt f) -> p dt f", dt=8)
        for ft in range(4):
            b = ft % 2
            for dt in range(8):
                OP("pe", "matmul", [wk, ("hT", hi)], [POK[b]], PO[b][:], lhsT=wv[:, dt, ft * 128:(ft + 1) * 128],
                   rhs=hT[:, dt, :], start=(dt == 0), stop=(dt == 7))
            OP("act", "activation", [POK[b]], [("Ud", ft)], Ud[:, ft, :, :].rearrange("p j n -> p n j"),
               PO[b][:].rearrange("p (n j) -> p n j", j=TAU), AF.Copy)
            yield

    def s5_u(wt, wk):
        drain(s5_u_gen(wt, wk))

    def statein_gen(wt, wk, ES, esk):
        sv = wt.rearrange("p (q j r c) -> p q j r c", q=4, j=TAU, r=2)
        for ri in range(2):
            for qp in range(2):
                for q4 in range(4):
                    for q2 in range(2):
                        qq = 2 * qp + q2
                        rows = slice(32 * qq, 32 * qq + 32)
                        for j in range(TAU):
                            OP("pe", "matmul", [wk, ("Ud", q4)], [POK[q2]], PO[q2][:, q4 * NSUB:(q4 + 1) * NSUB],
                               lhsT=sv[rows, q4, j, ri, :], rhs=Ud[rows, q4, j, :], start=(j == 0), stop=(j == TAU - 1),
                               tile_position=(32 * qq, 0))
                    if q4 % 2 == 1:
                        yield
                for q2 in range(2):
                    qq = 2 * qp + q2
                    OP("act", "activation", [POK[q2]], [esk],
                       ES[:, :, ri, :].rearrange("p n (a b) -> p b a n", b=4)[:, qq],
                       PO[q2][:].rearrange("p (a n) -> p a n", a=4), AF.Copy)

    def scan2(ES, esk, d, reverse, SXo, sxk, eng="pool"):
        v5 = ES[:].rearrange("p (b k) r g -> p b k r g", b=8)
        ck = ("carry", d)

        def step(prev, cur, tab, t1, t2, nb, rk, wk_):
            ar2, nai, ai = tab[:, d, 0:2, :], tab[:, d, 2, :], tab[:, d, 3, :]
            if nb:
                ar2, nai, ai = bc(ar2, 1, nb), bc(nai, 1, nb), bc(ai, 1, nb)
                i0, i1 = (slice(None), slice(None), 0, slice(None)), (slice(None), slice(None), 1, slice(None))
            else:
                i0, i1 = (slice(None), 0, slice(None)), (slice(None), 1, slice(None))
            OP(eng, "tensor_tensor", rk + ["scanA", "scanB"], ["sct1"], t1, prev, ar2, ALU.mult)
            OP(eng, "tensor_tensor", rk + ["scanA", "scanB"], ["sct2"], t2[i0], prev[i1], nai, ALU.mult)
            OP(eng, "tensor_tensor", rk + ["scanA", "scanB"], ["sct2"], t2[i1], prev[i0], ai, ALU.mult)
            OP(eng, "tensor_tensor", ["sct1"] + rk, wk_, cur, cur, t1, ALU.add)
            OP(eng, "tensor_tensor", ["sct2"] + rk, wk_, cur, cur, t2, ALU.add)

        ks = range(14, -1, -1) if reverse else range(1, 16)
        for k in ks:
            kp = k + 1 if reverse else k - 1
            step(v5[:, :, kp], v5[:, :, k], scanA, sct1[:], sct2[:], 8, [esk], [esk])
        if not reverse:
            OP(eng, "tensor_copy", [ck], ["VE"], VE[:, 0], carry[d][:])
            for b in range(8):
                OP(eng, "tensor_copy", [esk, "VE"], ["VE"], VE[:, b + 1], v5[:, b, 15])
                step(VE[:, b], VE[:, b + 1], scanB, sct1[:, 0], sct2[:, 0], 0, ["VE"], ["VE"])
            vin = VE[:, 0:8]
            OP(eng, "tensor_copy", ["VE"], [ck], carry[d][:], VE[:, 8])
        else:
            OP(eng, "tensor_copy", [ck], ["VE"], VE[:, 8], carry[d][:])
            for b in range(7, -1, -1):
                OP(eng, "tensor_copy", [esk, "VE"], ["VE"], VE[:, b], v5[:, b, 0])
                step(VE[:, b + 1], VE[:, b], scanB, sct1[:, 0], sct2[:, 0], 0, ["VE"], ["VE"])
            vin = VE[:, 1:9]
            OP(eng, "tensor_copy", ["VE"], [ck], carry[d][:], VE[:, 0])
        for hb in range(4):
            bs = slice(2 * hb, 2 * hb + 2)
            sr = v5[:, bs, :, 0, :]
            si = v5[:, bs, :, 1, :]
            pr = bc(PW[:, d, 0], 1, 2)
            pi = bc(PW[:, d, 1], 1, 2)
            vr = bc(vin[:, bs, 0, :], 2, 16)
            vi = bc(vin[:, bs, 1, :], 2, 16)
            for (pa, va, tgt, op) in ((pr, vr, sr, ALU.add), (pi, vi, sr, ALU.subtract), (pr, vi, si, ALU.add),
                                      (pi, vr, si, ALU.add)):
                OP(eng, "tensor_tensor", ["PW", "VE"], ["sctb"], sctb[:], pa, va, ALU.mult)
                OP(eng, "tensor_tensor", ["sctb", esk], [esk], tgt, tgt, sctb[:], op)
        if not reverse:
            OP(eng, "tensor_copy", ["VE", sxk], [sxk], SXo[:, :, :, 0], VE[:, 0])
            OP(eng, "tensor_copy", [esk, sxk], [sxk], SXo[:, :, :, 1:NSUB].rearrange("p r g n -> p n r g"),
               ES[:, 0:NSUB - 1, :, :])
        else:
            OP(eng, "tensor_copy", ["VE", sxk], [sxk], SXo[:, :, :, NSUB - 1], VE[:, 8])
            OP(eng, "tensor_copy", [esk, sxk], [sxk], SXo[:, :, :, 0:NSUB - 1].rearrange("p r g n -> p n r g"),
               ES[:, 1:NSUB, :, :])

    def bwd_pass_gen(s):
        OP("pool", "memset", [("carry", 1)], [("carry", 1)], carry[1][:], 0.0)
        for m in range(NMT[s] - 1, -1, -1):
            DMA("pool", "ebld", [("ebd", s, m)], ["ESr"], ESr[:].rearrange("p n r g -> p (n r g)"), EB_s[s][m])
            scan2(ESr, "ESr", 1, True, Sxr, "Sxr")
            DMA("pool", "sbst", ["Sxr"], [("sbd", s, m)], SB_s[s][m], Sxr[:].rearrange("p r g n -> p (r g n)"))
            yield

    def bwd_pass(s):
        for _ in bwd_pass_gen(s):
            pass

    def load_x(src_ap, base, rk=(), xi=0):
        for r in range(4):
            DMA("act", "xld%d" % r, list(rk), [xk(xi, r)], XT[xi][:, r, :], src_ap[base + r * 128:base + (r + 1) * 128, :])

    def P_gen(s, m, xi):
        load_x(x_in[s], m * TT, (), xi)
        yield
        yield from norm_gen(0, s, xi, 0)

    def Q_gen(s, m, xi):
        base = m * TT
        for _ in range(2):
            if deferred:
                cast_dma(*deferred.pop(0))
        DMA("act", "x1st", xk(xi), [("x1d", s, m)], X1_s[s][base:base + TT, :].rearrange("(r p) d -> p r d", p=128),
            XT[xi][:])
        wt, wk = wload("q", Wmi_s[0], 4096)
        yield
        yield from norm_gen(1, s, xi, 1)
        wt2, wk2 = wload("q", SIN_s[0], 4096)
        yield from s5_u_gen(wt, wk, 1)
        if m == 0:
            OP("pool", "memset", [("carry", 0)], [("carry", 0)], carry[0][:], 0.0)
        yield from statein_gen(wt2, wk2, ESf, "ESf")
        wt3, wk3 = wload("q", SIN_s[1], 4096)
        scan2(ESf, "ESf", 0, False, Sx[0], ("Sx", 0))
        DMA("pool", "sfst", [("Sx", 0)], [("sfd", s, m)], SF_s[s][m], Sx[0][:].rearrange("p r g n -> p (r g n)"))
        yield
        yield from statein_gen(wt3, wk3, ESb, "ESb")
        DMA("act", "ebst", ["ESb"], [("ebd", s, m)], EB_s[s][m], ESb[:].rearrange("p n r g -> p (n r g)"))
        yield

    def PB_gen(s, m, xi):
        load_x(X1_s[s], m * TT, [("x1d", s, m)], xi)
        yield
        yield from norm_gen(1, s, xi, 0)

    def phase_b_mixer(s, m, xi):
        base = m * TT
        xt, hT = XT[xi], HT[0]
        DMA("act", "sfld", [("sfd", s, m)], [("Sx", 0)], Sx[0][:].rearrange("p r g n -> p (r g n)"), SF_s[s][m])
        DMA("act", "sbld", [("sbd", s, m)], [("Sx", 1)], Sx[1][:].rearrange("p r g n -> p (r g n)"), SB_s[s][m])
        wt, wk = wload("w", Wmi_s[0], 4096)
        s5_u(wt, wk)
        wu, wuk = wload("w", Wmi_s[1], 4096)
        wv_, wvk = wload("w", Wmi_s[2], 4096)
        wuv = wu.rearrange("p (dt f) -> p dt f", dt=8)
        wvv = wv_.rearrange("p (dt f) -> p dt f", dt=8)
        nxt_y = wload("o", YC_s[0], 3072)

        def g_tail(r):
            p = r % 2
            tok = slice(r * 128, (r + 1) * 128)
            for ct in range(4):
                OP("pe", "transpose", [("ygn", p), "ident_b"], ["tra"], TRA[:, ct, :], ygn[p][:, ct * 128:(ct + 1) * 128],
                   ident_b[:])
            OP("act", "activation", ["tra"], [("ycat", 1)], ycatT[:, 4:8, tok], TRA[:, 0:4, :], AF.Copy)

        ny = [nxt_y]
        def zpart(r):
            p = r % 2
            ft = r
            tok = slice(r * 128, (r + 1) * 128)
            for dt in range(8):
                OP("pe", "matmul", [wuk, ("hT", 0)], ["pg0"], PG[0][:], lhsT=hT[:, dt, tok], rhs=wuv[:, dt, :], start=(dt == 0),
                   stop=(dt == 7))
            for dt in range(8):
                OP("pe", "matmul", [wvk, ("hT", 0)], ["pg1"], PG[1][:], lhsT=hT[:, dt, tok], rhs=wvv[:, dt, :], start=(dt == 0),
                   stop=(dt == 7))
            OP("act", "activation", ["pg0"], [("ug", p)], ug[p][:], PG[0][:], AF.Gelu_apprx_tanh)
            OP("act", "activation", ["pg1"], ["vg"], vg[p][:], PG[1][:], AF.Gelu_apprx_tanh)
            OP("dve", "bn_stats", ["vg"], ["bnst"], bnst[:], vg[p][:])
            OP("dve", "bn_aggr", ["bnst"], ["bnmv"], bnmv[:], bnst[:])
            OP("act", "activation", ["bnmv"], ["bnrs"], st_rstd[:, 4:5], bnmv[:, 1:2], AF.Sqrt, bias=eps_t[:], scale=1.0)
            OP("dve", "reciprocal", ["bnrs"], ["bnrs"], st_rstd[:, 4:5], st_rstd[:, 4:5])
            OP("dve", "tensor_scalar", ["vg", "bnmv", "bnrs"], ["vg"], vg[p][:], vg[p][:], bnmv[:, 0:1],
               st_rstd[:, 4:5], ALU.subtract, ALU.mult)
            OP("dve", "tensor_tensor", ["vg", "lng_rep"], ["vg"], vg[p][:], vg[p][:], lng_rep[:], ALU.mult)
            OP("dve", "tensor_tensor", ["vg", "lnb_rep"], [("vn2", p)], vn2[p][:], vg[p][:], lnb_rep[:], ALU.add)

        def ypart(r):
            p = r % 2
            ft = r
            tok = slice(r * 128, (r + 1) * 128)
            yc, yck = ny[0]
            if ft + 1 < 4:
                ny[0] = wload("o", YC_s[ft + 1], 3072)
            bdv = yc[:, 0:1024].rearrange("p (d l c) -> p d l c", d=2, l=TAU)
            sov = yc[:, 1024:3072].rearrange("p (g d i r c) -> p g d i r c", g=4, d=2, i=TAU, r=2)
            b = 2 + ft % 2
            for i in range(TAU):
                reg = PG[b][:, i * NSUB:(i + 1) * NSUB]
                first = True
                for j in range(TAU):
                    if j <= i:
                        OP("pe", "matmul", [yck, ("Ud", ft)], [PGK[b]], reg, lhsT=bdv[:, 0, i - j, :], rhs=Ud[:, ft, j, :],
                           start=first, stop=False)
                        first = False
                    if j >= i:
                        OP("pe", "matmul", [yck, ("Ud", ft)], [PGK[b]], reg, lhsT=bdv[:, 1, j - i, :], rhs=Ud[:, ft, j, :],
                           start=first, stop=False)
                        first = False
                for qq in range(4):
                    gp = 4 * ft + qq
                    for d in range(2):
                        for ri in range(2):
                            last = (qq == 3 and d == 1 and ri == 1)
                            OP("pe", "matmul", [yck, ("Sx", d)], [PGK[b]],
                               PG[b][32 * qq:32 * qq + 32, i * NSUB:(i + 1) * NSUB],
                               lhsT=sov[:, qq, d, i, ri, :], rhs=Sx[d][:, ri, gp, :], start=False, stop=last,
                               tile_position=(0, 32 * qq))
            OP("act", "activation", [PGK[b]], [("y1f", ft)], y1f[:, ft, :].rearrange("p (n i) -> p n i", i=TAU),
               PG[b][:].rearrange("p (i n) -> p n i", i=TAU), AF.Gelu_apprx_tanh)
            OP("act", "activation", [("y1f", ft)], [("y1b", ft)], y1b[:, ft, :], y1f[:, ft, :], AF.Copy)

        def sppart(r):
            p = r % 2
            ft = r
            tok = slice(r * 128, (r + 1) * 128)
            for h in range(4):
                OP("pe", "matmul", ["wspT", ("vn2", p)], [POK[p]], PO[p][:, h * 128:(h + 1) * 128], lhsT=wspT[:, h, :],
                   rhs=vn2[p][:, h * 128:(h + 1) * 128], start=True, stop=True)
            for h in range(4):
                OP("dve", "scalar_tensor_tensor", [POK[p], "bsp_c", ("ug", p)], ["ygm"], ygm[p][:, h * 128:(h + 1) * 128],
                   PO[p][:, h * 128:(h + 1) * 128], bsp_c[:, h:h + 1], ug[p][:, h * 128:(h + 1) * 128], ALU.add, ALU.mult)
            OP("act", "activation", ["ygm"], [("ygn", p), "gssq"], ygn[p][:], ygm[p][:], AF.Square,
               accum_out=st_ssq[:, 5:6])
            OP("act", "activation", ["gssq"], ["grs"], st_rstd[:, 5:6], st_ssq[:, 5:6], AF.Sqrt, bias=eps_t[:], scale=1.0 / 512)
            OP("dve", "reciprocal", ["grs"], ["grs"], st_rstd[:, 5:6], st_rstd[:, 5:6])
            OP("act", "activation", ["ygm", "grs"], [("ygn", p)], ygn[p][:], ygm[p][:], AF.Identity,
               scale=st_rstd[:, 5:6])

        zpart(0)
        ypart(0)
        zpart(1)
        sppart(0)
        ypart(1)
        zpart(2)
        sppart(1)
        g_tail(0)
        ypart(2)
        zpart(3)
        sppart(2)
        g_tail(1)
        ypart(3)
        wg, wgk = wload("w", Wgl_s, 2048)
        wgv = wg.rearrange("p (kt f) -> p kt f", kt=4)
        y1bk = [("y1b", ft) for ft in range(4)]
        for fo in range(4):
            b = fo % 2
            for kt in range(4):
                OP("pe", "matmul", [wgk] + y1bk, [POK[b]], PO[b][:], lhsT=wgv[:, kt, fo * 128:(fo + 1) * 128], rhs=y1b[:, kt, :],
                   start=(kt == 0), stop=(kt == 3))
            OP("act", "activation", [POK[b]], [("sg", b)], sg[b][:], PO[b][:], AF.Sigmoid)
            OP("dve", "tensor_tensor", [("y1f", fo), ("sg", b)], [("y1f", fo)], y1f[:, fo, :], y1f[:, fo, :], sg[b][:], ALU.mult)
        sppart(3)
        g_tail(2)
        g_tail(3)
        for fo in range(4):
            OP("act", "activation", [("y1f", fo)], [("y1b", fo)], sqb[:, fo, :], y1f[:, fo, :], AF.Square)
        for fo in range(4):
            OP("pe", "matmul", ["ones_b", ("y1b", fo)], ["po0"], PO[0][:], lhsT=ones_b[:], rhs=sqb[:, fo, :], start=(fo == 0),
               stop=(fo == 3))
        OP("act", "activation", ["po0"], ["ntmp"], rs5[:], PO[0][:], AF.Sqrt, bias=eps_t[:], scale=1.0 / 512)
        OP("dve", "reciprocal", ["ntmp"], ["ntmp"], rs5[:], rs5[:])
        OP("dve", "tensor_tensor", [("y1f", fo) for fo in range(4)] + ["ntmp"], [("ycat", 0)], ycatT[:, 0:4, :], y1f[:],
           bc(rs5[:], 1, 4), ALU.mult)
        nxt = wload("o", Wmo_s[0], 1024)
        for do in range(8):
            wt, wk = nxt
            if do + 1 < 8:
                nxt = wload("o", Wmo_s[do + 1], 1024)
            wv = wt.rearrange("p (kt f) -> p kt f", kt=8)
            b = do % 2
            for kt in range(8):
                OP("pe", "matmul", [wk, ("ycat", kt // 4)], [POK[b]], PO[b][:], lhsT=wv[:, kt, :], rhs=ycatT[:, kt, :],
                   start=(kt == 0), stop=(kt == 7))
            epilogue(b, modc[:, 5, s, do:do + 1], do, xi)

    def final_norm(s, m, xi):
        base = m * TT
        xt = XT[xi]
        for r in range(4):
            sl = r % 2
            OP("act", "activation", [xk(xi, r)], [("xn", sl), ("ssq", r)], xn[sl][:], xt[:, r, :], AF.Square,
               accum_out=st_ssq[:, r:r + 1])
            OP("act", "activation", [("ssq", r)], [("rstd", r)], st_rstd[:, r:r + 1], st_ssq[:, r:r + 1], AF.Sqrt,
               bias=eps_t[:], scale=1.0 / D)
            OP("dve", "reciprocal", [("rstd", r)], [("rstd", r)], st_rstd[:, r:r + 1], st_rstd[:, r:r + 1])
            OP("dve", "scalar_tensor_tensor", [xk(xi, r), ("rstd", r), "fg_rep"], ["ntmp"],
               ntmp[:].rearrange("p a b -> p (a b)"), xt[:, r, :], st_rstd[:, r:r + 1], fg_rep[:], ALU.mult, ALU.mult)
            DMA("act", "yo", ["ntmp"], [], y_out[s][base + r * 128:base + (r + 1) * 128, :],
                ntmp[:].rearrange("p a b -> p (a b)"))

    mk_ph = M.mark()
    ESf = M.sb("ESf", [128, NSUB, 2, 16], F32)
    ESb = M.sb("ESb", [128, NSUB, 2, 16], F32)
    XT[1] = M.sb("xt1", [128, 4, D], F32)
    HT[1] = M.sb("hT1", [128, 8, TT], BF16)
    rings["q"] = [M.sb("qbuf%d" % i, [128, 4096], BF16) for i in range(2)]
    tiles = [(s_, m_) for s_ in range(2) for m_ in range(NMT[s_])]
    bgen = None
    drain(P_gen(tiles[0][0], tiles[0][1], 0))
    qprev = None
    for i, (s_, m_) in enumerate(tiles):
        xi = i % 2
        ffn_step1(0, 0, qprev)
        drain(qprev)
        pnext = P_gen(tiles[i + 1][0], tiles[i + 1][1], 1 - xi) if i + 1 < len(tiles) else None
        ffn_step2(0, s_, xi, pnext)
        drain(pnext)
        qprev = Q_gen(s_, m_, xi)
    drain(qprev)
    while deferred:
        cast_dma(*deferred.pop(0))
    S.barrier()
    M.release(mk_ph)
    rings["w"].append(M.sb("wbuf2", [128, 4096], BF16))
    lng_rep = M.sb("lng_rep", [128, 512], F32)
    lnb_rep = M.sb("lnb_rep", [128, 512], F32)
    fg_rep = M.sb("fg_rep", [128, D], F32)
    DMA("act", "pl0", [], ["lng_rep"], lng_rep[:], W["gmlp_ln_g"].partition_broadcast(128))
    DMA("act", "pl1", [], ["lnb_rep"], lnb_rep[:], W["gmlp_ln_b"].partition_broadcast(128))
    DMA("act", "pl2", [], ["fg_rep"], fg_rep[:], W["final_norm_g"].partition_broadcast(128))
    Sx[1] = M.sb("Sx1", [128, 2, 16, NSUB], BF16)
    y1f = M.sb("y1f", [128, 4, TT], F32)
    y1b = M.sb("y1b", [128, 4, TT], BF16)
    rs5 = ntmp[:].rearrange("p a b -> p (a b)")[:, 0:TT]
    XT[1] = M.sb("xt1b", [128, 4, D], F32)
    ycatT = M.sb("ycatT", [128, 8, TT], BF16)
    ug = [M.sb("ug%d" % i, [128, 512], F32) for i in range(2)]
    vg = [M.sb("vg0", [128, 512], F32)] * 2
    vn2 = [M.sb("vn2_%d" % i, [128, 512], BF16) for i in range(2)]
    ygm = [M.sb("ygm0", [128, 512], F32)] * 2
    ygn = [M.sb("ygn%d" % i, [128, 512], BF16) for i in range(2)]
    bnst = M.sb("bnst", [128, 6], F32)
    bnmv = M.sb("bnmv", [128, 2], F32)
    sqb = y1b
    bwd_pass(1)
    bwd_pass(0)
    tilesB = [(s_, m_) for s_ in (1, 0) for m_ in range(NMT[s_] - 1, -1, -1)]
    drain(PB_gen(tilesB[0][0], tilesB[0][1], 0))
    for i, (s_, m_) in enumerate(tilesB):
        xi = i % 2
        phase_b_mixer(s_, m_, xi)
        norm_hT(2, s_, xi, 0)
        ffn_step1(1, 0)
        pnext = PB_gen(tilesB[i + 1][0], tilesB[i + 1][1], 1 - xi) if i + 1 < len(tilesB) else None
        ffn_step2(1, s_, xi, pnext)
        drain(pnext)
        final_norm(s_, m_, xi)

    S.barrier()
    S.emit()
    M.release(0)
    S.close()
    return nc


_NC_CACHE = {}


def kernel(**inputs):
    x_prompt = np.asarray(inputs["x_prompt"], dtype=np.float32)
    x_sample = np.asarray(inputs["x_sample"], dtype=np.float32)
    c_prompt = np.asarray(inputs["c_prompt"], dtype=np.float32)
    c_sample = np.asarray(inputs["c_sample"], dtype=np.float32)
    B, LP, _ = x_prompt.shape
    _, LS, _ = x_sample.shape
    n = 8
    key = (LP, LS)
    if key not in _NC_CACHE:
        _NC_CACHE[key] = build(LP, LS)
    nc = _NC_CACHE[key]
    wmap = {}
    for nm in WNAMES:
        a = np.asarray(inputs[nm], dtype=np.float32)
        if nm != "final_norm_g":
            a = a[0]
        wmap[nm] = np.ascontiguousarray(a)
    in_maps = []
    for i in range(n):
        mp = dict(wmap)
        mp["x_p"] = np.ascontiguousarray(x_prompt[i])
        mp["x_s"] = np.ascontiguousarray(x_sample[i])
        mp["c"] = np.ascontiguousarray(np.stack([c_prompt[i], c_sample[i]], axis=0))
        in_maps.append(mp)
    res = run_bass_kernel_spmd(nc, in_maps, core_ids=list(range(n)))
    yp = np.stack([np.asarray(res.results[i]["y_p"], dtype=np.float32) for i in range(n)], axis=0)
    ys = np.stack([np.asarray(res.results[i]["y_s"], dtype=np.float32) for i in range(n)], axis=0)
    return (yp, ys)
```

```python
import numpy as np
import concourse.bass as bass
import concourse.mybir as mybir
from concourse.bass_utils import run_bass_kernel_spmd

F32 = mybir.dt.float32
BF16 = mybir.dt.bfloat16
I32 = mybir.dt.int32
AF = mybir.ActivationFunctionType
ALU = mybir.AluOpType

ENGS = ("pe", "act", "dve", "pool", "sp")
D = 1024
DFF = 2816
NFT = 22
TAU = 4
TT = 512
NSUB = TT // TAU
EPS = 1e-6
PI = 3.14159265358979


class Sched:
    def __init__(self, nc):
        self.nc = nc
        self.q = {e: [] for e in ENGS}
        self.cnt = {e: 0 for e in ENGS}
        self.esem = {}
        self.last_w = {}
        self.readers = {}
        self.seen = {e: {} for e in ENGS}
        self.dsem = {}
        self._ctx = []

    def open(self):
        for e in ENGS:
            cm = self.nc.semaphore("es_" + e)
            self.esem[e] = cm.__enter__()
            self._ctx.append(cm)

    def dma_sem(self, name):
        if name not in self.dsem:
            cm = self.nc.semaphore("ds_" + name)
            h = cm.__enter__()
            self._ctx.append(cm)
            self.dsem[name] = [h, 0]
        return name

    def close(self):
        for cm in reversed(self._ctx):
            cm.__exit__(None, None, None)

    def _need(self, eng, ev, waits):
        if ev is None:
            return
        sk, val = ev
        if sk == ("e", "pe") and eng == "pe":
            return
        if self.seen[eng].get(sk, 0) >= val:
            return
        self.seen[eng][sk] = val
        waits[sk] = max(waits.get(sk, 0), val)

    def _deps(self, eng, reads, writes):
        waits = {}
        for k in reads:
            self._need(eng, self.last_w.get(k), waits)
        for k in writes:
            self._need(eng, self.last_w.get(k), waits)
            for sk, val in self.readers.get(k, {}).items():
                self._need(eng, (sk, val), waits)
        return waits

    def _commit(self, ev, reads, writes):
        for k in reads:
            d = self.readers.setdefault(k, {})
            d[ev[0]] = max(d.get(ev[0], 0), ev[1])
        for k in writes:
            self.last_w[k] = ev
            self.readers[k] = {}

    def op(self, eng, fn, reads=(), writes=()):
        waits = self._deps(eng, reads, writes)
        self.cnt[eng] += 1
        ev = (("e", eng), self.cnt[eng])
        self.q[eng].append((waits, fn, None))
        self._commit(ev, reads, writes)
        return ev

    def dma(self, eng, fn, sem, reads=(), writes=()):
        self.dma_sem(sem)
        waits = self._deps(eng, reads, writes)
        if self.dsem[sem][1]:
            self._need(eng, (("d", sem), self.dsem[sem][1]), waits)
        self.dsem[sem][1] += 16
        ev = (("d", sem), self.dsem[sem][1])
        self.q[eng].append((waits, fn, sem))
        self._commit(ev, reads, writes)
        return ev

    def wait_all(self, eng):
        waits = {}
        for e in ENGS:
            if self.cnt[e] and e != eng:
                self._need(eng, (("e", e), self.cnt[e]), waits)
        for name, (h, c) in self.dsem.items():
            if c:
                self._need(eng, (("d", name), c), waits)
        if waits:
            self.q[eng].append((waits, None, None))

    def barrier(self):
        for e in ENGS:
            self.wait_all(e)

    def _semh(self, sk):
        return self.esem[sk[1]] if sk[0] == "e" else self.dsem[sk[1]][0]

    def emit(self):
        nc = self.nc
        with nc.Block() as block:
            def mk(ename):
                def body(eng):
                    for waits, fn, dsem in self.q[ename]:
                        for sk, val in waits.items():
                            eng.wait_ge(self._semh(sk), val)
                        if fn is None:
                            continue
                        ins = fn(eng)
                        if dsem is not None:
                            ins.then_inc(self.dsem[dsem][0], 16)
                        else:
                            ins.then_inc(self.esem[ename], 1)
                return body
            block.tensor(mk("pe"))
            block.scalar(mk("act"))
            block.vector(mk("dve"))
            block.gpsimd(mk("pool"))
            block.sync(mk("sp"))


class Pool_:
    def __init__(self, nc):
        self.nc = nc
        self.stack = []

    def sb(self, name, shape, dt):
        cm = self.nc.sbuf_tensor(name, list(shape), dt)
        t = cm.__enter__()
        self.stack.append(cm)
        return t

    def ps(self, name, shape, dt):
        cm = self.nc.psum_tensor(name, list(shape), dt)
        t = cm.__enter__()
        self.stack.append(cm)
        return t

    def mark(self):
        return len(self.stack)

    def release(self, mark):
        while len(self.stack) > mark:
            self.stack.pop().__exit__(None, None, None)


WNAMES = ["w_ada", "b_ada", "norm_ffn1_g", "ffn1_w_in", "ffn1_w_out", "norm_mix_g", "w_mix_in",
          "s5_lam_re_f", "s5_lam_im_f", "s5_log_step_f", "s5_b_re_f", "s5_b_im_f", "s5_c_re_f", "s5_c_im_f",
          "s5_lam_re_b", "s5_lam_im_b", "s5_log_step_b", "s5_b_re_b", "s5_b_im_b", "s5_c_re_b", "s5_c_im_b",
          "s5_d", "s5_w_glu", "gmlp_ln_g", "gmlp_ln_b", "gmlp_w_sp", "gmlp_b_sp",
          "norm_out_s5_g", "norm_out_gmlp_g", "w_mix_out", "norm_ffn2_g", "ffn2_w_in", "ffn2_w_out",
          "final_norm_g"]
WSHAPES = {
    "w_ada": [D, 9 * D], "b_ada": [9 * D], "norm_ffn1_g": [D], "ffn1_w_in": [D, 2 * DFF],
    "ffn1_w_out": [DFF, D], "norm_mix_g": [D], "w_mix_in": [D, 1536],
    "s5_lam_re_f": [32, 64], "s5_lam_im_f": [32, 64], "s5_log_step_f": [32],
    "s5_b_re_f": [32, 64, 16], "s5_b_im_f": [32, 64, 16], "s5_c_re_f": [32, 16, 64], "s5_c_im_f": [32, 16, 64],
    "s5_lam_re_b": [32, 64], "s5_lam_im_b": [32, 64], "s5_log_step_b": [32],
    "s5_b_re_b": [32, 64, 16], "s5_b_im_b": [32, 64, 16], "s5_c_re_b": [32, 16, 64], "s5_c_im_b": [32, 16, 64],
    "s5_d": [512], "s5_w_glu": [512, 512], "gmlp_ln_g": [512], "gmlp_ln_b": [512],
    "gmlp_w_sp": [4, 128, 128], "gmlp_b_sp": [4, 128],
    "norm_out_s5_g": [512], "norm_out_gmlp_g": [512], "w_mix_out": [D, D], "norm_ffn2_g": [D],
    "ffn2_w_in": [D, 2 * DFF], "ffn2_w_out": [DFF, D], "final_norm_g": [D],
}


def build(LP, LS, dbg=None):
    nc = bass.Bass("TRN2", target_bir_lowering=False)
    S = Sched(nc)
    S.open()
    M = Pool_(nc)
    LEN = [LP, LS]
    NMT = [LP // TT, LS // TT]

    def din(name, shape, dt=F32):
        return nc.dram_tensor(name, list(shape), dt, kind="ExternalInput").ap()

    x_in = [din("x_p", [LP, D]), din("x_s", [LS, D])]
    c_in = din("c", [2, D])
    W = {n: din(n, WSHAPES[n]) for n in WNAMES}
    y_out = [nc.dram_tensor("y_p", [LP, D], F32, kind="ExternalOutput").ap(),
             nc.dram_tensor("y_s", [LS, D], F32, kind="ExternalOutput").ap()]
    dbg_out = {}
    if dbg:
        for k, shp in dbg.items():
            dbg_out[k] = nc.dram_tensor("dbg_" + k, list(shp), F32, kind="ExternalOutput").ap()

    def scr(name, shape, dt):
        return nc.dram_tensor(name, list(shape), dt, kind="Internal").ap()

    Win_s = [scr("win_s%d" % k, [11, 128, 4096], BF16) for k in range(2)]
    Wout_s = [scr("wout_s%d" % k, [8, 128, NFT * 128], BF16) for k in range(2)]
    Wmi_s = scr("wmi_s", [3, 128, 4096], BF16)
    Wmo_s = scr("wmo_s", [8, 128, 1024], BF16)
    Wgl_s = scr("wgl_s", [128, 2048], BF16)
    SIN_s = scr("sin_s", [2, 128, 4096], BF16)
    SOUT_s = scr("sout_s", [2, 128, 4096], BF16)
    BD_s = scr("bd_s", [128, 4096], BF16)
    YC_s = scr("yc_s", [4, 128, 3072], BF16)
    X1_s = [scr("x1_s%d" % s, [LEN[s], D], F32) for s in range(2)]
    SF_s = [scr("sf_s%d" % s, [NMT[s], 128, 4096], BF16) for s in range(2)]
    SB_s = [scr("sb_s%d" % s, [NMT[s], 128, 4096], BF16) for s in range(2)]
    EB_s = [scr("eb_s%d" % s, [NMT[s], 128, 4096], F32) for s in range(2)]

    def OP(eng, meth, reads, writes, *a, **kw):
        return S.op(eng, lambda e: getattr(e, meth)(*a, **kw), reads, writes)

    def DMA(eng, sem, reads, writes, out, in_, **kw):
        return S.dma(eng, lambda e: e.dma_start(out=out, in_=in_, **kw), sem, reads, writes)

    def bc(ap, axis, n):
        shp = list(ap.shape)
        shp.insert(axis, n)
        return ap.unsqueeze(axis).broadcast_to(shp)

    ident_f = M.sb("ident_f", [128, 128], F32)
    ident_b = M.sb("ident_b", [128, 128], BF16)
    ones_b = M.sb("ones_b", [128, 128], BF16)
    eps_t = M.sb("eps_t", [128, 1], F32)
    modc = M.sb("modc", [128, 9, 2, 8], F32)
    wspT = M.sb("wspT", [128, 4, 128], BF16)
    bsp_c = M.sb("bsp_c", [128, 4], F32)
    scanA = M.sb("scanA", [128, 2, 4, 16], F32)
    scanB = M.sb("scanB", [128, 2, 4, 16], F32)
    PW = M.sb("PW", [128, 2, 2, 16, 16], F32)

    PG = [M.ps("pg%d" % i, [128, 512], F32) for i in range(4)]
    PO = [M.ps("po%d" % i, [128, 512], F32) for i in range(2)]
    TRA = M.ps("tra", [128, 8, 128], BF16)
    TRB = M.ps("trb", [128, 2, 4, 128], BF16)
    PGK = ["pg0", "pg1", "pg2", "pg3"]
    POK = ["po0", "po1"]

    iot = M.sb("iot", [128, 128], I32)
    OP("pool", "iota", [], ["iot"], iot[:], [[1, 128]], base=0, channel_multiplier=-1)
    OP("dve", "tensor_scalar", ["iot"], ["ident_f"], ident_f[:], iot[:], 0.0, None, ALU.is_equal)
    OP("dve", "tensor_copy", ["ident_f"], ["ident_b"], ident_b[:], ident_f[:])
    OP("dve", "memset", [], ["ones_b"], ones_b[:], 1.0)
    OP("dve", "memset", [], ["eps_t"], eps_t[:], EPS)

    mk_pro = M.mark()
    cT = M.sb("cT", [128, 8, 2], F32)
    bcol = M.sb("bcol", [128, 72], F32)
    ngc = M.sb("ngc", [128, 3, 8], F32)
    stg = [M.sb("stg%d" % i, [128, 4096], F32) for i in range(2)]
    stb = [M.sb("stb%d" % i, [128, 4096], BF16) for i in range(2)]
    for s_ in range(2):
        DMA("act", "pl%d" % (6 + s_), [], [("cTl", s_)], cT[:, :, s_], c_in[s_].rearrange("(dt p) -> p dt", p=128),
            allow_slow_non_contiguous=True)
    DMA("act", "pl1", [], ["bcol"], bcol[:], W["b_ada"].rearrange("(ft p) -> p ft", p=128), allow_slow_non_contiguous=True)
    for j, nm in enumerate(["norm_ffn1_g", "norm_mix_g", "norm_ffn2_g"]):
        DMA("act", "pl%d" % (2 + j), [], [("ngc", j)], ngc[:, j, :], W[nm].rearrange("(dt p) -> p dt", p=128),
            allow_slow_non_contiguous=True)
    OP("act", "activation", [("cTl", 0), ("cTl", 1)], ["cT"], cT[:], cT[:], AF.Silu)
    modps = PG[0]
    wada_v = W["w_ada"].rearrange("(dt p) f -> p dt f", p=128)
    for ch in range(18):
        sl = ch % 2
        DMA("sp", "cst%d" % sl, [], [("stg", sl)], stg[sl][:].rearrange("p (dt f) -> p dt f", dt=8),
            wada_v[:, :, ch * 512:(ch + 1) * 512])
        sv = stg[sl][:].rearrange("p (dt f) -> p dt f", dt=8)
        for f4 in range(4):
            ft = ch * 4 + f4
            for dt in range(8):
                OP("pe", "matmul", [("stg", sl), "cT"], ["pg0"], modps[:, ft * 2:ft * 2 + 2],
                   lhsT=sv[:, dt, f4 * 128:(f4 + 1) * 128], rhs=cT[:, dt, :], start=(dt == 0), stop=(dt == 7))
    OP("dve", "tensor_tensor", ["pg0", "bcol"], ["modc"], modc[:].rearrange("p k s d -> p k d s"),
       modps[:, 0:144].rearrange("p (k d s) -> p k d s", k=9, d=8),
       bc(bcol[:].rearrange("p (k d) -> p k d", k=9), 3, 2), ALU.add)
    for j in range(3):
        OP("dve", "scalar_tensor_tensor", ["modc", ("ngc", j)], ["modc"], modc[:, 3 * j + 1, :, :],
           modc[:, 3 * j + 1, :, :], 1.0, bc(ngc[:, j, :], 1, 2), ALU.add, ALU.mult)
    for k in (2, 8):
        OP("dve", "tensor_scalar", ["modc"], ["modc"], modc[:, k, :, :], modc[:, k, :, :], 0.5, None, ALU.mult)

    DMA("act", "pl3", [], ["bsp_c"], bsp_c[:], W["gmlp_b_sp"].rearrange("h q -> q h"), allow_slow_non_contiguous=True)
    wsp_n = M.sb("wsp_n", [128, 4, 128], F32)
    DMA("act", "pl4", [], ["wsp_n"], wsp_n[:], W["gmlp_w_sp"].rearrange("h q k -> q h k"))
    for h in range(4):
        OP("pe", "transpose", ["wsp_n", "ident_f"], ["po0"], PO[0][:, h * 128:(h + 1) * 128], wsp_n[:, h, :], ident_f[:])
    OP("act", "activation", ["po0"], ["wspT"], wspT[:].rearrange("p h q -> p (h q)"), PO[0][:], AF.Copy)
    gcat = M.sb("gcat", [128, 8], F32)
    DMA("act", "pl5", [], [("gcat", 0)], gcat[:, 0:4], W["norm_out_s5_g"].rearrange("(k p) -> p k", p=128),
        allow_slow_non_contiguous=True)
    DMA("act", "pl6", [], [("gcat", 1)], gcat[:, 4:8], W["norm_out_gmlp_g"].rearrange("(k p) -> p k", p=128),
        allow_slow_non_contiguous=True)
    dcol = M.sb("dcol", [128, 4], F32)
    DMA("act", "pl7", [], ["dcol"], dcol[:], W["s5_d"].rearrange("(k p) -> p k", p=128), allow_slow_non_contiguous=True)

    cvt_i = [0]

    def convert(src, dst, nfree, scale_bc=None):
        i = cvt_i[0]
        cvt_i[0] += 1
        sl = i % 2
        sshape = list(src.shape)
        sv = stg[sl][:, 0:nfree]
        if len(sshape) == 3:
            sv = sv.rearrange("p (a b) -> p a b", a=sshape[1])
        elif len(sshape) == 4:
            sv = sv.rearrange("p (a b c) -> p a b c", a=sshape[1], b=sshape[2])
        if len(sshape) == 4:
            for a_ in range(sshape[2]):
                DMA("sp", "cst%d_%d" % (sl, a_), [], [("stg", sl)] if a_ == 0 else [("stgx", sl, a_)], sv[:, :, a_, :], src[:, :, a_, :])
        else:
            DMA("sp", "cst%d" % sl, [], [("stg", sl)], sv, src)
        if scale_bc is not None:
            OP("dve", "tensor_tensor", [("stg", sl), ("gcat", 0), ("gcat", 1)], [("stb", sl)],
               stb[sl][:, 0:nfree].rearrange("p (a b) -> p a b", a=sshape[1]), sv, scale_bc, ALU.mult)
        elif i % 3 == 0:
            OP("act", "activation", [("stg", sl), ("stgx", sl, 1)], [("stb", sl), ("stgx", sl, 1)], stb[sl][:, 0:nfree], stg[sl][:, 0:nfree], AF.Copy)
        elif i % 3 == 1:
            OP("dve", "tensor_copy", [("stg", sl), ("stgx", sl, 1)], [("stb", sl), ("stgx", sl, 1)], stb[sl][:, 0:nfree], stg[sl][:, 0:nfree])
        else:
            OP("pool", "tensor_copy", [("stg", sl), ("stgx", sl, 1)], [("stb", sl), ("stgx", sl, 1)], stb[sl][:, 0:nfree], stg[sl][:, 0:nfree])
        DMA("act", "cso%d" % sl, [("stb", sl)], [], dst, stb[sl][:, 0:nfree])

    cvd_i = [0]
    deferred = []

    def cast_dma(dst, src):
        i = cvd_i[0]
        cvd_i[0] += 1
        DMA("pool", "cv%d" % (i % 3), [], [], dst, src)

    for k, (wi, wo) in enumerate([("ffn1_w_in", "ffn1_w_out"), ("ffn2_w_in", "ffn2_w_out")]):
        wiv = W[wi].rearrange("(dt p) (gu f) -> p dt gu f", p=128, gu=2)
        wov = W[wo].rearrange("(ft p) d -> p ft d", p=128)
        jobs = []
        for c in range(11):
            for gu in range(2):
                jobs.append((Win_s[k][c].rearrange("p (dt gu f) -> p dt gu f", dt=8, gu=2)[:, :, gu, :],
                             wiv[:, :, gu, c * 256:(c + 1) * 256]))
        for do in range(8):
            jobs.append((Wout_s[k][do].rearrange("p (ft f) -> p ft f", ft=NFT), wov[:, :, do * 128:(do + 1) * 128]))
        if k == 0:
            for dst, src in jobs:
                cast_dma(dst, src)
        else:
            deferred.extend(jobs)
    wmv = W["w_mix_in"].rearrange("(dt p) f -> p dt f", p=128)
    for c3 in range(3):
        cast_dma(Wmi_s[c3].rearrange("p (dt f) -> p dt f", dt=8), wmv[:, :, c3 * 512:(c3 + 1) * 512])
    cast_dma(Wgl_s.rearrange("p (kt f) -> p kt f", kt=4), W["s5_w_glu"].rearrange("(kt p) f -> p kt f", p=128))
    wmo = W["w_mix_out"].rearrange("(kt p) d -> p kt d", p=128)
    for do in range(8):
        convert(wmo[:, :, do * 128:(do + 1) * 128], Wmo_s[do], 1024, scale_bc=bc(gcat[:], 2, 128))

    def s5t(name, shape=(128, 16), dt=F32):
        return M.sb(name, list(shape), dt)

    tmpA = s5t("tmpA"); tmpB = s5t("tmpB"); tmpC = s5t("tmpC")
    tmpI = s5t("tmpI", (128, 16), I32)
    bdst = M.sb("bdst", [128, 4, 2, 4, 128], F32)
    sinst = M.sb("sinst", [128, 4, 4, 2, 128], BF16)
    soutst = M.sb("soutst", [128, 16, 2, 4, 2, 32], BF16)
    bmask = M.sb("bmask", [128, 4, 32], F32)
    OP("dve", "memset", [], ["bmask"], bmask[:], 0.0)
    for q in range(4):
        OP("dve", "memset", ["bmask"], ["bmask"], bmask[32 * q:32 * q + 32, q, :], 1.0)
    ZB = [[M.sb("zb%d_%d" % (q, ri), [128, 128], F32) for ri in range(2)] for q in range(4)]
    for q in range(4):
        for ri in range(2):
            OP("dve", "memset", [], [("zb", q, ri)], ZB[q][ri][:], 0.0)
    pli = [0]

    def plsem():
        pli[0] += 1
        return "pl%d" % (pli[0] % 8)

    def sin_of(out, th, key_out, key_th):
        OP("dve", "tensor_scalar", [key_th], ["tmpA"], tmpA[:], th[:], 1.0 / (2 * PI), None, ALU.mult)
        OP("dve", "tensor_copy", ["tmpA"], ["tmpI"], tmpI[:], tmpA[:])
        OP("dve", "tensor_copy", ["tmpI"], ["tmpA"], tmpA[:], tmpI[:])
        OP("dve", "scalar_tensor_tensor", ["tmpA", key_th], ["tmpB"], tmpB[:], tmpA[:], -2 * PI, th[:], ALU.mult, ALU.add)
        OP("dve", "tensor_scalar", ["tmpB"], ["tmpA"], tmpA[:], tmpB[:], PI, None, ALU.is_gt)
        OP("dve", "scalar_tensor_tensor", ["tmpA", "tmpB"], ["tmpC"], tmpC[:], tmpA[:], -2 * PI, tmpB[:], ALU.mult, ALU.add)
        OP("dve", "tensor_scalar", ["tmpC"], ["tmpA"], tmpA[:], tmpC[:], -PI, None, ALU.is_lt)
        OP("dve", "scalar_tensor_tensor", ["tmpA", "tmpC"], ["tmpB"], tmpB[:], tmpA[:], 2 * PI, tmpC[:], ALU.mult, ALU.add)
        OP("act", "activation", ["tmpB"], [key_out], out[:], tmpB[:], AF.Sin)

    def TT_(eng, out, a, b, op, r, w):
        OP(eng, "tensor_tensor", r, w, out, a, b, op)

    for d, sfx in enumerate(["f", "b"]):
        pf = "d%d_" % d
        mk_d = M.mark()
        lre = s5t(pf + "lre"); lim = s5t(pf + "lim"); lsb = s5t(pf + "lsb")
        DMA("act", plsem(), [], [pf + "lre"], lre[:], W["s5_lam_re_" + sfx].rearrange("(gp two) p -> (two p) gp", two=2),
            allow_slow_non_contiguous=True)
        DMA("act", plsem(), [], [pf + "lim"], lim[:], W["s5_lam_im_" + sfx].rearrange("(gp two) p -> (two p) gp", two=2),
            allow_slow_non_contiguous=True)
        lsv = W["s5_log_step_" + sfx].rearrange("(gp two) -> two gp", two=2)
        for two in range(2):
            DMA("act", plsem(), [], [(pf + "lsb", two)], lsb[two * 64:(two + 1) * 64, :], lsv[two].partition_broadcast(64),
                allow_slow_non_contiguous=True)
        Bre = s5t(pf + "Bre", (128, 16, 16)); Bim = s5t(pf + "Bim", (128, 16, 16))
        DMA("act", plsem(), [], [pf + "Bre"], Bre[:], W["s5_b_re_" + sfx].rearrange("(gp two) p h -> (two p) gp h", two=2))
        DMA("act", plsem(), [], [pf + "Bim"], Bim[:], W["s5_b_im_" + sfx].rearrange("(gp two) p h -> (two p) gp h", two=2))
        CT = []
        for t, cn in enumerate(["s5_c_re_" + sfx, "s5_c_im_" + sfx]):
            CA = M.sb(pf + "CA%d" % t, [128, 2, 128], F32)
            for gp in range(16):
                DMA("act", plsem(), [], [(pf + "CA%d" % t, gp)],
                    CA[16 * (gp % 8):16 * (gp % 8) + 16, gp // 8, :].rearrange("ho (two p) -> ho two p", two=2),
                    W[cn][2 * gp:2 * gp + 2].rearrange("two ho p -> ho two p"))
            ct = s5t(pf + "CT%d" % t, (128, 16, 16))
            for half in range(2):
                OP("pe", "transpose", [(pf + "CA%d" % t, gp_) for gp_ in range(16)] + ["ident_f"], ["po1"], PO[1][:, half * 128:(half + 1) * 128],
                   CA[:, half, :], ident_f[:])
            OP("act", "activation", ["po1"], [pf + "CT%d" % t], ct[:].rearrange("p g h -> p (g h)"), PO[1][:, 0:256], AF.Copy)
            CT.append(ct)
        dtt = s5t(pf + "dt"); xr = s5t(pf + "xr"); xi = s5t(pf + "xi"); xi2 = s5t(pf + "xi2")
        mag = s5t(pf + "mag"); sn = s5t(pf + "sn"); cs = s5t(pf + "cs")
        OP("act", "activation", [(pf + "lsb", 0), (pf + "lsb", 1)], [pf + "dt"], dtt[:], lsb[:], AF.Exp)
        TT_("dve", xr[:], lre[:], dtt[:], ALU.mult, [pf + "lre", pf + "dt"], [pf + "xr"])
        TT_("dve", xi[:], lim[:], dtt[:], ALU.mult, [pf + "lim", pf + "dt"], [pf + "xi"])
        OP("dve", "tensor_scalar", [pf + "xi"], [pf + "xi2"], xi2[:], xi[:], PI / 2, None, ALU.add)
        OP("act", "activation", [pf + "xr"], [pf + "mag"], mag[:], xr[:], AF.Exp)
        sin_of(sn, xi, pf + "sn", pf + "xi")
        sin_of(cs, xi2, pf + "cs", pf + "xi2")
        APr = s5t(pf + "APr", (128, 5, 16)); APi = s5t(pf + "APi", (128, 5, 16))
        kr, ki = pf + "APr", pf + "APi"
        OP("dve", "memset", [], [kr], APr[:, 0, :], 1.0)
        OP("dve", "memset", [], [ki], APi[:, 0, :], 0.0)
        TT_("dve", APr[:, 1, :], mag[:], cs[:], ALU.mult, [pf + "mag", pf + "cs", kr], [kr])
        TT_("dve", APi[:, 1, :], mag[:], sn[:], ALU.mult, [pf + "mag", pf + "sn", ki], [ki])
        for k in range(2, 5):
            TT_("dve", tmpA[:], APr[:, k - 1, :], APr[:, 1, :], ALU.mult, [kr], ["tmpA"])
            TT_("dve", tmpB[:], APi[:, k - 1, :], APi[:, 1, :], ALU.mult, [ki], ["tmpB"])
            TT_("dve", APr[:, k, :], tmpA[:], tmpB[:], ALU.subtract, ["tmpA", "tmpB", kr], [kr])
            TT_("dve", tmpA[:], APr[:, k - 1, :], APi[:, 1, :], ALU.mult, [kr, ki], ["tmpA"])
            TT_("dve", tmpB[:], APi[:, k - 1, :], APr[:, 1, :], ALU.mult, [kr, ki], ["tmpB"])
            TT_("dve", APi[:, k, :], tmpA[:], tmpB[:], ALU.add, ["tmpA", "tmpB", ki], [ki])
        OP("dve", "tensor_copy", [kr], ["scanA"], scanA[:, d, 0, :], APr[:, TAU, :])
        OP("dve", "tensor_copy", [kr], ["scanA"], scanA[:, d, 1, :], APr[:, TAU, :])
        OP("dve", "tensor_scalar", [ki], ["scanA"], scanA[:, d, 2, :], APi[:, TAU, :], -1.0, None, ALU.mult)
        OP("dve", "tensor_copy", [ki], ["scanA"], scanA[:, d, 3, :], APi[:, TAU, :])
        def pwi(j):
            return (j - 1) if d == 0 else (16 - j)
        OP("dve", "tensor_copy", [kr], ["PW"], PW[:, d, 0, pwi(1), :], APr[:, TAU, :])
        OP("dve", "tensor_copy", [ki], ["PW"], PW[:, d, 1, pwi(1), :], APi[:, TAU, :])
        for j in range(2, 17):
            pr_, pi_ = PW[:, d, 0, pwi(j - 1), :], PW[:, d, 1, pwi(j - 1), :]
            TT_("dve", tmpA[:], pr_, APr[:, TAU, :], ALU.mult, ["PW", kr], ["tmpA"])
            TT_("dve", tmpB[:], pi_, APi[:, TAU, :], ALU.mult, ["PW", ki], ["tmpB"])
            TT_("dve", PW[:, d, 0, pwi(j), :], tmpA[:], tmpB[:], ALU.subtract, ["tmpA", "tmpB", "PW"], ["PW"])
            TT_("dve", tmpA[:], pr_, APi[:, TAU, :], ALU.mult, ["PW", ki], ["tmpA"])
            TT_("dve", tmpB[:], pi_, APr[:, TAU, :], ALU.mult, ["PW", kr], ["tmpB"])
            TT_("dve", PW[:, d, 1, pwi(j), :], tmpA[:], tmpB[:], ALU.add, ["tmpA", "tmpB", "PW"], ["PW"])
        OP("dve", "tensor_copy", ["PW"], ["scanB"], scanB[:, d, 0, :], PW[:, d, 0, pwi(16), :])
        OP("dve", "tensor_copy", ["PW"], ["scanB"], scanB[:, d, 1, :], PW[:, d, 0, pwi(16), :])
        OP("dve", "tensor_scalar", ["PW"], ["scanB"], scanB[:, d, 2, :], PW[:, d, 1, pwi(16), :], -1.0, None, ALU.mult)
        OP("dve", "tensor_copy", ["PW"], ["scanB"], scanB[:, d, 3, :], PW[:, d, 1, pwi(16), :])
        fr = s5t(pf + "fr"); fi = s5t(pf + "fi"); nr = s5t(pf + "nr"); den = s5t(pf + "den")
        OP("dve", "tensor_scalar", [kr], [pf + "nr"], nr[:], APr[:, 1, :], -1.0, None, ALU.add)
        TT_("dve", tmpA[:], lre[:], lre[:], ALU.mult, [pf + "lre"], ["tmpA"])
        TT_("dve", tmpB[:], lim[:], lim[:], ALU.mult, [pf + "lim"], ["tmpB"])
        TT_("dve", den[:], tmpA[:], tmpB[:], ALU.add, ["tmpA", "tmpB"], [pf + "den"])
        OP("dve", "reciprocal", [pf + "den"], [pf + "den"], den[:], den[:])
        TT_("dve", tmpA[:], nr[:], lre[:], ALU.mult, [pf + "nr", pf + "lre"], ["tmpA"])
        TT_("dve", tmpB[:], APi[:, 1, :], lim[:], ALU.mult, [ki, pf + "lim"], ["tmpB"])
        TT_("dve", tmpC[:], tmpA[:], tmpB[:], ALU.add, ["tmpA", "tmpB"], ["tmpC"])
        TT_("dve", fr[:], tmpC[:], den[:], ALU.mult, ["tmpC", pf + "den"], [pf + "fr"])
        TT_("dve", tmpA[:], APi[:, 1, :], lre[:], ALU.mult, [ki, pf + "lre"], ["tmpA"])
        TT_("dve", tmpB[:], nr[:], lim[:], ALU.mult, [pf + "nr", pf + "lim"], ["tmpB"])
        TT_("dve", tmpC[:], tmpA[:], tmpB[:], ALU.subtract, ["tmpA", "tmpB"], ["tmpC"])
        TT_("dve", fi[:], tmpC[:], den[:], ALU.mult, ["tmpC", pf + "den"], [pf + "fi"])
        t3a = s5t(pf + "t3a", (128, 16, 16)); t3b = s5t(pf + "t3b", (128, 16, 16))
        t3r = s5t(pf + "t3r", (128, 16, 16)); t3i = s5t(pf + "t3i", (128, 16, 16))

        def cmul(outr, outi, inr, ini, fre, fim, kin, kf, kout, neg_im=False):
            frb, fib = bc(fre, 2, 16), bc(fim, 2, 16)
            TT_("dve", t3a[:], inr, frb, ALU.mult, kin + kf, [pf + "t3a"])
            TT_("dve", t3b[:], ini, fib, ALU.mult, kin + kf, [pf + "t3b"])
            TT_("dve", outr, t3a[:], t3b[:], ALU.subtract, [pf + "t3a", pf + "t3b"], kout)
            TT_("dve", t3a[:], inr, fib, ALU.mult, kin + kf, [pf + "t3a"])
            TT_("dve", t3b[:], ini, frb, ALU.mult, kin + kf, [pf + "t3b"])
            if neg_im:
                OP("dve", "scalar_tensor_tensor", [pf + "t3a", pf + "t3b"], kout, outi, t3a[:], -1.0, t3b[:],
                   ALU.mult, ALU.subtract)
            else:
                TT_("dve", outi, t3a[:], t3b[:], ALU.add, [pf + "t3a", pf + "t3b"], kout)

        bbr = s5t(pf + "bbr", (128, 16, 16)); bbi = s5t(pf + "bbi", (128, 16, 16))
        cmul(bbr[:], bbi[:], Bre[:], Bim[:], fr[:], fi[:], [pf + "Bre", pf + "Bim"], [pf + "fr", pf + "fi"],
             [pf + "bb"])
        MBP = M.sb(pf + "MBP", [128, 2, 4, 16, 32], F32)
        MCP = M.sb(pf + "MCP", [128, 2, 5, 16, 32], F32)
        OP("pool", "memset", [], [pf + "MBP"], MBP[:], 0.0)
        OP("pool", "memset", [], [pf + "MCP"], MCP[:], 0.0)
        for e in range(4):
            cmul(t3r[:], t3i[:], bbr[:], bbi[:], APr[:, e, :], APi[:, e, :], [pf + "bb"], [kr, ki], [pf + "t3ri"])
            for ri, src in enumerate([t3r, t3i]):
                for two in range(2):
                    ps_ = slice(two * 64, two * 64 + 64)
                    OP("dve", "tensor_copy", [pf + "t3ri", pf + "MBP"], [pf + "MBP"],
                       MBP[ps_, ri, e, :, two * 16:(two + 1) * 16], src[ps_, :, :])
        for k in range(5):
            cmul(t3r[:], t3i[:], CT[0][:], CT[1][:], APr[:, k, :], APi[:, k, :], [pf + "CT0", pf + "CT1"], [kr, ki],
                 [pf + "t3ri"], neg_im=True)
            for ri, src in enumerate([t3r, t3i]):
                for two in range(2):
                    ps_ = slice(two * 64, two * 64 + 64)
                    OP("dve", "tensor_copy", [pf + "t3ri", pf + "MCP"], [pf + "MCP"],
                       MCP[ps_, ri, k, :, two * 16:(two + 1) * 16], src[ps_, :, :])
        for ftq in range(4):
            for j in range(4):
                e = (TAU - 1 - j) if d == 0 else j
                for ri in range(2):
                    bank = (j * 2 + ri) % 2
                    OP("pe", "transpose", [pf + "MBP", "ident_f"], [POK[bank]], PO[bank][:, 0:128],
                       MBP[:, ri, e, 4 * ftq:4 * ftq + 4, :].rearrange("p a b -> p (a b)"), ident_f[:])
                    OP("act", "activation", [POK[bank]], ["sinst"], sinst[:, ftq, j, ri, :], PO[bank][:, 0:128], AF.Copy)
        DMA("act", plsem(), ["sinst"], [], SIN_s[d], sinst[:].rearrange("p a b c e -> p (a b c e)"))
        for i in range(4):
            k = (i + 1) if d == 0 else (TAU - i)
            for ri in range(2):
                OP("dve", "tensor_copy", [pf + "MCP", "soutst"], ["soutst"], soutst[:, :, d, i, ri, :], MCP[:, ri, k, :, :])
        for ft in range(4):
            for q in range(4):
                gp = 4 * ft + q
                for ri in range(2):
                    OP("dve", "tensor_copy", [pf + "MBP", ("zb", q, ri)], [("zb", q, ri)],
                       ZB[q][ri][:, 32 * q:32 * q + 32], MBP[:, ri, 0, gp, :])
            for dl in range(4):
                bank = dl % 2
                for q in range(4):
                    gp = 4 * ft + q
                    for ri in range(2):
                        OP("pe", "matmul", [("zb", q, ri), pf + "MCP"], [POK[bank]], PO[bank][:, 0:32],
                           lhsT=ZB[q][ri][:], rhs=MCP[:, ri, dl, gp, :], start=(q == 0 and ri == 0),
                           stop=(q == 3 and ri == 1))
                TT_("dve", bdst[:, ft, d, dl, :].rearrange("p (q c) -> p q c", q=4), bc(PO[bank][:, 0:32], 1, 4),
                    bmask[:], ALU.mult, [POK[bank], "bmask", "bdst"], ["bdst"])
            if d == 0:
                OP("dve", "scalar_tensor_tensor", ["ident_f", "dcol", "bdst"], ["bdst"], bdst[:, ft, 0, 0, :],
                   ident_f[:], dcol[:, ft:ft + 1], bdst[:, ft, 0, 0, :], ALU.mult, ALU.add)
        S.barrier()
        M.release(mk_d)
    OP("act", "activation", ["bdst"], [("stb", 0)], stb[0][:], bdst[:].rearrange("p a b c e -> p (a b c e)"), AF.Copy)
    for ft in range(4):
        DMA("act", plsem(), [("stb", 0)], [], YC_s[ft][:, 0:1024], stb[0][:, ft * 1024:(ft + 1) * 1024])
        DMA("act", plsem(), ["soutst"], [], YC_s[ft][:, 1024:3072],
            soutst[:, 4 * ft:4 * ft + 4].rearrange("p a b c e f -> p (a b c e f)"))

    S.barrier()
    M.release(mk_pro)

    XT = [M.sb("xt0", [128, 4, D], F32), None]
    HT = [M.sb("hT0", [128, 8, TT], BF16), None]

    def xk(xi, r=None):
        return [("xt", xi, r_) for r_ in range(4)] if r is None else ("xt", xi, r)
    xn = [M.sb("xn%d" % i, [128, D], BF16) for i in range(2)]
    ntmp = M.sb("ntmp", [128, 8, 128], F32)
    hh = M.sb("hh", [128, NFT, TT], BF16)
    sg = [M.sb("sg%d" % i, [128, TT], BF16) for i in range(2)]
    ob = [M.sb("ob0", [128, TT], BF16)] * 2
    st_ssq = M.sb("st_ssq", [128, 8], F32)
    st_rstd = M.sb("st_rstd", [128, 8], F32)
    Ud = M.sb("Ud", [128, 4, TAU, NSUB], BF16)
    ESr = M.sb("ESr", [128, NSUB, 2, 16], F32)
    Sxr = M.sb("Sxr", [128, 2, 16, NSUB], BF16)
    VE = M.sb("VE", [128, 9, 2, 16], F32)
    sct1 = M.sb("sct1", [128, 8, 2, 16], F32)
    sct2 = M.sb("sct2", [128, 8, 2, 16], F32)
    sctb = M.sb("sctb", [128, 2, 16, 16], F32)
    carry = [M.sb("carry%d" % d, [128, 2, 16], F32) for d in range(2)]
    Sx = [M.sb("Sx0", [128, 2, 16, NSUB], BF16), None]
    rings = {"w": [M.sb("wbuf%d" % i, [128, 4096], BF16) for i in range(2)],
             "o": [M.sb("obuf%d" % i, [128, 3072], BF16) for i in range(2)],
             "q": []}
    ring_i = {"w": 0, "o": 0, "q": 0}

    def wload(kind, src, nfree):
        i = ring_i[kind]
        ring_i[kind] += 1
        sl = i % len(rings[kind])
        buf = rings[kind][sl]
        key = (kind + "buf", sl)
        DMA("sp", "%s%d" % (kind, sl), [], [key], buf[:, 0:nfree], src)
        return buf[:, 0:nfree], key

    def pump(g):
        if g is not None:
            next(g, None)

    def drain(g):
        if g is not None:
            for _ in g:
                pass

    def norm_gen(site, s, xi, hi):
        xt, hT = XT[xi], HT[hi]
        gs = modc[:, 3 * site + 1, s, :]
        sh = modc[:, 3 * site + 0, s, :]
        for r in range(4):
            sl = r % 2
            OP("act", "activation", [xk(xi, r)], [("xn", sl), ("ssq", r)], xn[sl][:], xt[:, r, :], AF.Square,
               accum_out=st_ssq[:, r:r + 1])
        ssk = [("ssq", r) for r in range(4)]
        rsk = [("rstd", r) for r in range(4)]
        OP("act", "activation", ssk, rsk, st_rstd[:, 0:4], st_ssq[:, 0:4], AF.Sqrt, bias=eps_t[:], scale=1.0 / D)
        OP("dve", "reciprocal", rsk, rsk, st_rstd[:, 0:4], st_rstd[:, 0:4])
        yield

        def tail(r):
            sl = r % 2
            for dt in range(8):
                OP("pe", "transpose", [("xn", sl), "ident_b"], ["tra"], TRA[:, dt, :], xn[sl][:, dt * 128:(dt + 1) * 128],
                   ident_b[:])
            OP("dve", "tensor_tensor", ["tra", "modc"], ["ntmp"], ntmp[:], TRA[:], bc(gs, 2, 128), ALU.mult)
            OP("dve", "tensor_tensor", ["ntmp", "modc"], [("hT", hi)], hT[:, :, r * 128:(r + 1) * 128], ntmp[:],
               bc(sh, 2, 128), ALU.add)

        for r in range(4):
            sl = r % 2
            OP("act", "activation", [xk(xi, r), ("rstd", r)], [("xn", sl)], xn[sl][:], xt[:, r, :], AF.Identity,
               scale=st_rstd[:, r:r + 1])
            if r > 0:
                tail(r - 1)
            yield
        tail(3)
        yield

    def norm_hT(site, s, xi=0, hi=0):
        drain(norm_gen(site, s, xi, hi))

    ep_i = [0]

    ep_pending = [None]

    def ep_flush():
        if ep_pending[0] is None:
            return
        sl, do, xi = ep_pending[0]
        ep_pending[0] = None
        xt = XT[xi]
        for r in range(4):
            OP("pe", "transpose", ["ob", "ident_b"], ["trb"], TRB[:, sl, r, :], ob[sl][:, r * 128:(r + 1) * 128],
               ident_b[:])
        OP("dve", "tensor_tensor", xk(xi) + ["trb"], xk(xi), xt[:, :, do * 128:(do + 1) * 128],
           xt[:, :, do * 128:(do + 1) * 128], TRB[:, sl, :, :], ALU.add)

    def epilogue(po_idx, gcol, do, xi=0):
        ep_flush()
        i = ep_i[0]
        ep_i[0] += 1
        sl = i % 2
        OP("act", "activation", [POK[po_idx], "modc"], ["ob"], ob[sl][:], PO[po_idx][:], AF.Identity, scale=gcol)
        ep_pending[0] = (sl, do, xi)

    def ffn_step1(k, hi=0, filler=None):
        hT = HT[hi]
        nxt = wload("w", Win_s[k][0], 4096)
        for c in range(11):
            wt, wk = nxt
            if c + 1 < 11:
                nxt = wload("w", Win_s[k][c + 1], 4096)
            wv = wt.rearrange("p (dt gu f) -> p dt gu f", dt=8, gu=2)
            for f2 in range(2):
                ft = 2 * c + f2
                b = ft % 2
                for gu in range(2):
                    for dt in range(8):
                        OP("pe", "matmul", [wk, ("hT", hi)], [PGK[2 * gu + b]], PG[2 * gu + b][:],
                           lhsT=wv[:, dt, gu, f2 * 128:(f2 + 1) * 128], rhs=hT[:, dt, :], start=(dt == 0), stop=(dt == 7))
                OP("act", "activation", [PGK[b]], [("sg", b)], sg[b][:], PG[b][:], AF.Silu)
                OP("dve", "tensor_tensor", [("sg", b), PGK[2 + b]], [("hh", ft)], hh[:, ft, :], sg[b][:], PG[2 + b][:],
                   ALU.mult)
                pump(filler)

    def ffn_step2(k, s, xi=0, filler=None):
        site = 0 if k == 0 else 2
        nxt = wload("o", Wout_s[k][0], NFT * 128)
        for do in range(8):
            wt, wk = nxt
            if do + 1 < 8:
                nxt = wload("o", Wout_s[k][do + 1], NFT * 128)
            wv = wt.rearrange("p (ft f) -> p ft f", ft=NFT)
            b = do % 2
            for ft in range(NFT):
                OP("pe", "matmul", [wk, ("hh", ft)], [POK[b]], PO[b][:], lhsT=wv[:, ft, :], rhs=hh[:, ft, :],
                   start=(ft == 0), stop=(ft == NFT - 1))
            epilogue(b, modc[:, 3 * site + 2, s, do:do + 1], do, xi)
            pump(filler)
        ep_flush()

    def ffn(k, s):
        norm_hT(0 if k == 0 else 2, s)
        ffn_step1(k)
        ffn_step2(k, s)

    def s5_u_gen(wt, wk, hi=0):
        hT = HT[hi]
        wv = wt.rearrange("p (dt f) -> p dt f", dt=8)
        for ft in range(4):
            b = ft % 2
            for dt in range(8):
                OP("pe", "matmul", [wk, ("hT", hi)], [POK[b]], PO[b][:], lhsT=wv[:, dt, ft * 128:(ft + 1) * 128],
                   rhs=hT[:, dt, :], start=(dt == 0), stop=(dt == 7))
            OP("act", "activation", [POK[b]], [("Ud", ft)], Ud[:, ft, :, :].rearrange("p j n -> p n j"),
               PO[b][:].rearrange("p (n j) -> p n j", j=TAU), AF.Copy)
            yield

    def s5_u(wt, wk):
        drain(s5_u_gen(wt, wk))

    def statein_gen(wt, wk, ES, esk):
        sv = wt.rearrange("p (q j r c) -> p q j r c", q=4, j=TAU, r=2)
        for ri in range(2):
            for qp in range(2):
                for q4 in range(4):
                    for q2 in range(2):
                        qq = 2 * qp + q2
                        rows = slice(32 * qq, 32 * qq + 32)
                        for j in range(TAU):
                            OP("pe", "matmul", [wk, ("Ud", q4)], [POK[q2]], PO[q2][:, q4 * NSUB:(q4 + 1) * NSUB],
                               lhsT=sv[rows, q4, j, ri, :], rhs=Ud[rows, q4, j, :], start=(j == 0), stop=(j == TAU - 1),
                               tile_position=(32 * qq, 0))
                    if q4 % 2 == 1:
                        yield
                for q2 in range(2):
                    qq = 2 * qp + q2
                    OP("act", "activation", [POK[q2]], [esk],
                       ES[:, :, ri, :].rearrange("p n (a b) -> p b a n", b=4)[:, qq],
                       PO[q2][:].rearrange("p (a n) -> p a n", a=4), AF.Copy)

    def scan2(ES, esk, d, reverse, SXo, sxk, eng="pool"):
        v5 = ES[:].rearrange("p (b k) r g -> p b k r g", b=8)
        ck = ("carry", d)

        def step(prev, cur, tab, t1, t2, nb, rk, wk_):
            ar2, nai, ai = tab[:, d, 0:2, :], tab[:, d, 2, :], tab[:, d, 3, :]
            if nb:
                ar2, nai, ai = bc(ar2, 1, nb), bc(nai, 1, nb), bc(ai, 1, nb)
                i0, i1 = (slice(None), slice(None), 0, slice(None)), (slice(None), slice(None), 1, slice(None))
            else:
                i0, i1 = (slice(None), 0, slice(None)), (slice(None), 1, slice(None))
            OP(eng, "tensor_tensor", rk + ["scanA", "scanB"], ["sct1"], t1, prev, ar2, ALU.mult)
            OP(eng, "tensor_tensor", rk + ["scanA", "scanB"], ["sct2"], t2[i0], prev[i1], nai, ALU.mult)
            OP(eng, "tensor_tensor", rk + ["scanA", "scanB"], ["sct2"], t2[i1], prev[i0], ai, ALU.mult)
            OP(eng, "tensor_tensor", ["sct1"] + rk, wk_, cur, cur, t1, ALU.add)
            OP(eng, "tensor_tensor", ["sct2"] + rk, wk_, cur, cur, t2, ALU.add)

        ks = range(14, -1, -1) if reverse else range(1, 16)
        for k in ks:
            kp = k + 1 if reverse else k - 1
            step(v5[:, :, kp], v5[:, :, k], scanA, sct1[:], sct2[:], 8, [esk], [esk])
        if not reverse:
            OP(eng, "tensor_copy", [ck], ["VE"], VE[:, 0], carry[d][:])
            for b in range(8):
                OP(eng, "tensor_copy", [esk, "VE"], ["VE"], VE[:, b + 1], v5[:, b, 15])
                step(VE[:, b], VE[:, b + 1], scanB, sct1[:, 0], sct2[:, 0], 0, ["VE"], ["VE"])
            vin = VE[:, 0:8]
            OP(eng, "tensor_copy", ["VE"], [ck], carry[d][:], VE[:, 8])
        else:
            OP(eng, "tensor_copy", [ck], ["VE"], VE[:, 8], carry[d][:])
            for b in range(7, -1, -1):
                OP(eng, "tensor_copy", [esk, "VE"], ["VE"], VE[:, b], v5[:, b, 0])
                step(VE[:, b + 1], VE[:, b], scanB, sct1[:, 0], sct2[:, 0], 0, ["VE"], ["VE"])
            vin = VE[:, 1:9]
            OP(eng, "tensor_copy", ["VE"], [ck], carry[d][:], VE[:, 0])
        for hb in range(4):
            bs = slice(2 * hb, 2 * hb + 2)
            sr = v5[:, bs, :, 0, :]
            si = v5[:, bs, :, 1, :]
            pr = bc(PW[:, d, 0], 1, 2)
            pi = bc(PW[:, d, 1], 1, 2)
            vr = bc(vin[:, bs, 0, :], 2, 16)
            vi = bc(vin[:, bs, 1, :], 2, 16)
            for (pa, va, tgt, op) in ((pr, vr, sr, ALU.add), (pi, vi, sr, ALU.subtract), (pr, vi, si, ALU.add),
                                      (pi, vr, si, ALU.add)):
                OP(eng, "tensor_tensor", ["PW", "VE"], ["sctb"], sctb[:], pa, va, ALU.mult)
                OP(eng, "tensor_tensor", ["sctb", esk], [esk], tgt, tgt, sctb[:], op)
        if not reverse:
            OP(eng, "tensor_copy", ["VE", sxk], [sxk], SXo[:, :, :, 0], VE[:, 0])
            OP(eng, "tensor_copy", [esk, sxk], [sxk], SXo[:, :, :, 1:NSUB].rearrange("p r g n -> p n r g"),
               ES[:, 0:NSUB - 1, :, :])
        else:
            OP(eng, "tensor_copy", ["VE", sxk], [sxk], SXo[:, :, :, NSUB - 1], VE[:, 8])
            OP(eng, "tensor_copy", [esk, sxk], [sxk], SXo[:, :, :, 0:NSUB - 1].rearrange("p r g n -> p n r g"),
               ES[:, 1:NSUB, :, :])

    def bwd_pass_gen(s):
        OP("pool", "memset", [("carry", 1)], [("carry", 1)], carry[1][:], 0.0)
        for m in range(NMT[s] - 1, -1, -1):
            DMA("pool", "ebld", [("ebd", s, m)], ["ESr"], ESr[:].rearrange("p n r g -> p (n r g)"), EB_s[s][m])
            scan2(ESr, "ESr", 1, True, Sxr, "Sxr")
            DMA("pool", "sbst", ["Sxr"], [("sbd", s, m)], SB_s[s][m], Sxr[:].rearrange("p r g n -> p (r g n)"))
            yield

    def bwd_pass(s):
        for _ in bwd_pass_gen(s):
            pass

    def load_x(src_ap, base, rk=(), xi=0):
        for r in range(4):
            DMA("act", "xld%d" % r, list(rk), [xk(xi, r)], XT[xi][:, r, :], src_ap[base + r * 128:base + (r + 1) * 128, :])

    def P_gen(s, m, xi):
        load_x(x_in[s], m * TT, (), xi)
        yield
        yield from norm_gen(0, s, xi, 0)

    def Q_gen(s, m, xi):
        base = m * TT
        for _ in range(2):
            if deferred:
                cast_dma(*deferred.pop(0))
        DMA("act", "x1st", xk(xi), [("x1d", s, m)], X1_s[s][base:base + TT, :].rearrange("(r p) d -> p r d", p=128),
            XT[xi][:])
        wt, wk = wload("q", Wmi_s[0], 4096)
        yield
        yield from norm_gen(1, s, xi, 1)
        wt2, wk2 = wload("q", SIN_s[0], 4096)
        yield from s5_u_gen(wt, wk, 1)
        if m == 0:
            OP("pool", "memset", [("carry", 0)], [("carry", 0)], carry[0][:], 0.0)
        yield from statein_gen(wt2, wk2, ESf, "ESf")
        wt3, wk3 = wload("q", SIN_s[1], 4096)
        scan2(ESf, "ESf", 0, False, Sx[0], ("Sx", 0))
        DMA("pool", "sfst", [("Sx", 0)], [("sfd", s, m)], SF_s[s][m], Sx[0][:].rearrange("p r g n -> p (r g n)"))
        yield
        yield from statein_gen(wt3, wk3, ESb, "ESb")
        DMA("act", "ebst", ["ESb"], [("ebd", s, m)], EB_s[s][m], ESb[:].rearrange("p n r g -> p (n r g)"))
        yield

    def PB_gen(s, m, xi):
        load_x(X1_s[s], m * TT, [("x1d", s, m)], xi)
        yield
        yield from norm_gen(1, s, xi, 0)

    def phase_b_mixer(s, m, xi):
        base = m * TT
        xt, hT = XT[xi], HT[0]
        DMA("act", "sfld", [("sfd", s, m)], [("Sx", 0)], Sx[0][:].rearrange("p r g n -> p (r g n)"), SF_s[s][m])
        DMA("act", "sbld", [("sbd", s, m)], [("Sx", 1)], Sx[1][:].rearrange("p r g n -> p (r g n)"), SB_s[s][m])
        wt, wk = wload("w", Wmi_s[0], 4096)
        s5_u(wt, wk)
        wu, wuk = wload("w", Wmi_s[1], 4096)
        wv_, wvk = wload("w", Wmi_s[2], 4096)
        wuv = wu.rearrange("p (dt f) -> p dt f", dt=8)
        wvv = wv_.rearrange("p (dt f) -> p dt f", dt=8)
        nxt_y = wload("o", YC_s[0], 3072)

        def g_tail(r):
            p = r % 2
            tok = slice(r * 128, (r + 1) * 128)
            for ct in range(4):
                OP("pe", "transpose", [("ygn", p), "ident_b"], ["tra"], TRA[:, ct, :], ygn[p][:, ct * 128:(ct + 1) * 128],
                   ident_b[:])
            OP("act", "activation", ["tra"], [("ycat", 1)], ycatT[:, 4:8, tok], TRA[:, 0:4, :], AF.Copy)

        ny = [nxt_y]
        def zpart(r):
            p = r % 2
            ft = r
            tok = slice(r * 128, (r + 1) * 128)
            for dt in range(8):
                OP("pe", "matmul", [wuk, ("hT", 0)], ["pg0"], PG[0][:], lhsT=hT[:, dt, tok], rhs=wuv[:, dt, :], start=(dt == 0),
                   stop=(dt == 7))
            for dt in range(8):
                OP("pe", "matmul", [wvk, ("hT", 0)], ["pg1"], PG[1][:], lhsT=hT[:, dt, tok], rhs=wvv[:, dt, :], start=(dt == 0),
                   stop=(dt == 7))
            OP("act", "activation", ["pg0"], [("ug", p)], ug[p][:], PG[0][:], AF.Gelu_apprx_tanh)
            OP("act", "activation", ["pg1"], ["vg"], vg[p][:], PG[1][:], AF.Gelu_apprx_tanh)
            OP("dve", "bn_stats", ["vg"], ["bnst"], bnst[:], vg[p][:])
            OP("dve", "bn_aggr", ["bnst"], ["bnmv"], bnmv[:], bnst[:])
            OP("act", "activation", ["bnmv"], ["bnrs"], st_rstd[:, 4:5], bnmv[:, 1:2], AF.Sqrt, bias=eps_t[:], scale=1.0)
            OP("dve", "reciprocal", ["bnrs"], ["bnrs"], st_rstd[:, 4:5], st_rstd[:, 4:5])
            OP("dve", "tensor_scalar", ["vg", "bnmv", "bnrs"], ["vg"], vg[p][:], vg[p][:], bnmv[:, 0:1],
               st_rstd[:, 4:5], ALU.subtract, ALU.mult)
            OP("dve", "tensor_tensor", ["vg", "lng_rep"], ["vg"], vg[p][:], vg[p][:], lng_rep[:], ALU.mult)
            OP("dve", "tensor_tensor", ["vg", "lnb_rep"], [("vn2", p)], vn2[p][:], vg[p][:], lnb_rep[:], ALU.add)

        def ypart(r):
            p = r % 2
            ft = r
            tok = slice(r * 128, (r + 1) * 128)
            yc, yck = ny[0]
            if ft + 1 < 4:
                ny[0] = wload("o", YC_s[ft + 1], 3072)
            bdv = yc[:, 0:1024].rearrange("p (d l c) -> p d l c", d=2, l=TAU)
            sov = yc[:, 1024:3072].rearrange("p (g d i r c) -> p g d i r c", g=4, d=2, i=TAU, r=2)
            b = 2 + ft % 2
            for i in range(TAU):
                reg = PG[b][:, i * NSUB:(i + 1) * NSUB]
                first = True
                for j in range(TAU):
                    if j <= i:
                        OP("pe", "matmul", [yck, ("Ud", ft)], [PGK[b]], reg, lhsT=bdv[:, 0, i - j, :], rhs=Ud[:, ft, j, :],
                           start=first, stop=False)
                        first = False
                    if j >= i:
                        OP("pe", "matmul", [yck, ("Ud", ft)], [PGK[b]], reg, lhsT=bdv[:, 1, j - i, :], rhs=Ud[:, ft, j, :],
                           start=first, stop=False)
                        first = False
                for qq in range(4):
                    gp = 4 * ft + qq
                    for d in range(2):
                        for ri in range(2):
                            last = (qq == 3 and d == 1 and ri == 1)
                            OP("pe", "matmul", [yck, ("Sx", d)], [PGK[b]],
                               PG[b][32 * qq:32 * qq + 32, i * NSUB:(i + 1) * NSUB],
                               lhsT=sov[:, qq, d, i, ri, :], rhs=Sx[d][:, ri, gp, :], start=False, stop=last,
                               tile_position=(0, 32 * qq))
            OP("act", "activation", [PGK[b]], [("y1f", ft)], y1f[:, ft, :].rearrange("p (n i) -> p n i", i=TAU),
               PG[b][:].rearrange("p (i n) -> p n i", i=TAU), AF.Gelu_apprx_tanh)
            OP("act", "activation", [("y1f", ft)], [("y1b", ft)], y1b[:, ft, :], y1f[:, ft, :], AF.Copy)

        def sppart(r):
            p = r % 2
            ft = r
            tok = slice(r * 128, (r + 1) * 128)
            for h in range(4):
                OP("pe", "matmul", ["wspT", ("vn2", p)], [POK[p]], PO[p][:, h * 128:(h + 1) * 128], lhsT=wspT[:, h, :],
                   rhs=vn2[p][:, h * 128:(h + 1) * 128], start=True, stop=True)
            for h in range(4):
                OP("dve", "scalar_tensor_tensor", [POK[p], "bsp_c", ("ug", p)], ["ygm"], ygm[p][:, h * 128:(h + 1) * 128],
                   PO[p][:, h * 128:(h + 1) * 128], bsp_c[:, h:h + 1], ug[p][:, h * 128:(h + 1) * 128], ALU.add, ALU.mult)
            OP("act", "activation", ["ygm"], [("ygn", p), "gssq"], ygn[p][:], ygm[p][:], AF.Square,
               accum_out=st_ssq[:, 5:6])
            OP("act", "activation", ["gssq"], ["grs"], st_rstd[:, 5:6], st_ssq[:, 5:6], AF.Sqrt, bias=eps_t[:], scale=1.0 / 512)
            OP("dve", "reciprocal", ["grs"], ["grs"], st_rstd[:, 5:6], st_rstd[:, 5:6])
            OP("act", "activation", ["ygm", "grs"], [("ygn", p)], ygn[p][:], ygm[p][:], AF.Identity,
               scale=st_rstd[:, 5:6])

        zpart(0)
        ypart(0)
        zpart(1)
        sppart(0)
        ypart(1)
        zpart(2)
        sppart(1)
        g_tail(0)
        ypart(2)
        zpart(3)
        sppart(2)
        g_tail(1)
        ypart(3)
        sppart(3)
        g_tail(2)
        g_tail(3)
        wg, wgk = wload("w", Wgl_s, 2048)
        wgv = wg.rearrange("p (kt f) -> p kt f", kt=4)
        y1bk = [("y1b", ft) for ft in range(4)]
        for fo in range(4):
            b = fo % 2
            for kt in range(4):
                OP("pe", "matmul", [wgk] + y1bk, [POK[b]], PO[b][:], lhsT=wgv[:, kt, fo * 128:(fo + 1) * 128], rhs=y1b[:, kt, :],
                   start=(kt == 0), stop=(kt == 3))
            OP("act", "activation", [POK[b]], [("sg", b)], sg[b][:], PO[b][:], AF.Sigmoid)
            OP("dve", "tensor_tensor", [("y1f", fo), ("sg", b)], [("y1f", fo)], y1f[:, fo, :], y1f[:, fo, :], sg[b][:], ALU.mult)
        for fo in range(4):
            OP("act", "activation", [("y1f", fo)], [("y1b", fo)], sqb[:, fo, :], y1f[:, fo, :], AF.Square)
        for fo in range(4):
            OP("pe", "matmul", ["ones_b", ("y1b", fo)], ["po0"], PO[0][:], lhsT=ones_b[:], rhs=sqb[:, fo, :], start=(fo == 0),
               stop=(fo == 3))
        OP("act", "activation", ["po0"], ["ntmp"], rs5[:], PO[0][:], AF.Sqrt, bias=eps_t[:], scale=1.0 / 512)
        OP("dve", "reciprocal", ["ntmp"], ["ntmp"], rs5[:], rs5[:])
        OP("dve", "tensor_tensor", [("y1f", fo) for fo in range(4)] + ["ntmp"], [("ycat", 0)], ycatT[:, 0:4, :], y1f[:],
           bc(rs5[:], 1, 4), ALU.mult)
        nxt = wload("o", Wmo_s[0], 1024)
        for do in range(8):
            wt, wk = nxt
            if do + 1 < 8:
                nxt = wload("o", Wmo_s[do + 1], 1024)
            wv = wt.rearrange("p (kt f) -> p kt f", kt=8)
            b = do % 2
            for kt in range(8):
                OP("pe", "matmul", [wk, ("ycat", kt // 4)], [POK[b]], PO[b][:], lhsT=wv[:, kt, :], rhs=ycatT[:, kt, :],
                   start=(kt == 0), stop=(kt == 7))
            epilogue(b, modc[:, 5, s, do:do + 1], do, xi)
        ep_flush()

    def final_norm(s, m, xi):
        base = m * TT
        xt = XT[xi]
        for r in range(4):
            sl = r % 2
            OP("act", "activation", [xk(xi, r)], [("xn", sl), ("ssq", r)], xn[sl][:], xt[:, r, :], AF.Square,
               accum_out=st_ssq[:, r:r + 1])
            OP("act", "activation", [("ssq", r)], [("rstd", r)], st_rstd[:, r:r + 1], st_ssq[:, r:r + 1], AF.Sqrt,
               bias=eps_t[:], scale=1.0 / D)
            OP("dve", "reciprocal", [("rstd", r)], [("rstd", r)], st_rstd[:, r:r + 1], st_rstd[:, r:r + 1])
            OP("dve", "scalar_tensor_tensor", [xk(xi, r), ("rstd", r), "fg_rep"], ["ntmp"],
               ntmp[:].rearrange("p a b -> p (a b)"), xt[:, r, :], st_rstd[:, r:r + 1], fg_rep[:], ALU.mult, ALU.mult)
            DMA("act", "yo", ["ntmp"], [], y_out[s][base + r * 128:base + (r + 1) * 128, :],
                ntmp[:].rearrange("p a b -> p (a b)"))

    mk_ph = M.mark()
    ESf = M.sb("ESf", [128, NSUB, 2, 16], F32)
    ESb = M.sb("ESb", [128, NSUB, 2, 16], F32)
    XT[1] = M.sb("xt1", [128, 4, D], F32)
    HT[1] = M.sb("hT1", [128, 8, TT], BF16)
    rings["q"] = [M.sb("qbuf%d" % i, [128, 4096], BF16) for i in range(2)]
    tiles = [(s_, m_) for s_ in range(2) for m_ in range(NMT[s_])]
    bgen = None
    drain(P_gen(tiles[0][0], tiles[0][1], 0))
    qprev = None
    for i, (s_, m_) in enumerate(tiles):
        xi = i % 2
        ffn_step1(0, 0, qprev)
        drain(qprev)
        pnext = P_gen(tiles[i + 1][0], tiles[i + 1][1], 1 - xi) if i + 1 < len(tiles) else None
        ffn_step2(0, s_, xi, pnext)
        drain(pnext)
        qprev = Q_gen(s_, m_, xi)
    drain(qprev)
    while deferred:
        cast_dma(*deferred.pop(0))
    S.barrier()
    M.release(mk_ph)
    rings["w"].append(M.sb("wbuf2", [128, 4096], BF16))
    lng_rep = M.sb("lng_rep", [128, 512], F32)
    lnb_rep = M.sb("lnb_rep", [128, 512], F32)
    fg_rep = M.sb("fg_rep", [128, D], F32)
    DMA("act", "pl0", [], ["lng_rep"], lng_rep[:], W["gmlp_ln_g"].partition_broadcast(128))
    DMA("act", "pl1", [], ["lnb_rep"], lnb_rep[:], W["gmlp_ln_b"].partition_broadcast(128))
    DMA("act", "pl2", [], ["fg_rep"], fg_rep[:], W["final_norm_g"].partition_broadcast(128))
    Sx[1] = M.sb("Sx1", [128, 2, 16, NSUB], BF16)
    y1f = M.sb("y1f", [128, 4, TT], F32)
    y1b = M.sb("y1b", [128, 4, TT], BF16)
    rs5 = ntmp[:].rearrange("p a b -> p (a b)")[:, 0:TT]
    XT[1] = M.sb("xt1b", [128, 4, D], F32)
    ycatT = M.sb("ycatT", [128, 8, TT], BF16)
    ug = [M.sb("ug%d" % i, [128, 512], F32) for i in range(2)]
    vg = [M.sb("vg0", [128, 512], F32)] * 2
    vn2 = [M.sb("vn2_%d" % i, [128, 512], BF16) for i in range(2)]
    ygm = [M.sb("ygm0", [128, 512], F32)] * 2
    ygn = [M.sb("ygn%d" % i, [128, 512], BF16) for i in range(2)]
    bnst = M.sb("bnst", [128, 6], F32)
    bnmv = M.sb("bnmv", [128, 2], F32)
    sqb = y1b
    bwd_pass(1)
    bwd_pass(0)
    tilesB = [(s_, m_) for s_ in (1, 0) for m_ in range(NMT[s_] - 1, -1, -1)]
    drain(PB_gen(tilesB[0][0], tilesB[0][1], 0))
    for i, (s_, m_) in enumerate(tilesB):
        xi = i % 2
        phase_b_mixer(s_, m_, xi)
        norm_hT(2, s_, xi, 0)
        ffn_step1(1, 0)
        pnext = PB_gen(tilesB[i + 1][0], tilesB[i + 1][1], 1 - xi) if i + 1 < len(tilesB) else None
        ffn_step2(1, s_, xi, pnext)
        drain(pnext)
        final_norm(s_, m_, xi)

    S.barrier()
    S.emit()
    M.release(0)
    S.close()
    return nc


_NC_CACHE = {}


def kernel(**inputs):
    x_prompt = np.asarray(inputs["x_prompt"], dtype=np.float32)
    x_sample = np.asarray(inputs["x_sample"], dtype=np.float32)
    c_prompt = np.asarray(inputs["c_prompt"], dtype=np.float32)
    c_sample = np.asarray(inputs["c_sample"], dtype=np.float32)
    B, LP, _ = x_prompt.shape
    _, LS, _ = x_sample.shape
    n = 8
    key = (LP, LS)
    if key not in _NC_CACHE:
        _NC_CACHE[key] = build(LP, LS)
    nc = _NC_CACHE[key]
    wmap = {}
    for nm in WNAMES:
        a = np.asarray(inputs[nm], dtype=np.float32)
        if nm != "final_norm_g":
            a = a[0]
        wmap[nm] = np.ascontiguousarray(a)
    in_maps = []
    for i in range(n):
        mp = dict(wmap)
        mp["x_p"] = np.ascontiguousarray(x_prompt[i])
        mp["x_s"] = np.ascontiguousarray(x_sample[i])
        mp["c"] = np.ascontiguousarray(np.stack([c_prompt[i], c_sample[i]], axis=0))
        in_maps.append(mp)
    res = run_bass_kernel_spmd(nc, in_maps, core_ids=list(range(n)))
    yp = np.stack([np.asarray(res.results[i]["y_p"], dtype=np.float32) for i in range(n)], axis=0)
    ys = np.stack([np.asarray(res.results[i]["y_s"], dtype=np.float32) for i in range(n)], axis=0)
    return (yp, ys)
```
